# Optimizing a Trainium2 kernel written in Bass

```python
import math
import jax, jax.numpy as jnp
from jax import lax
import numpy as np

D_MODEL = 1024
BATCH = 2
SEQ = 16384
DEPTH = 2

D_FF = 2816
DN_ALPHA = (2.0 * DEPTH) ** 0.25
DN_BETA = (8.0 * DEPTH) ** -0.25
LN_EPS = 1e-5
N_MOD = 9
RG_WIDTH = D_MODEL
RG_BLOCKS = 16
RG_BLOCK = RG_WIDTH // RG_BLOCKS
CONV_W = 4
RG_C = 8.0
GLA_HEADS = 4
GLA_DK = (D_MODEL // 2) // GLA_HEADS
GLA_DV = D_MODEL // GLA_HEADS
GLA_RANK = 16
GLA_TAU = 16.0
GLA_CHUNK = 64
EVEN_SPLITS = (RG_WIDTH, RG_WIDTH, GLA_HEADS * GLA_DK, GLA_HEADS * GLA_DK,
               GLA_HEADS * GLA_DV, GLA_HEADS * GLA_DV, GLA_RANK)
EVEN_OFFSETS = tuple(int(o) for o in np.cumsum(EVEN_SPLITS)[:-1])
EVEN_IN = sum(EVEN_SPLITS)
EVEN_MIX = RG_WIDTH + GLA_HEADS * GLA_DV
MLA_HEADS = 8
MLA_NOPE = 128
MLA_ROPE = 64
MLA_V = 128
MLA_Q_RANK = 256
MLA_KV_RANK = 128
MLA_QBLOCK = 128
ROPE_THETA = 10000.0
ODD_IN = MLA_Q_RANK + MLA_KV_RANK + MLA_ROPE
N_EVEN = (DEPTH + 1) // 2
N_ODD = DEPTH // 2

kernel_name = "hybrid_rglru_gla_mla_macaron_deepnorm_adaln"

F32 = jnp.float32


def layer_norm(x, g, b):
    xf = x.astype(F32)
    mu = jnp.mean(xf, -1, keepdims=True)
    var = jnp.mean(jnp.square(xf - mu), -1, keepdims=True)
    return ((xf - mu) * lax.rsqrt(var + LN_EPS) * g + b).astype(x.dtype)


def rms_norm(x, g):
    xf = x.astype(F32)
    return (xf * lax.rsqrt(jnp.mean(xf * xf, -1, keepdims=True) + LN_EPS) * g).astype(x.dtype)


def modulate(x, shift, scale):
    return x * (1 + scale[:, None, :]) + shift[:, None, :]


def deepnorm_residual(x, f, gate, g, b, weight=1.0):
    return layer_norm(DN_ALPHA * x + weight * (1 + gate[:, None, :]) * f, g, b)


def swiglu(u, w_in, w_out):
    gate, up = jnp.split(u @ w_in, 2, axis=-1)
    return (jax.nn.silu(gate) * up) @ w_out


def causal_depthwise_conv(x, w, b):
    C = x.shape[-1]
    y = lax.conv_general_dilated(x, w[:, None, :], window_strides=(1,),
                                 padding=[(CONV_W - 1, 0)],
                                 dimension_numbers=('NWC', 'WIO', 'NWC'),
                                 feature_group_count=C)
    return y + b


def rg_lru(x, w_a, b_a, w_x, b_x, lam):
    B_, S, C = x.shape
    xb = x.reshape(B_, S, RG_BLOCKS, RG_BLOCK)
    r = jax.nn.sigmoid(jnp.einsum('bsni,nij->bsnj', xb, w_a).reshape(B_, S, C) + b_a)
    i = jax.nn.sigmoid(jnp.einsum('bsni,nij->bsnj', xb, w_x).reshape(B_, S, C) + b_x)
    log_a = -RG_C * r.astype(F32) * jax.nn.softplus(-lam.astype(F32))
    a = jnp.exp(log_a)
    u = jnp.sqrt(-jnp.expm1(2.0 * log_a)) * (i * x).astype(F32)

    def combine(left, right):
        a1, b1 = left
        a2, b2 = right
        return a1 * a2, a2 * b1 + b2

    _, h = lax.associative_scan(combine, (a, u), axis=1)
    return h.astype(x.dtype)


def gla_chunked(q, k, v, log_alpha):
    B_, S, H, DK = q.shape
    DV = v.shape[-1]
    C = GLA_CHUNK
    N = S // C

    def chunk(t):
        return t.astype(F32).reshape(B_, N, C, H, t.shape[-1]).transpose(0, 1, 3, 2, 4)

    qc, kc, vc = chunk(q), chunk(k), chunk(v)
    bc = jnp.cumsum(chunk(log_alpha), axis=3)
    b_last = bc[..., -1:, :]
    q_dec = qc * jnp.exp(bc)
    k_inv = kc * jnp.exp(-bc)
    k_end = kc * jnp.exp(b_last - bc)
    mask = jnp.tril(jnp.ones((C, C), dtype=bool))
    scores = jnp.where(mask, jnp.einsum('bnhid,bnhjd->bnhij', q_dec, k_inv), 0.0)
    o_intra = jnp.einsum('bnhij,bnhjv->bnhiv', scores, vc)

    def step(state, inp):
        q_n, k_n, v_n, dec_n = inp
        o = jnp.einsum('bhcd,bhdv->bhcv', q_n, state)
        state = dec_n[..., None] * state + jnp.einsum('bhcd,bhcv->bhdv', k_n, v_n)
        return state, o

    s0 = jnp.zeros((B_, H, DK, DV), F32)
    xs = (q_dec.swapaxes(0, 1), k_end.swapaxes(0, 1), vc.swapaxes(0, 1),
          jnp.exp(b_last[..., 0, :]).swapaxes(0, 1))
    _, o_inter = lax.scan(step, s0, xs)
    o = o_intra + o_inter.swapaxes(0, 1)
    return o.transpose(0, 1, 3, 2, 4).reshape(B_, S, H, DV)


def even_mixer(u, w_in, conv_w, conv_b, rg_wa, rg_ba, rg_wx, rg_bx, rg_lam,
               gla_wa2, gla_ba, gla_norm_g, w_out):
    B_, S, _ = u.shape
    xa, ga, q, k, v, g, za = jnp.split(u @ w_in, EVEN_OFFSETS, axis=-1)
    xa = causal_depthwise_conv(xa, conv_w, conv_b)
    ya = rg_lru(xa, rg_wa, rg_ba, rg_wx, rg_bx, rg_lam) * jax.nn.gelu(ga)
    log_alpha = jax.nn.log_sigmoid((za @ gla_wa2 + gla_ba).astype(F32)) / GLA_TAU
    o = gla_chunked(q.reshape(B_, S, GLA_HEADS, GLA_DK) * (GLA_DK ** -0.5),
                    k.reshape(B_, S, GLA_HEADS, GLA_DK),
                    v.reshape(B_, S, GLA_HEADS, GLA_DV),
                    log_alpha.reshape(B_, S, GLA_HEADS, GLA_DK))
    yb = rms_norm(o, gla_norm_g).reshape(B_, S, GLA_HEADS * GLA_DV).astype(u.dtype) * jax.nn.silu(g)
    return jnp.concatenate([ya, yb], axis=-1) @ w_out


def rope_tables(positions):
    half = MLA_ROPE // 2
    freqs = ROPE_THETA ** (-jnp.arange(half, dtype=F32) / half)
    ang = positions.astype(F32)[..., None] * freqs
    return jnp.cos(ang)[:, :, None, :], jnp.sin(ang)[:, :, None, :]


def apply_rope(x, cos, sin):
    half = MLA_ROPE // 2
    x1, x2 = x[..., :half].astype(F32), x[..., half:].astype(F32)
    return jnp.concatenate([x1 * cos - x2 * sin, x2 * cos + x1 * sin], -1).astype(x.dtype)


def causal_block_attention(q, k, v):
    B_, S, H, Dqk = q.shape
    scale = Dqk ** -0.5
    nblk = S // MLA_QBLOCK
    qb = q.reshape(B_, nblk, MLA_QBLOCK, H, Dqk).swapaxes(0, 1)
    key_idx = jnp.arange(S)

    def one_block(args):
        q_blk, blk = args
        s = jnp.einsum('bqhd,bkhd->bhqk', q_blk, k).astype(F32) * scale
        q_idx = blk * MLA_QBLOCK + jnp.arange(MLA_QBLOCK)
        s = jnp.where(key_idx[None, :] <= q_idx[:, None], s, -jnp.inf)
        p = jax.nn.softmax(s, axis=-1).astype(v.dtype)
        return jnp.einsum('bhqk,bkhd->bqhd', p, v)

    o = lax.map(one_block, (qb, jnp.arange(nblk)))
    return o.swapaxes(0, 1).reshape(B_, S, H, v.shape[-1])


def odd_mixer(u, positions, w_in, q_norm_g, w_q_up, kv_norm_g, w_kv_up, w_out):
    B_, S, _ = u.shape
    cq, ckv, k_pe = jnp.split(u @ w_in, [MLA_Q_RANK, MLA_Q_RANK + MLA_KV_RANK], axis=-1)
    q = (rms_norm(cq, q_norm_g) @ w_q_up).reshape(B_, S, MLA_HEADS, MLA_NOPE + MLA_ROPE)
    kv = (rms_norm(ckv, kv_norm_g) @ w_kv_up).reshape(B_, S, MLA_HEADS, MLA_NOPE + MLA_V)
    q_nope, q_pe = q[..., :MLA_NOPE], q[..., MLA_NOPE:]
    k_nope, v = kv[..., :MLA_NOPE], kv[..., MLA_NOPE:]
    cos, sin = rope_tables(positions)
    q_pe = apply_rope(q_pe, cos, sin)
    k_pe = apply_rope(k_pe[:, :, None, :], cos, sin)
    q = jnp.concatenate([q_nope, q_pe], -1)
    k = jnp.concatenate([k_nope, jnp.broadcast_to(k_pe, (B_, S, MLA_HEADS, MLA_ROPE))], -1)
    o = causal_block_attention(q, k, v)
    return o.reshape(B_, S, MLA_HEADS * MLA_V) @ w_out


def setup_inputs(seed: int = 0) -> dict:
    key = jax.random.key(seed)
    ks = iter(jax.random.split(key, 40))
    nrm = lambda shape, s: jax.random.normal(next(ks), shape, F32) * s
    x = jax.random.normal(next(ks), (BATCH, SEQ, D_MODEL), F32)
    c = jax.random.normal(next(ks), (BATCH, D_MODEL), F32)
    offs = jax.random.randint(next(ks), (BATCH, 1), 0, 1024, dtype=jnp.int32)
    positions = jnp.arange(SEQ, dtype=jnp.int32)[None, :] + offs
    mod_w = nrm((DEPTH, D_MODEL, N_MOD * D_MODEL), 0.1 * D_MODEL ** -0.5)
    mod_b = nrm((DEPTH, N_MOD * D_MODEL), 0.01)
    ffn_in = nrm((DEPTH, 2, D_MODEL, 2 * D_FF), D_MODEL ** -0.5)
    ffn_out = nrm((DEPTH, 2, D_FF, D_MODEL), DN_BETA * D_FF ** -0.5)
    ln_g = 1.0 + nrm((DEPTH, 3, D_MODEL), 0.02)
    ln_b = nrm((DEPTH, 3, D_MODEL), 0.02)
    even_w_in = nrm((N_EVEN, D_MODEL, EVEN_IN), D_MODEL ** -0.5)
    conv_w = nrm((N_EVEN, CONV_W, RG_WIDTH), CONV_W ** -0.5)
    conv_b = nrm((N_EVEN, RG_WIDTH), 0.01)
    rg_wa = nrm((N_EVEN, RG_BLOCKS, RG_BLOCK, RG_BLOCK), RG_BLOCK ** -0.5)
    rg_ba = nrm((N_EVEN, RG_WIDTH), 0.01)
    rg_wx = nrm((N_EVEN, RG_BLOCKS, RG_BLOCK, RG_BLOCK), RG_BLOCK ** -0.5)
    rg_bx = nrm((N_EVEN, RG_WIDTH), 0.01)
    a_c = jax.random.uniform(next(ks), (N_EVEN, RG_WIDTH), F32, 0.9, 0.999)
    a0 = a_c ** (1.0 / RG_C)
    rg_lam = jnp.log(a0) - jnp.log1p(-a0)
    gla_wa2 = nrm((N_EVEN, GLA_RANK, GLA_HEADS * GLA_DK), GLA_RANK ** -0.5)
    gla_ba = nrm((N_EVEN, GLA_HEADS * GLA_DK), 0.1)
    gla_norm_g = 1.0 + nrm((N_EVEN, GLA_DV), 0.02)
    even_w_out = nrm((N_EVEN, EVEN_MIX, D_MODEL), DN_BETA * EVEN_MIX ** -0.5)
    odd_w_in = nrm((N_ODD, D_MODEL, ODD_IN), D_MODEL ** -0.5)
    q_norm_g = 1.0 + nrm((N_ODD, MLA_Q_RANK), 0.02)
    w_q_up = nrm((N_ODD, MLA_Q_RANK, MLA_HEADS * (MLA_NOPE + MLA_ROPE)), MLA_Q_RANK ** -0.5)
    kv_norm_g = 1.0 + nrm((N_ODD, MLA_KV_RANK), 0.02)
    w_kv_up = nrm((N_ODD, MLA_KV_RANK, MLA_HEADS * (MLA_NOPE + MLA_V)), MLA_KV_RANK ** -0.5)
    odd_w_out = nrm((N_ODD, MLA_HEADS * MLA_V, D_MODEL), DN_BETA * (MLA_HEADS * MLA_V) ** -0.5)
    return {"x": x, "c": c, "positions": positions, "mod_w": mod_w, "mod_b": mod_b,
            "ffn_in": ffn_in, "ffn_out": ffn_out, "ln_g": ln_g, "ln_b": ln_b,
            "even_w_in": even_w_in, "conv_w": conv_w, "conv_b": conv_b,
            "rg_wa": rg_wa, "rg_ba": rg_ba, "rg_wx": rg_wx, "rg_bx": rg_bx, "rg_lam": rg_lam,
            "gla_wa2": gla_wa2, "gla_ba": gla_ba, "gla_norm_g": gla_norm_g,
            "even_w_out": even_w_out, "odd_w_in": odd_w_in, "q_norm_g": q_norm_g,
            "w_q_up": w_q_up, "kv_norm_g": kv_norm_g, "w_kv_up": w_kv_up,
            "odd_w_out": odd_w_out}


def reference(x, c, positions, mod_w, mod_b, ffn_in, ffn_out, ln_g, ln_b,
              even_w_in, conv_w, conv_b, rg_wa, rg_ba, rg_wx, rg_bx, rg_lam,
              gla_wa2, gla_ba, gla_norm_g, even_w_out, odd_w_in, q_norm_g,
              w_q_up, kv_norm_g, w_kv_up, odd_w_out):
    c_act = jax.nn.silu(c)
    for l in range(DEPTH):
        mod = c_act @ mod_w[l] + mod_b[l]
        sh1, sc1, g1, sh2, sc2, g2, sh3, sc3, g3 = jnp.split(mod, N_MOD, axis=-1)
        f = swiglu(modulate(x, sh1, sc1), ffn_in[l, 0], ffn_out[l, 0])
        x = deepnorm_residual(x, f, g1, ln_g[l, 0], ln_b[l, 0], 0.5)
        u = modulate(x, sh2, sc2)
        if l % 2 == 0:
            e = l // 2
            m = even_mixer(u, even_w_in[e], conv_w[e], conv_b[e], rg_wa[e], rg_ba[e],
                           rg_wx[e], rg_bx[e], rg_lam[e], gla_wa2[e], gla_ba[e],
                           gla_norm_g[e], even_w_out[e])
        else:
            o = l // 2
            m = odd_mixer(u, positions, odd_w_in[o], q_norm_g[o], w_q_up[o],
                          kv_norm_g[o], w_kv_up[o], odd_w_out[o])
        x = deepnorm_residual(x, m, g2, ln_g[l, 1], ln_b[l, 1])
        f = swiglu(modulate(x, sh3, sc3), ffn_in[l, 1], ffn_out[l, 1])
        x = deepnorm_residual(x, f, g3, ln_g[l, 2], ln_b[l, 2], 0.5)
    return x
```

```python
import contextlib
import math
import numpy as np
import concourse.bass as bass
import concourse.mybir as mybir
from concourse.bass_utils import run_bass_kernel_spmd

F32 = mybir.dt.float32
BF16 = mybir.dt.bfloat16
I32 = mybir.dt.int32
AF = mybir.ActivationFunctionType
ALU = mybir.AluOpType
AX = mybir.AxisListType

D = 1024
DFF = 2816
NJ = DFF // 128
DEPTH = 2
DN_ALPHA = (2.0 * DEPTH) ** 0.25
LN_EPS = 1e-5
NCORES = 8

ENGS = ("pe", "act", "dve", "pool", "sp")


class Op:
    __slots__ = ("eng", "fn", "reads", "writes", "is_dma", "semkey", "deps", "signal",
                 "idx", "sigval", "sem")

    def __init__(self, eng, fn, reads, writes, is_dma=False, semkey=None):
        self.eng = eng
        self.fn = fn
        self.reads = tuple(reads)
        self.writes = tuple(writes)
        self.is_dma = is_dma
        self.semkey = semkey
        self.deps = []
        self.signal = False


class Prog:
    def __init__(self):
        self.nc = bass.Bass("TRN2", target_bir_lowering=False)
        self.ops = []
        self.stack = contextlib.ExitStack()
        self.last_writer = {}
        self.readers = {}
        self.last_dma_on_key = {}

    def sb(self, name, shape, dt):
        return self.stack.enter_context(self.nc.sbuf_tensor("s_" + name, list(shape), dt))

    def ps(self, name, shape, dt=F32):
        return self.stack.enter_context(self.nc.psum_tensor("p_" + name, list(shape), dt))

    def dram_in(self, name, shape, dt):
        return self.nc.dram_tensor(name, list(shape), dt, kind="ExternalInput").ap()

    def dram_out(self, name, shape, dt):
        return self.nc.dram_tensor(name, list(shape), dt, kind="ExternalOutput").ap()

    def dram_tmp(self, name, shape, dt):
        return self.nc.dram_tensor(name, list(shape), dt, kind="Internal").ap()

    def _add(self, op):
        deps = set()
        strong = set()
        for b in op.reads:
            w = self.last_writer.get(b)
            if w is not None:
                deps.add(w)
                strong.add(w)
        for b in op.writes:
            w = self.last_writer.get(b)
            if w is not None:
                deps.add(w)
                strong.add(w)
            for r in self.readers.get(b, ()):
                deps.add(r)
        if op.is_dma:
            prev = self.last_dma_on_key.get(op.semkey)
            if prev is not None:
                deps.add(prev)
            self.last_dma_on_key[op.semkey] = op
        deps.discard(op)
        op.idx = len(self.ops)
        op.deps = [d for d in deps if d.is_dma or op.is_dma or d.eng != op.eng
                   or (d in strong and op.eng != "pe")]
        for d in op.deps:
            d.signal = True
        for b in op.writes:
            self.last_writer[b] = op
            self.readers[b] = []
        for b in op.reads:
            self.readers.setdefault(b, []).append(op)
        self.ops.append(op)
        return op

    def op(self, eng, fn, reads=(), writes=()):
        return self._add(Op(eng, fn, reads, writes))

    def dma(self, queue, out, in_, reads=(), writes=(), semkey=None):
        if semkey is None:
            semkey = ("dma", (writes[0] if len(writes) else reads[0]))

        def fn(e, out=out, in_=in_):
            return e.dma_start(out=out, in_=in_)
        return self._add(Op(queue, fn, reads, writes, is_dma=True, semkey=semkey))

    def build(self):
        nc = self.nc
        st = self.stack
        ops = self.ops
        eng_sem = {e: st.enter_context(nc.semaphore("s_" + e)) for e in ENGS}
        dma_sem = {}
        for o in ops:
            if o.is_dma and o.semkey not in dma_sem:
                dma_sem[o.semkey] = st.enter_context(nc.semaphore("d%d" % len(dma_sem)))
        eng_cnt = {e: 0 for e in ENGS}
        dma_cnt = {k: 0 for k in dma_sem}
        for o in ops:
            if o.is_dma:
                dma_cnt[o.semkey] += 16
                o.sigval = dma_cnt[o.semkey]
                o.sem = dma_sem[o.semkey]
            elif o.signal:
                eng_cnt[o.eng] += 1
                o.sigval = eng_cnt[o.eng]
                o.sem = eng_sem[o.eng]
        final_waits = [(dma_sem[k], v) for k, v in dma_cnt.items()]
        block = st.enter_context(nc.Block())
        per_eng = {e: [o for o in ops if o.eng == e] for e in ENGS}

        def emit(eng_obj, ename):
            seen = {}
            for o in per_eng[ename]:
                need = {}
                for d in o.deps:
                    key = id(d.sem)
                    if key not in need or need[key][1] < d.sigval:
                        need[key] = (d.sem, d.sigval)
                for key, (sem, val) in need.items():
                    if seen.get(key, 0) >= val:
                        continue
                    eng_obj.wait_ge(sem, val)
                    seen[key] = val
                ins = o.fn(eng_obj)
                if o.is_dma:
                    ins.then_inc(o.sem, 16)
                elif o.signal:
                    ins.then_inc(o.sem, 1)
            if ename == "sp":
                for sem, val in final_waits:
                    if val > 0:
                        eng_obj.wait_ge(sem, val)

        @block.tensor
        def _(e):
            emit(e, "pe")

        @block.scalar
        def _(e):
            emit(e, "act")

        @block.vector
        def _(e):
            emit(e, "dve")

        @block.gpsimd
        def _(e):
            emit(e, "pool")

        @block.sync
        def _(e):
            emit(e, "sp")

        st.close()
        return nc


class Ctx:
    def __init__(self, P):
        self.P = P
        self.banks = [P.ps("bank%d" % i, [128, 512]) for i in range(8)]
        self.ones_bf = P.sb("ones_bf", [128, 128], BF16)
        self.epsc = P.sb("epsc", [128, 1], F32)
        P.op("pool", lambda e: e.memset(self.ones_bf[:], 1.0 / D), writes=["ones_bf"])
        P.op("pool", lambda e: e.memset(self.epsc[:], LN_EPS / (DN_ALPHA * DN_ALPHA)), writes=["epsc"])


def emit_mod(P, C, tag, cT_d, mw_d, mb_d, wgt, stage_buf, bank_idx=7):
    cT = P.sb(tag + "cT", [128, 8], F32)
    cact = P.sb(tag + "cact", [128, 8], F32)
    mb = P.sb(tag + "mb", [128, 24], F32)
    modv = P.sb(tag + "modv", [128, 24], F32)
    sc1 = P.sb(tag + "sc1", [128, 8], F32)
    gs = P.sb(tag + "gs", [128, 8], F32)
    k = tag + "mod"
    P.dma("sp", cT[:], cT_d, writes=[k + "cT"])
    P.dma("sp", mb[:], mb_d, writes=[k + "mb"])
    P.op("act", lambda e: e.activation(out=cact[:], in_=cT[:], func=AF.Silu), reads=[k + "cT"], writes=[k + "cact"])
    pm = C.banks[bank_idx]
    mwv = mw_d.rearrange("(kc p) n -> p kc n", p=128)
    for piece in range(12):
        sbuf, skey = stage_buf[piece % 2]
        P.dma("sp", sbuf[:], mwv[:, :, piece * 256:(piece + 1) * 256], writes=[skey])
        for jj in range(2):
            jc = piece * 2 + jj

            def mm(e, sbuf=sbuf, jj=jj, jc=jc):
                ins = None
                for kc in range(8):
                    ins = e.matmul(pm[:, jc:jc + 1], lhsT=sbuf[:, kc, jj * 128:(jj + 1) * 128],
                                   rhs=cact[:, kc:kc + 1], start=(kc == 0), stop=(kc == 7))
                return ins
            P.op("pe", mm, reads=[skey, k + "cact"], writes=[("bank", bank_idx)])
    P.op("dve", lambda e: e.tensor_tensor(out=modv[:], in0=pm[:, 0:24], in1=mb[:], op=ALU.add),
         reads=[("bank", bank_idx), k + "mb"], writes=[k + "modv"])
    P.op("dve", lambda e: e.tensor_scalar(out=sc1[:], in0=modv[:, 8:16], scalar1=1.0, scalar2=1.0, op0=ALU.add, op1=ALU.mult),
         reads=[k + "modv"], writes=[k + "vec"])
    P.op("dve", lambda e: e.tensor_scalar(out=gs[:], in0=modv[:, 16:24], scalar1=1.0, scalar2=wgt / DN_ALPHA,
                                          op0=ALU.add, op1=ALU.mult),
         reads=[k + "modv"], writes=[k + "vec"])
    return modv, sc1, gs, k + "vec"


def emit_ln(P, C, tag, s, ysrc, ykeys, gT, bT, gbkey, dst, dkeys, tmp):
    ybf, ysq = tmp["ybf"], tmp["ysq"]
    kb = tag + "ln"
    for dc in range(8):
        P.op("pool", lambda e, dc=dc: e.tensor_copy(out=ybf[:, dc, :], in_=ysrc(dc)),
             reads=[ykeys(dc)], writes=[(kb, "ybf", dc)])
        P.op("act", lambda e, dc=dc: e.activation(out=ysq[:, dc, :], in_=ysrc(dc), func=AF.Square),
             reads=[ykeys(dc)], writes=[(kb, "ysq", dc)])
    b1, b2 = C.banks[6], C.banks[7]

    def mm1(e):
        ins = None
        for dc in range(8):
            ins = e.matmul(b1[:], lhsT=C.ones_bf[:], rhs=ybf[:, dc, :], start=(dc == 0), stop=(dc == 7))
        return ins

    def mm2(e):
        ins = None
        for dc in range(8):
            ins = e.matmul(b2[:], lhsT=C.ones_bf[:], rhs=ysq[:, dc, :], start=(dc == 0), stop=(dc == 7))
        return ins
    P.op("pe", mm1, reads=[(kb, "ybf", dc) for dc in range(8)] + ["ones_bf"], writes=[("bank", 6)])
    P.op("pe", mm2, reads=[(kb, "ysq", dc) for dc in range(8)] + ["ones_bf"], writes=[("bank", 7)])
    mean, var, rstd, nmr = tmp["mean"], tmp["var"], tmp["rstd"], tmp["nmr"]
    P.op("dve", lambda e: e.tensor_copy(out=mean[:], in_=b1[:]), reads=[("bank", 6)], writes=[(kb, "mean")])
    P.op("dve", lambda e: e.tensor_tensor(out=var[:], in0=mean[:], in1=mean[:], op=ALU.mult),
         reads=[(kb, "mean")], writes=[(kb, "var")])
    P.op("dve", lambda e: e.tensor_tensor(out=var[:], in0=b2[:], in1=var[:], op=ALU.subtract),
         reads=[("bank", 7), (kb, "var")], writes=[(kb, "var")])
    P.op("act", lambda e: e.activation(out=rstd[:], in_=var[:], func=AF.Sqrt, bias=C.epsc[:, 0:1], scale=1.0),
         reads=[(kb, "var"), "epsc"], writes=[(kb, "rstd")])
    P.op("dve", lambda e: e.reciprocal(out=rstd[:], in_=rstd[:]), reads=[(kb, "rstd")], writes=[(kb, "rstd")])
    P.op("dve", lambda e: e.scalar_tensor_tensor(out=nmr[:], in0=mean[:], scalar=-1.0, in1=rstd[:],
                                                 op0=ALU.mult, op1=ALU.mult),
         reads=[(kb, "mean"), (kb, "rstd")], writes=[(kb, "nmr")])
    t1r, t2r = tmp["t1"], tmp["t2"]
    for dc in range(8):
        t1 = t1r[dc % 2]
        t2 = t2r[dc % 2]
        P.op("dve", lambda e, dc=dc, t1=t1: e.tensor_tensor(out=t1[:], in0=ysrc(dc), in1=rstd[:], op=ALU.mult),
             reads=[ykeys(dc), (kb, "rstd")], writes=[(kb, "t1", dc % 2)])
        P.op("pool", lambda e, t1=t1, t2=t2: e.tensor_tensor(out=t2[:], in0=t1[:], in1=nmr[:], op=ALU.add),
             reads=[(kb, "t1", dc % 2), (kb, "nmr")], writes=[(kb, "t2", dc % 2)])
        P.op("act", lambda e, dc=dc, t2=t2: e.activation(out=dst(dc), in_=t2[:], func=AF.Identity,
                                                          bias=bT[:, dc:dc + 1], scale=gT[:, dc:dc + 1]),
             reads=[(kb, "t2", dc % 2), gbkey], writes=[dkeys(dc)])


def ln_tmp(P, tag):
    return {
        "ybf": P.sb(tag + "ybf", [128, 8, 512], BF16),
        "ysq": P.sb(tag + "ysq", [128, 8, 512], BF16),
        "mean": P.sb(tag + "mean", [128, 512], F32),
        "var": P.sb(tag + "var", [128, 512], F32),
        "rstd": P.sb(tag + "rstd", [128, 512], F32),
        "nmr": P.sb(tag + "nmr", [128, 512], F32),
        "t1": [P.sb(tag + "t1_%d" % i, [128, 512], F32) for i in range(2)],
        "t2": [P.sb(tag + "t2_%d" % i, [128, 512], F32) for i in range(2)],
    }


class FFNRes:
    def __init__(self, P):
        self.xt = P.sb("f_xt", [128, 8, 1024], F32)
        self.ub = P.sb("f_ub", [128, 8, 1024], BF16)
        self.hT = P.sb("f_hT", [128, NJ, 1024], BF16)
        self.win = [P.sb("f_win%d" % i, [128, 8, 2, 256], BF16) for i in range(3)]
        self.wout = [P.sb("f_wout%d" % i, [128, NJ, 128], BF16) for i in range(2)]
        self.sil = [P.sb("f_sil%d" % i, [128, 512], F32) for i in range(2)]
        self.lnt = ln_tmp(P, "f_")
        self.stage = [(P.sb("f_mst%d" % i, [128, 8, 256], F32), "f_mst%d" % i) for i in range(2)]
        self.g = P.sb("f_lng", [128, 8], F32)
        self.b = P.sb("f_lnb", [128, 8], F32)


def emit_ffn(P, C, R, tag, x_d, y_d, ntok, win_d, wout_d, sh, sc1, gs, veckey, lng_d, lnb_d, dbg=None):
    nsub = ntok // 512
    assert nsub % 2 == 0
    xv = x_d.rearrange("(c p) t -> p c t", p=128)
    yv = y_d.rearrange("(c p) t -> p c t", p=128)
    wv = win_d.rearrange("(kc p) (g n) -> p kc g n", p=128, g=2)
    wov = wout_d.rearrange("(j p) n -> p j n", p=128)
    P.dma("sp", R.g[:], lng_d, writes=["f_gb"], semkey="f_g")
    P.dma("sp", R.b[:], lnb_d, writes=["f_gb"], semkey="f_b")
    xt, ub, hT = R.xt, R.ub, R.hT
    win_cnt = [0]
    wout_cnt = [0]
    pair_cnt = [0]
    obank_cnt = [0]
    sil_cnt = [0]

    def load_win(blk):
        i = win_cnt[0] % 3
        win_cnt[0] += 1
        for g in range(2):
            P.dma("pool", R.win[i][:, :, g, :], wv[:, :, g, blk * 256:(blk + 1) * 256], writes=[("f_win", i, g)])
        return i

    def load_wout(dc):
        i = wout_cnt[0] % 2
        wout_cnt[0] += 1
        P.dma("pool", R.wout[i][:], wov[:, :, dc * 128:(dc + 1) * 128], writes=[("f_wout", i)])
        return i

    NBLK = NJ // 2
    for tp in range(nsub // 2):
        subs = [0, 1]
        t0 = tp * 1024
        for s in subs:
            for half in range(2):
                P.dma("sp", xt[:, half * 4:(half + 1) * 4, s * 512:(s + 1) * 512],
                      xv[:, half * 4:(half + 1) * 4, t0 + s * 512:t0 + (s + 1) * 512],
                      writes=[("f_xt", s, dc) for dc in range(half * 4, half * 4 + 4)],
                      semkey=("f_xtld", s, half))
        pend = [load_win(0), load_win(1)]
        for s in subs:
            for dc in range(8):
                P.op("act", lambda e, s=s, dc=dc: e.activation(
                    out=ub[:, dc, s * 512:(s + 1) * 512], in_=xt[:, dc, s * 512:(s + 1) * 512],
                    func=AF.Identity, bias=sh[:, dc:dc + 1], scale=sc1[:, dc:dc + 1]),
                    reads=[("f_xt", s, dc), veckey], writes=[("f_ub", s, dc)])
        wo_pend = []
        for blk in range(NBLK):
            wi = pend.pop(0)
            if blk + 2 < NBLK:
                pend.append(load_win(blk + 2))
            if blk == NBLK - 2:
                wo_pend.append(load_wout(0))
            if blk == NBLK - 1:
                wo_pend.append(load_wout(1))
            wt = R.win[wi]
            for jj in range(2):
                j = blk * 2 + jj
                for s in subs:
                    pr = pair_cnt[0] % 2
                    pair_cnt[0] += 1
                    bg, bu = C.banks[2 * pr], C.banks[2 * pr + 1]

                    def mmg(e, wt=wt, jj=jj, s=s, bg=bg, g=0):
                        ins = None
                        for kc in range(8):
                            ins = e.matmul(bg[:], lhsT=wt[:, kc, g, jj * 128:(jj + 1) * 128],
                                           rhs=ub[:, kc, s * 512:(s + 1) * 512], start=(kc == 0), stop=(kc == 7))
                        return ins

                    def mmu(e, wt=wt, jj=jj, s=s, bu=bu, g=1):
                        ins = None
                        for kc in range(8):
                            ins = e.matmul(bu[:], lhsT=wt[:, kc, g, jj * 128:(jj + 1) * 128],
                                           rhs=ub[:, kc, s * 512:(s + 1) * 512], start=(kc == 0), stop=(kc == 7))
                        return ins
                    ubk = [("f_ub", s, dc) for dc in range(8)]
                    P.op("pe", mmg, reads=[("f_win", wi, 0)] + ubk, writes=[("bank", 2 * pr)])
                    P.op("pe", mmu, reads=[("f_win", wi, 1)] + ubk, writes=[("bank", 2 * pr + 1)])
                    si = sil_cnt[0] % 2
                    sil_cnt[0] += 1
                    sl = R.sil[si]
                    P.op("act", lambda e, sl=sl, bg=bg: e.activation(out=sl[:], in_=bg[:], func=AF.Silu),
                         reads=[("bank", 2 * pr)], writes=[("f_sil", si)])
                    P.op("dve", lambda e, sl=sl, bu=bu, j=j, s=s: e.tensor_tensor(
                        out=hT[:, j, s * 512:(s + 1) * 512], in0=sl[:], in1=bu[:], op=ALU.mult),
                        reads=[("f_sil", si), ("bank", 2 * pr + 1)], writes=[("f_hT", s, j)])
        for dc in range(8):
            wi = wo_pend.pop(0)
            wt = R.wout[wi]
            for s in subs:
                ob = 4 + obank_cnt[0] % 2
                obank_cnt[0] += 1
                bo = C.banks[ob]

                def mmo(e, wt=wt, s=s, bo=bo):
                    ins = None
                    for j in range(NJ):
                        ins = e.matmul(bo[:], lhsT=wt[:, j, :], rhs=hT[:, j, s * 512:(s + 1) * 512],
                                       start=(j == 0), stop=(j == NJ - 1))
                    return ins
                P.op("pe", mmo, reads=[("f_wout", wi)] + [("f_hT", s, j) for j in range(NJ)], writes=[("bank", ob)])
                P.op("dve", lambda e, dc=dc, s=s, bo=bo: e.scalar_tensor_tensor(
                    out=xt[:, dc, s * 512:(s + 1) * 512], in0=bo[:], scalar=gs[:, dc:dc + 1],
                    in1=xt[:, dc, s * 512:(s + 1) * 512], op0=ALU.mult, op1=ALU.add),
                    reads=[("bank", ob), veckey, ("f_xt", s, dc)], writes=[("f_xt", s, dc)])
            if dc + 2 < 8:
                wo_pend.append(load_wout(dc + 2))
        if dbg is not None and tp == 0:
            P.dma("sp", dbg["ub"], ub[:, :, 0:512], reads=[("f_ub", 0, dc) for dc in range(8)], semkey="dbg1")
            P.dma("sp", dbg["hT"], hT[:, :, 0:512], reads=[("f_hT", 0, j) for j in range(NJ)], semkey="dbg2")
            P.dma("sp", dbg["y"], xt[:, :, 0:512], reads=[("f_xt", 0, dc) for dc in range(8)], semkey="dbg3")
        for s in subs:
            emit_ln(P, C, "f_", s,
                    lambda dc, s=s: xt[:, dc, s * 512:(s + 1) * 512], lambda dc, s=s: ("f_xt", s, dc),
                    R.g, R.b, "f_gb",
                    lambda dc, s=s: xt[:, dc, s * 512:(s + 1) * 512], lambda dc, s=s: ("f_xt", s, dc),
                    R.lnt)
            for half in range(2):
                P.dma("sp", yv[:, half * 4:(half + 1) * 4, t0 + s * 512:t0 + (s + 1) * 512],
                      xt[:, half * 4:(half + 1) * 4, s * 512:(s + 1) * 512],
                      reads=[("f_xt", s, dc) for dc in range(half * 4, half * 4 + 4)],
                      semkey=("f_xtst", s, half))


def build_ffn_prog(ntok, wgt=0.5, debug=False):
    P = Prog()
    dbg = None
    if debug:
        dbg = {"ub": P.dram_out("d_ub", [128, 8, 512], BF16), "hT": P.dram_out("d_hT", [128, NJ, 512], BF16),
               "y": P.dram_out("d_y", [128, 8, 512], F32)}
    x_d = P.dram_in("xT", [D, ntok], F32)
    cT_d = P.dram_in("cT", [128, 8], F32)
    mw_d = P.dram_in("mw", [D, 3 * D], F32)
    mb_d = P.dram_in("mb", [128, 24], F32)
    win_d = P.dram_in("win", [D, 2 * DFF], F32)
    wout_d = P.dram_in("wout", [DFF, D], F32)
    lng_d = P.dram_in("lng", [128, 8], F32)
    lnb_d = P.dram_in("lnb", [128, 8], F32)
    y_d = P.dram_out("yT", [D, ntok], F32)
    C = Ctx(P)
    R = FFNRes(P)
    modv, sc1, gs, vk = emit_mod(P, C, "m_", cT_d, mw_d, mb_d, wgt, R.stage)
    emit_ffn(P, C, R, "f", x_d, y_d, ntok, win_d, wout_d, modv[:, 0:8], sc1, gs, vk, lng_d, lnb_d, dbg=dbg)
    return P.build()


def fm(v):
    v = np.asarray(v)
    return np.ascontiguousarray(v.reshape(-1, 128).T)


def ffn_inputs(xT, c_b, mod_w_l, mod_b_l, sub, ffn_in_lf, ffn_out_lf, ln_g, ln_b):
    return {
        "xT": np.ascontiguousarray(xT),
        "cT": fm(c_b),
        "mw": np.ascontiguousarray(mod_w_l[:, sub * 3 * D:(sub + 1) * 3 * D]),
        "mb": fm(mod_b_l[sub * 3 * D:(sub + 1) * 3 * D]),
        "win": np.ascontiguousarray(ffn_in_lf),
        "wout": np.ascontiguousarray(ffn_out_lf),
        "lng": fm(ln_g),
        "lnb": fm(ln_b),
    }


EV_COLS = 1296


def build_even_prog(S):
    P = Prog()
    op = P.op
    x_d = P.dram_in("xT", [D, S], F32)
    cT_d = P.dram_in("cT", [128, 8], F32)
    mw_d = P.dram_in("mw", [D, 3 * D], F32)
    mb_d = P.dram_in("mb", [128, 24], F32)
    win_d = P.dram_in("win", [D, EV_COLS], F32)
    cw_d = P.dram_in("convw", [128, 2, 4], F32)
    vec_d = P.dram_in("rgvec", [128, 5, 2], F32)
    rgw_d = P.dram_in("rgw", [2, 4, 64, 64], F32)
    wa2_d = P.dram_in("wa2", [16, 128], F32)
    gba_d = P.dram_in("gba", [128, 1], F32)
    ident_d = P.dram_in("ident", [128, 128], F32)
    mask_d = P.dram_in("maskT", [128, 128], F32)
    cm_d = P.dram_in("cmask", [128, 512], F32)
    y_d = P.dram_out("y", [512, S], BF16)

    banks = [P.ps("bank%d" % i, [128, 512]) for i in range(7)]
    tps = P.ps("tps", [128, 512], BF16)

    class C_:
        pass
    C = C_()
    C.banks = banks
    stage = [(P.sb("mst%d" % i, [128, 8, 256], F32), "mst%d" % i) for i in range(2)]
    modv, sc1, gs_unused, vk = emit_mod(P, C, "m_", cT_d, mw_d, mb_d, 1.0, stage, bank_idx=2)
    sh = modv[:, 0:8]

    wi = P.sb("wi", [128, 8, EV_COLS], BF16)
    wv = win_d.rearrange("(kc p) n -> p kc n", p=128)
    for h in range(2):
        P.dma("pool", wi[:, h * 4:(h + 1) * 4, :], wv[:, h * 4:(h + 1) * 4, :], writes=[("wi", h)])
    WI = [("wi", 0), ("wi", 1)]
    cw = P.sb("cw", [128, 2, 4], F32)
    vec = P.sb("vec", [128, 5, 2], F32)
    P.dma("sp", cw[:], cw_d, writes=["cw"])
    P.dma("sp", vec[:], vec_d, writes=["vec"])
    rgf = P.sb("rgf", [128, 2, 2, 128], F32)
    rgb = P.sb("rgb", [128, 2, 2, 128], BF16)
    op("pool", lambda e: e.memset(rgf[:], 0.0), writes=["rgf"])
    for ax in range(2):
        for blk in range(4):
            c, hf = blk // 2, blk % 2
            P.dma("sp", rgf[hf * 64:(hf + 1) * 64, ax, c, hf * 64:(hf + 1) * 64], rgw_d[ax, blk],
                  writes=["rgf"], semkey=("rgf", ax, blk))
    op("dve", lambda e: e.tensor_copy(out=rgb[:], in_=rgf[:]), reads=["rgf"], writes=["rgb"])
    wa2f = P.sb("wa2f", [16, 128], F32)
    wa2b = P.sb("wa2b", [16, 128], BF16)
    P.dma("sp", wa2f[:], wa2_d, writes=["wa2f"])
    op("dve", lambda e: e.tensor_copy(out=wa2b[:], in_=wa2f[:]), reads=["wa2f"], writes=["wa2b"])
    nba = P.sb("nba", [128, 1], F32)
    P.dma("sp", nba[:], gba_d, writes=["nba"])
    op("dve", lambda e: e.tensor_scalar(out=nba[:], in0=nba[:], scalar1=-1.0, scalar2=1.0, op0=ALU.mult, op1=ALU.mult),
       reads=["nba"], writes=["nba"])
    identf = P.sb("identf", [128, 128], F32)
    ident = P.sb("ident", [128, 128], BF16)
    maskT = P.sb("maskT", [128, 128], F32)
    cmask = P.sb("cmask", [128, 512], F32)
    P.dma("sp", identf[:], ident_d, writes=["identf"])
    P.dma("sp", maskT[:], mask_d, writes=["maskT"])
    P.dma("sp", cmask[:], cm_d, writes=["cmask"])
    op("dve", lambda e: e.tensor_copy(out=ident[:], in_=identf[:]), reads=["identf"], writes=["ident"])
    ones256 = P.sb("ones256", [128, 128], BF16)
    op("pool", lambda e: e.memset(ones256[:], 1.0 / 256.0), writes=["ones256"])
    eps1 = P.sb("eps1", [128, 1], F32)
    op("pool", lambda e: e.memset(eps1[:], LN_EPS), writes=["eps1"])
    cl = P.sb("cl", [128, 2, 2], F32)
    tmpl = P.sb("tmpl", [128, 2], F32)
    op("act", lambda e: e.activation(out=tmpl[:], in_=vec[:, 3, :], func=AF.Exp, scale=-1.0), reads=["vec"], writes=["tmpl"])
    op("act", lambda e: e.activation(out=tmpl[:], in_=tmpl[:], func=AF.Ln, bias=1.0, scale=1.0), reads=["tmpl"], writes=["tmpl"])
    op("dve", lambda e: e.tensor_scalar(out=cl[:, 0, :], in0=tmpl[:], scalar1=-8.0, scalar2=1.0, op0=ALU.mult, op1=ALU.mult),
       reads=["tmpl"], writes=["cl"])
    op("dve", lambda e: e.tensor_scalar(out=cl[:, 1, :], in0=tmpl[:], scalar1=-16.0, scalar2=1.0, op0=ALU.mult, op1=ALU.mult),
       reads=["tmpl"], writes=["cl"])

    def two(name, shape, dt):
        return [P.sb("%s%d" % (name, i), shape, dt) for i in range(2)]
    xt = two("xt", [128, 8, 512], F32)
    ub = two("ub", [128, 8, 512], BF16)
    xbuf = P.sb("xbuf", [128, 2, 515], F32)
    op("pool", lambda e: e.memset(xbuf[:], 0.0), writes=[("xbuf", 0), ("xbuf", 1)])
    acc = [P.sb("acc", [128, 2, 512], F32)]
    xcb = [P.sb("xcb", [128, 2, 512], BF16)]
    rr = [P.sb("rr", [128, 2, 512], F32)]
    ii = [P.sb("ii", [128, 2, 512], F32)]
    aa = [P.sb("aa", [128, 2, 512], F32)]
    ss = [P.sb("ss", [128, 2, 512], F32)]
    uu = [P.sb("uu", [128, 2, 512], F32)]
    hh = two("hh", [128, 2, 512], F32)
    gel = [P.sb("gel", [128, 2, 512], F32)]
    yo = two("yo", [128, 4, 512], BF16)
    zab = [P.sb("zab", [16, 512], BF16)]
    sp_ = [P.sb("sp", [128, 512], F32)]
    cs = [P.sb("cs", [128, 512], F32)]
    eb = [P.sb("eb", [128, 512], F32)]
    einv = [P.sb("einv", [128, 512], F32)]
    E8 = two("E8", [128, 8], F32)
    qd = [P.sb("qd", [128, 512], BF16)]
    kinv = [P.sb("kinv", [128, 512], BF16)]
    kend = [P.sb("kend", [128, 512], BF16)]
    vtok = [P.sb("vtok", [128, 4, 256], BF16)]
    sT = [P.sb("sT", [128, 4, 128], BF16)]
    ktok = [P.sb("ktok", [128, 4, 128], BF16)]
    sg = [P.sb("sg", [128, 2, 512], F32)]
    osq = [P.sb("osq", [128, 2, 512], BF16)]
    rs = [P.sb("rs", [128, 512], F32)]
    tt = [P.sb("tt", [128, 2, 512], F32)]
    Sf = P.sb("Sf", [128, 256], F32)
    Sb = two("Sb", [128, 256], BF16)
    op("pool", lambda e: e.memset(Sf[:], 0.0), writes=["Sf"])
    op("pool", lambda e: e.memset(Sb[1][:], 0.0), writes=[("Sb", 1)])

    xv = x_d.rearrange("(c p) t -> p c t", p=128)
    yv = y_d.rearrange("(c p) t -> p c t", p=128)
    NT = S // 512
    rot = [0]

    def nb():
        i = rot[0] % 3
        rot[0] += 1
        return i

    def proj(col0, ncol, tb_, dst_bank, bkey, cols=slice(0, 512)):
        def mm(e):
            ins = None
            for kc in range(8):
                ins = e.matmul(banks[dst_bank][0:ncol, :], lhsT=wi[:, kc, col0:col0 + ncol], rhs=ub[tb_][:, kc, :],
                               start=(kc == 0), stop=(kc == 7))
            return ins
        op("pe", mm, reads=WI + [("ub", tb_, dc) for dc in range(8)], writes=[("bank", dst_bank)])

    def load_x(t):
        tb_ = t % 2
        for half in range(2):
            P.dma("sp", xt[tb_][:, half * 4:(half + 1) * 4, :], xv[:, half * 4:(half + 1) * 4, t * 512:(t + 1) * 512],
                  writes=[("xt", tb_, dc) for dc in range(half * 4, half * 4 + 4)], semkey=("xtld", tb_, half))

    load_x(0)
    sbi = 1
    for t in range(NT):
        tb_ = t % 2
        if t + 1 < NT:
            load_x(t + 1)
        for dc in range(8):
            op("act", lambda e, dc=dc, tb_=tb_: e.activation(out=ub[tb_][:, dc, :], in_=xt[tb_][:, dc, :], func=AF.Identity,
                                                          bias=sh[:, dc:dc + 1], scale=sc1[:, dc:dc + 1]),
               reads=[("xt", tb_, dc), vk], writes=[("ub", tb_, dc)])
        UB = [("ub", tb_, dc) for dc in range(8)]
        b = nb()
        proj(1280, 16, tb_, b, None)
        op("act", lambda e, b=b, tb_=tb_: e.activation(out=zab[0][:], in_=banks[b][0:16, :], func=AF.Identity),
           reads=[("bank", b)], writes=[("zab", 0)])
        b = nb()
        op("pe", lambda e, b=b, tb_=tb_: e.matmul(banks[b][:], lhsT=wa2b[:], rhs=zab[0][:], start=True, stop=True),
           reads=["wa2b", ("zab", 0)], writes=[("bank", b)])
        op("act", lambda e, b=b, tb_=tb_: e.activation(out=sp_[0][:], in_=banks[b][:], func=AF.Exp, scale=-1.0, bias=nba[:, 0:1]),
           reads=[("bank", b), "nba"], writes=[("sp", 0)])
        op("act", lambda e, tb_=tb_: e.activation(out=sp_[0][:], in_=sp_[0][:], func=AF.Ln, bias=1.0, scale=1.0),
           reads=[("sp", 0)], writes=[("sp", 0)])
        op("dve", lambda e, tb_=tb_: e.tensor_tensor_scan(out=cs[0][:], data0=cmask[:], data1=sp_[0][:], initial=0.0,
                                                          op0=ALU.mult, op1=ALU.add),
           reads=["cmask", ("sp", 0)], writes=[("cs", 0)])
        op("act", lambda e, tb_=tb_: e.activation(out=eb[0][:], in_=cs[0][:], func=AF.Exp, scale=-1.0 / 16.0),
           reads=[("cs", 0)], writes=[("eb", 0)])
        op("act", lambda e, tb_=tb_: e.activation(out=einv[0][:], in_=cs[0][:], func=AF.Exp, scale=1.0 / 16.0),
           reads=[("cs", 0)], writes=[("einv", 0)])
        op("act", lambda e, tb_=tb_: e.activation(out=E8[tb_][:], in_=cs[0][:, 63::64], func=AF.Exp, scale=-1.0 / 16.0),
           reads=[("cs", 0)], writes=[("E8", tb_)])
        b = nb()
        proj(640, 128, tb_, b, None)
        op("dve", lambda e, b=b, tb_=tb_: e.tensor_tensor(out=kinv[0][:], in0=banks[b][:], in1=einv[0][:], op=ALU.mult),
           reads=[("bank", b), ("einv", 0)], writes=[("kinv", 0)])
        op("pool", lambda e, tb_=tb_: e.tensor_tensor(out=kend[0][:].rearrange("p (c t) -> p c t", c=8),
                                                      in0=kinv[0][:].rearrange("p (c t) -> p c t", c=8),
                                                      in1=E8[tb_][:].unsqueeze(2).to_broadcast([128, 8, 64]), op=ALU.mult),
           reads=[("kinv", 0), ("E8", tb_)], writes=[("kend", 0)])
        b = nb()
        proj(512, 128, tb_, b, None)
        op("dve", lambda e, b=b, tb_=tb_: e.scalar_tensor_tensor(out=qd[0][:], in0=banks[b][:], scalar=128.0 ** -0.5,
                                                                in1=eb[0][:], op0=ALU.mult, op1=ALU.mult),
           reads=[("bank", b), ("eb", 0)], writes=[("qd", 0)])
        for gp in range(2):
            b = nb()

            def mmv(e, b=b, gp=gp, tb_=tb_):
                ins = None
                for g2 in range(2):
                    gi = gp * 2 + g2
                    for kc in range(8):
                        ins = e.matmul(banks[b][:, g2 * 256:(g2 + 1) * 256], lhsT=ub[tb_][:, kc, gi * 128:(gi + 1) * 128],
                                       rhs=wi[:, kc, 768:1024], start=(kc == 0), stop=(kc == 7))
                return ins
            op("pe", mmv, reads=WI + UB, writes=[("bank", b)])
            op("act", lambda e, b=b, gp=gp, tb_=tb_: e.activation(
                out=vtok[0][:, gp * 2:gp * 2 + 2, :].rearrange("p g v -> p (g v)"), in_=banks[b][:], func=AF.Identity),
               reads=[("bank", b)], writes=[("vtok", 0, gp)])
        for vc in range(2):
            b = nb()
            proj(1024 + vc * 128, 128, tb_, b, None)
            op("act", lambda e, b=b, vc=vc, tb_=tb_: e.activation(out=sg[0][:, vc, :], in_=banks[b][:], func=AF.Silu),
               reads=[("bank", b)], writes=[("sg", 0, vc)])
        for c in range(2):
            b = nb()
            proj(c * 128, 128, tb_, b, None)
            op("pool", lambda e, c=c: e.tensor_copy(out=xbuf[:, c, 0:3], in_=xbuf[:, c, 512:515]),
               reads=[("xbuf", c)], writes=[("xbuf", c)])
            op("act", lambda e, b=b, c=c: e.activation(out=xbuf[:, c, 3:515], in_=banks[b][:], func=AF.Identity),
               reads=[("bank", b)], writes=[("xbuf", c)])
            op("act", lambda e, c=c, tb_=tb_: e.activation(out=acc[0][:, c, :], in_=xbuf[:, c, 3:515], func=AF.Identity,
                                                        scale=cw[:, c, 3:4], bias=vec[:, 0, c:c + 1]),
               reads=[("xbuf", c), "cw", "vec"], writes=[("acc", 0, c)])
            for k in range(3):
                op("dve", lambda e, c=c, k=k, tb_=tb_: e.scalar_tensor_tensor(
                    out=acc[0][:, c, :], in0=xbuf[:, c, k:k + 512], scalar=cw[:, c, k:k + 1], in1=acc[0][:, c, :],
                    op0=ALU.mult, op1=ALU.add),
                   reads=[("xbuf", c), "cw", ("acc", 0, c)], writes=[("acc", 0, c)])
            op("pool", lambda e, c=c, tb_=tb_: e.tensor_copy(out=xcb[0][:, c, :], in_=acc[0][:, c, :]),
               reads=[("acc", 0, c)], writes=[("xcb", 0, c)])
            ba_ = nb()
            op("pe", lambda e, c=c, ba_=ba_, tb_=tb_: e.matmul(banks[ba_][:], lhsT=rgb[:, 0, c, :], rhs=xcb[0][:, c, :],
                                                           start=True, stop=True),
               reads=["rgb", ("xcb", 0, c)], writes=[("bank", ba_)])
            op("act", lambda e, c=c, ba_=ba_, tb_=tb_: e.activation(out=rr[0][:, c, :], in_=banks[ba_][:], func=AF.Sigmoid,
                                                                bias=vec[:, 1, c:c + 1], scale=1.0),
               reads=[("bank", ba_), "vec"], writes=[("rr", 0, c)])
            bx_ = nb()
            op("pe", lambda e, c=c, bx_=bx_, tb_=tb_: e.matmul(banks[bx_][:], lhsT=rgb[:, 1, c, :], rhs=xcb[0][:, c, :],
                                                           start=True, stop=True),
               reads=["rgb", ("xcb", 0, c)], writes=[("bank", bx_)])
            op("act", lambda e, c=c, bx_=bx_, tb_=tb_: e.activation(out=ii[0][:, c, :], in_=banks[bx_][:], func=AF.Sigmoid,
                                                                bias=vec[:, 2, c:c + 1], scale=1.0),
               reads=[("bank", bx_), "vec"], writes=[("ii", 0, c)])
            op("act", lambda e, c=c, tb_=tb_: e.activation(out=aa[0][:, c, :], in_=rr[0][:, c, :], func=AF.Exp,
                                                        scale=cl[:, 0, c:c + 1]),
               reads=[("rr", 0, c), "cl"], writes=[("aa", 0, c)])
            op("act", lambda e, c=c, tb_=tb_: e.activation(out=ss[0][:, c, :], in_=rr[0][:, c, :], func=AF.Exp,
                                                        scale=cl[:, 1, c:c + 1]),
               reads=[("rr", 0, c), "cl"], writes=[("ss", 0, c)])
            op("dve", lambda e, c=c, tb_=tb_: e.tensor_scalar(out=ss[0][:, c, :], in0=ss[0][:, c, :], scalar1=1.0, scalar2=0.0,
                                                           op0=ALU.subtract, op1=ALU.min),
               reads=[("ss", 0, c)], writes=[("ss", 0, c)])
            op("act", lambda e, c=c, tb_=tb_: e.activation(out=ss[0][:, c, :], in_=ss[0][:, c, :], func=AF.Sqrt,
                                                        scale=-1.0),
               reads=[("ss", 0, c)], writes=[("ss", 0, c)])
            op("dve", lambda e, c=c, tb_=tb_: e.tensor_tensor(out=uu[0][:, c, :], in0=ii[0][:, c, :], in1=acc[0][:, c, :],
                                                           op=ALU.mult),
               reads=[("ii", 0, c), ("acc", 0, c)], writes=[("uu", 0, c)])
            op("pool", lambda e, c=c, tb_=tb_: e.tensor_tensor(out=uu[0][:, c, :], in0=uu[0][:, c, :], in1=ss[0][:, c, :],
                                                            op=ALU.mult),
               reads=[("uu", 0, c), ("ss", 0, c)], writes=[("uu", 0, c)])
            init = 0.0 if t == 0 else hh[1 - tb_][:, c, 511:512]
            op("dve", lambda e, c=c, tb_=tb_, init=init: e.tensor_tensor_scan(
                out=hh[tb_][:, c, :], data0=aa[0][:, c, :], data1=uu[0][:, c, :], initial=init, op0=ALU.mult, op1=ALU.add),
               reads=[("aa", 0, c), ("uu", 0, c), ("hh", 1 - tb_, c)], writes=[("hh", tb_, c)])
            bg_ = nb()
            proj(256 + c * 128, 128, tb_, bg_, None)
            op("act", lambda e, c=c, bg_=bg_, tb_=tb_: e.activation(out=gel[0][:, c, :], in_=banks[bg_][:],
                                                                func=AF.Gelu_apprx_tanh),
               reads=[("bank", bg_)], writes=[("gel", 0, c)])
            op("dve", lambda e, c=c, tb_=tb_: e.tensor_tensor(out=yo[tb_][:, c, :], in0=hh[tb_][:, c, :], in1=gel[0][:, c, :],
                                                           op=ALU.mult),
               reads=[("hh", tb_, c), ("gel", 0, c)], writes=[("yo", tb_, c)])
        for gi in range(4):
            gsl = slice(gi * 128, (gi + 1) * 128)
            op("pe", lambda e, gsl=gsl, tb_=tb_: e.matmul(banks[3][:, gsl], lhsT=kinv[0][:, gsl], rhs=qd[0][:, gsl],
                                                      start=True, stop=True),
               reads=[("kinv", 0), ("qd", 0)], writes=[("bank3", gi)])
            op("dve", lambda e, gi=gi, gsl=gsl, tb_=tb_: e.tensor_tensor(out=sT[0][:, gi, :], in0=banks[3][:, gsl], in1=maskT[:],
                                                                    op=ALU.mult),
               reads=[("bank3", gi), "maskT"], writes=[("sT", 0, gi)])
            op("pe", lambda e, gsl=gsl, tb_=tb_: e.transpose(tps[:, gsl], kend[0][:, gsl], ident[:]),
               reads=[("kend", 0), "ident"], writes=[("tps", gi)])
            op("pool" if False else "act", lambda e, gi=gi, gsl=gsl, tb_=tb_: e.activation(out=ktok[0][:, gi, :], in_=tps[:, gsl],
                                                                                       func=AF.Identity),
               reads=[("tps", gi)], writes=[("ktok", 0, gi)])
            for hf in range(2):
                c = gi * 2 + hf
                po = hf * 64
                csl = slice(c * 64, (c + 1) * 64)
                for vc in range(2):
                    def mmo(e, vc=vc, csl=csl, po=po, gi=gi, tb_=tb_, sbi=sbi):
                        e.matmul(banks[5 + vc][:, csl], lhsT=Sb[sbi][:, vc * 128:(vc + 1) * 128], rhs=qd[0][:, csl],
                                 start=True, stop=False)
                        return e.matmul(banks[5 + vc][:, csl], lhsT=vtok[0][po:po + 64, gi, vc * 128:(vc + 1) * 128],
                                        rhs=sT[0][po:po + 64, gi, po:po + 64], start=False, stop=True)
                    op("pe", mmo, reads=[("Sb", sbi), ("qd", 0), ("vtok", 0, gi // 2), ("sT", 0, gi)],
                       writes=[("obank", vc, c)])
                slot = c % 2
                op("pe", lambda e, po=po, gi=gi, tb_=tb_, slot=slot: e.matmul(
                    banks[4][:, slot * 256:(slot + 1) * 256], lhsT=ktok[0][po:po + 64, gi, :], rhs=vtok[0][po:po + 64, gi, :],
                    start=True, stop=True),
                   reads=[("ktok", 0, gi), ("vtok", 0, gi // 2)], writes=[("bank4", slot)])
                op("dve", lambda e, c=c, tb_=tb_, slot=slot: e.scalar_tensor_tensor(
                    out=Sf[:], in0=Sf[:], scalar=E8[tb_][:, c:c + 1], in1=banks[4][:, slot * 256:(slot + 1) * 256],
                    op0=ALU.mult, op1=ALU.add),
                   reads=["Sf", ("E8", tb_), ("bank4", slot)], writes=["Sf"])
                sbi = 1 - sbi
                op("act", lambda e, sbi=sbi: e.activation(out=Sb[sbi][:], in_=Sf[:], func=AF.Identity),
                   reads=["Sf"], writes=[("Sb", sbi)])
        OB = lambda vc: [("obank", vc, c) for c in range(8)]
        for vc in range(2):
            op("act", lambda e, vc=vc, tb_=tb_: e.activation(out=osq[0][:, vc, :], in_=banks[5 + vc][:], func=AF.Square),
               reads=OB(vc), writes=[("osq", 0, vc)])
        b = nb()

        def mms(e, b=b, tb_=tb_):
            e.matmul(banks[b][:], lhsT=ones256[:], rhs=osq[0][:, 0, :], start=True, stop=False)
            return e.matmul(banks[b][:], lhsT=ones256[:], rhs=osq[0][:, 1, :], start=False, stop=True)
        op("pe", mms, reads=["ones256", ("osq", 0, 0), ("osq", 0, 1)], writes=[("bank", b)])
        op("act", lambda e, b=b, tb_=tb_: e.activation(out=rs[0][:], in_=banks[b][:], func=AF.Sqrt, bias=eps1[:, 0:1], scale=1.0),
           reads=[("bank", b), "eps1"], writes=[("rs", 0)])
        op("dve", lambda e, tb_=tb_: e.reciprocal(out=rs[0][:], in_=rs[0][:]), reads=[("rs", 0)], writes=[("rs", 0)])
        for vc in range(2):
            op("dve", lambda e, vc=vc, tb_=tb_: e.tensor_tensor(out=tt[0][:, vc, :], in0=banks[5 + vc][:], in1=rs[0][:],
                                                             op=ALU.mult),
               reads=OB(vc) + [("rs", 0)], writes=[("tt", 0, vc)])
            op("dve", lambda e, vc=vc, tb_=tb_: e.scalar_tensor_tensor(
                out=yo[tb_][:, 2 + vc, :], in0=tt[0][:, vc, :], scalar=vec[:, 4, vc:vc + 1], in1=sg[0][:, vc, :],
                op0=ALU.mult, op1=ALU.mult),
               reads=[("tt", 0, vc), "vec", ("sg", 0, vc)], writes=[("yo", tb_, 2 + vc)])
        P.dma("sp", yv[:, :, t * 512:(t + 1) * 512], yo[tb_][:], reads=[("yo", tb_, i) for i in range(4)],
              semkey=("yst", tb_))
    return P.build()


def even_consts():
    ident = np.eye(128, dtype=np.float32)
    j = np.arange(128)[:, None]
    i = np.arange(128)[None, :]
    maskT = ((j // 64 == i // 64) & (j <= i)).astype(np.float32)
    cm = np.ones((128, 512), np.float32)
    cm[:, ::64] = 0.0
    return ident, maskT, cm


def even_inputs(x1T_b, c_b, mod_w_l, mod_b_l, q, I):
    ident, maskT, cm = even_consts()
    w = I["even_w_in"][0]
    cols = np.concatenate([
        np.arange(q * 256, (q + 1) * 256),
        1024 + np.arange(q * 256, (q + 1) * 256),
        2048 + np.arange(q * 128, (q + 1) * 128),
        2560 + np.arange(q * 128, (q + 1) * 128),
        3072 + np.arange(q * 256, (q + 1) * 256),
        4096 + np.arange(q * 256, (q + 1) * 256),
        5120 + np.arange(16)])
    ch = slice(q * 256, (q + 1) * 256)
    convw = np.ascontiguousarray(I["conv_w"][0][:, ch].reshape(4, 2, 128).transpose(2, 1, 0))
    vec = np.stack([fm(I["conv_b"][0][ch]), fm(I["rg_ba"][0][ch]), fm(I["rg_bx"][0][ch]), fm(I["rg_lam"][0][ch]),
                    fm(I["gla_norm_g"][0])], axis=1)
    rgw = np.stack([I["rg_wa"][0][q * 4:(q + 1) * 4], I["rg_wx"][0][q * 4:(q + 1) * 4]], axis=0)
    return {
        "xT": np.ascontiguousarray(x1T_b), "cT": fm(c_b),
        "mw": np.ascontiguousarray(mod_w_l[:, 3 * D:6 * D]), "mb": fm(mod_b_l[3 * D:6 * D]),
        "win": np.ascontiguousarray(w[:, cols]), "convw": convw, "rgvec": np.ascontiguousarray(vec),
        "rgw": np.ascontiguousarray(rgw), "wa2": np.ascontiguousarray(I["gla_wa2"][0][:, q * 128:(q + 1) * 128]),
        "gba": fm(I["gla_ba"][0][q * 128:(q + 1) * 128]),
        "ident": ident, "maskT": maskT, "cmask": cm,
    }


def build_mixout_prog(ntok, KC):
    P = Prog()
    op = P.op
    x_d = P.dram_in("xT", [D, ntok], F32)
    y_d = P.dram_in("yT", [KC * 128, ntok], BF16)
    cT_d = P.dram_in("cT", [128, 8], F32)
    mw_d = P.dram_in("mw", [D, 3 * D], F32)
    mb_d = P.dram_in("mb", [128, 24], F32)
    w_d = P.dram_in("wout", [KC * 128, D], F32)
    lng_d = P.dram_in("lng", [128, 8], F32)
    lnb_d = P.dram_in("lnb", [128, 8], F32)
    o_d = P.dram_out("oT", [D, ntok], F32)
    C = Ctx(P)
    stage = [(P.sb("mst%d" % i, [128, 8, 256], F32), "mst%d" % i) for i in range(2)]
    modv, sc1, gs, vk = emit_mod(P, C, "m_", cT_d, mw_d, mb_d, 1.0, stage)
    g = P.sb("lng", [128, 8], F32)
    b = P.sb("lnb", [128, 8], F32)
    P.dma("sp", g[:], lng_d, writes=["gb"], semkey="g")
    P.dma("sp", b[:], lnb_d, writes=["gb"], semkey="b")
    w = P.sb("w", [128, KC, D], BF16)
    wv = w_d.rearrange("(kc p) n -> p kc n", p=128)
    for h in range(KC // 4):
        P.dma("pool", w[:, h * 4:(h + 1) * 4, :], wv[:, h * 4:(h + 1) * 4, :], writes=[("w", h)])
    WK = [("w", h) for h in range(KC // 4)]
    xt = [P.sb("xt%d" % i, [128, 8, 512], F32) for i in range(2)]
    yt = [P.sb("yt%d" % i, [128, KC, 512], BF16) for i in range(2)]
    lnt = ln_tmp(P, "l_")
    xv = x_d.rearrange("(c p) t -> p c t", p=128)
    yv = y_d.rearrange("(c p) t -> p c t", p=128)
    ov = o_d.rearrange("(c p) t -> p c t", p=128)
    NT = ntok // 512

    def load(t):
        tb = t % 2
        for half in range(2):
            P.dma("sp", xt[tb][:, half * 4:(half + 1) * 4, :], xv[:, half * 4:(half + 1) * 4, t * 512:(t + 1) * 512],
                  writes=[("xt", tb, dc) for dc in range(half * 4, half * 4 + 4)], semkey=("xld", tb, half))
        for h in range(KC // 4):
            P.dma("sp", yt[tb][:, h * 4:(h + 1) * 4, :], yv[:, h * 4:(h + 1) * 4, t * 512:(t + 1) * 512],
                  writes=[("yt", tb, h)], semkey=("yld", tb, h))
    load(0)
    cnt = 0
    for t in range(NT):
        tb = t % 2
        if t + 1 < NT:
            load(t + 1)
        for dc in range(8):
            bi = cnt % 4
            cnt += 1

            def mm(e, dc=dc, bi=bi, tb=tb):
                ins = None
                for kc in range(KC):
                    ins = e.matmul(C.banks[bi][:], lhsT=w[:, kc, dc * 128:(dc + 1) * 128], rhs=yt[tb][:, kc, :],
                                   start=(kc == 0), stop=(kc == KC - 1))
                return ins
            op("pe", mm, reads=WK + [("yt", tb, h) for h in range(KC // 4)], writes=[("bank", bi)])
            op("dve", lambda e, dc=dc, bi=bi, tb=tb: e.scalar_tensor_tensor(
                out=xt[tb][:, dc, :], in0=C.banks[bi][:], scalar=gs[:, dc:dc + 1], in1=xt[tb][:, dc, :],
                op0=ALU.mult, op1=ALU.add),
               reads=[("bank", bi), vk, ("xt", tb, dc)], writes=[("xt", tb, dc)])
        emit_ln(P, C, "l_", 0, lambda dc, tb=tb: xt[tb][:, dc, :], lambda dc, tb=tb: ("xt", tb, dc), g, b, "gb",
                lambda dc, tb=tb: xt[tb][:, dc, :], lambda dc, tb=tb: ("xt", tb, dc), lnt)
        for half in range(2):
            P.dma("sp", ov[:, half * 4:(half + 1) * 4, t * 512:(t + 1) * 512], xt[tb][:, half * 4:(half + 1) * 4, :],
                  reads=[("xt", tb, dc) for dc in range(half * 4, half * 4 + 4)], semkey=("ost", tb, half))
    return P.build()


def mixout_inputs(xT, yT, c_b, mod_w_l, mod_b_l, w_out, ln_g, ln_b):
    return {"xT": np.ascontiguousarray(xT), "yT": np.ascontiguousarray(yT), "cT": fm(c_b),
            "mw": np.ascontiguousarray(mod_w_l[:, 3 * D:6 * D]), "mb": fm(mod_b_l[3 * D:6 * D]),
            "wout": np.ascontiguousarray(w_out), "lng": fm(ln_g), "lnb": fm(ln_b)}


QSCALE = 192.0 ** -0.5
C1_2PI = 6.28125
C2_2PI = 2.0 * math.pi - 6.28125
PI_LO = 3.141592


def build_mlaproj_prog(ntok):
    P = Prog()
    op = P.op
    x_d = P.dram_in("xT", [D, ntok], F32)
    cT_d = P.dram_in("cT", [128, 8], F32)
    mw_d = P.dram_in("mw", [D, 3 * D], F32)
    mb_d = P.dram_in("mb", [128, 24], F32)
    win_d = P.dram_in("win", [D, 448], F32)
    ng_d = P.dram_in("ng", [128, 3], F32)
    wq_d = P.dram_in("wq", [256, 1536], F32)
    wkv_d = P.dram_in("wkv", [128, 2048], F32)
    pos_d = P.dram_in("pos", [1, ntok], I32)
    fr_d = P.dram_in("fr", [64, 1], F32)
    q_o = P.dram_out("qT", [1536, ntok], BF16)
    kn_o = P.dram_out("kTn", [1024, ntok], BF16)
    kr_o = P.dram_out("kTr", [64, ntok], BF16)
    v_o = P.dram_out("v", [ntok, 1024], BF16)

    banks = [P.ps("bank%d" % i, [128, 512]) for i in range(8)]

    class C_:
        pass
    C = C_()
    C.banks = banks
    stage = [(P.sb("mst%d" % i, [128, 8, 256], F32), "mst%d" % i) for i in range(2)]
    modv, sc1, gs_unused, vk = emit_mod(P, C, "m_", cT_d, mw_d, mb_d, 1.0, stage, bank_idx=7)
    sh = modv[:, 0:8]

    wi = P.sb("wi", [128, 8, 448], BF16)
    wir = P.sb("wir", [128, 8, 64], BF16)
    P.dma("pool", wi[:], win_d.rearrange("(kc p) n -> p kc n", p=128), writes=["wi"])
    op("dve", lambda e: e.tensor_scalar(out=wir[:, :, 0:32], in0=wi[:, :, 416:448], scalar1=-1.0, scalar2=1.0,
                                        op0=ALU.mult, op1=ALU.mult), reads=["wi"], writes=["wir"])
    op("dve", lambda e: e.tensor_copy(out=wir[:, :, 32:64], in_=wi[:, :, 384:416]), reads=["wi"], writes=["wir"])
    wq = P.sb("wq", [128, 2, 1536], BF16)
    P.dma("pool", wq[:], wq_d.rearrange("(kc p) n -> p kc n", p=128), writes=["wq"])
    wqr = P.sb("wqr", [128, 2, 8, 64], BF16)
    for kc in range(2):
        wq4 = wq[:, kc, :].rearrange("p (h n) -> p h n", h=8)
        op("dve", lambda e, kc=kc, wq4=wq4: e.tensor_scalar(out=wqr[:, kc, :, 0:32], in0=wq4[:, :, 160:192], scalar1=-1.0,
                                                          scalar2=1.0, op0=ALU.mult, op1=ALU.mult),
           reads=["wq"], writes=["wqr"])
        op("dve", lambda e, kc=kc, wq4=wq4: e.tensor_copy(out=wqr[:, kc, :, 32:64], in_=wq4[:, :, 128:160]),
           reads=["wq"], writes=["wqr"])
    wkv = P.sb("wkv", [128, 2048], BF16)
    P.dma("pool", wkv[:], wkv_d, writes=["wkv"])
    ng = P.sb("ng", [128, 3], F32)
    P.dma("sp", ng[:], ng_d, writes=["ng"])
    ones256 = P.sb("ones256", [128, 128], BF16)
    ones128 = P.sb("ones128", [128, 128], BF16)
    op("pool", lambda e: e.memset(ones256[:], 1.0 / 256.0), writes=["ones256"])
    op("pool", lambda e: e.memset(ones128[:], 1.0 / 128.0), writes=["ones128"])
    eps1 = P.sb("eps1", [128, 1], F32)
    op("pool", lambda e: e.memset(eps1[:], LN_EPS), writes=["eps1"])

    pi_ = P.sb("posi", [64, 512], I32)
    fr = P.sb("fr", [64, 1], F32)
    ang = P.sb("ang", [64, 512], F32)
    kf = P.sb("kf", [64, 512], F32)
    ki = P.sb("ki", [64, 512], I32)
    cos2 = P.sb("cos2", [64, 512], F32)
    sin2 = P.sb("sin2", [64, 512], F32)
    cos2q = P.sb("cos2q", [64, 512], F32)
    sin2q = P.sb("sin2q", [64, 512], F32)
    P.dma("sp", fr[:], fr_d, writes=["fr"])

    def rope_tables(t):
        P.dma("sp", pi_[:], pos_d[0:1, t * 512:(t + 1) * 512].partition_broadcast(64), writes=["posi"])
        op("dve", lambda e: e.tensor_copy(out=ang[:], in_=pi_[:]), reads=["posi"], writes=["ang"])
        op("dve", lambda e: e.tensor_scalar(out=ang[:], in0=ang[:], scalar1=fr[:, 0:1], scalar2=1.0, op0=ALU.mult, op1=ALU.mult),
           reads=["ang", "fr"], writes=["ang"])
        op("dve", lambda e: e.tensor_scalar(out=kf[:], in0=ang[:], scalar1=1.0 / (2.0 * math.pi), scalar2=1.0, op0=ALU.mult,
                                            op1=ALU.mult), reads=["ang"], writes=["kf"])
        op("dve", lambda e: e.tensor_copy(out=ki[:], in_=kf[:]), reads=["kf"], writes=["ki"])
        op("dve", lambda e: e.tensor_copy(out=kf[:], in_=ki[:]), reads=["ki"], writes=["kf"])
        op("dve", lambda e: e.scalar_tensor_tensor(out=ang[:], in0=kf[:], scalar=-C1_2PI, in1=ang[:], op0=ALU.mult, op1=ALU.add),
           reads=["kf", "ang"], writes=["ang"])
        op("dve", lambda e: e.scalar_tensor_tensor(out=ang[:], in0=kf[:], scalar=-C2_2PI, in1=ang[:], op0=ALU.mult, op1=ALU.add),
           reads=["kf", "ang"], writes=["ang"])
        op("dve", lambda e: e.tensor_scalar(out=sin2[:], in0=ang[:], scalar1=-PI_LO, scalar2=PI_LO, op0=ALU.max, op1=ALU.min),
           reads=["ang"], writes=["sin2"])
        op("act", lambda e: e.activation(out=sin2[:], in_=sin2[:], func=AF.Sin), reads=["sin2"], writes=["sin2"])
        op("dve", lambda e: e.tensor_scalar(out=cos2[:], in0=ang[:], scalar1=math.pi / 2.0, scalar2=1.0, op0=ALU.add, op1=ALU.mult),
           reads=["ang"], writes=["cos2"])
        op("dve", lambda e: e.tensor_scalar(out=kf[:], in0=cos2[:], scalar1=math.pi, scalar2=2.0 * math.pi, op0=ALU.is_gt,
                                            op1=ALU.mult), reads=["cos2"], writes=["kf"])
        op("dve", lambda e: e.tensor_tensor(out=cos2[:], in0=cos2[:], in1=kf[:], op=ALU.subtract), reads=["cos2", "kf"],
           writes=["cos2"])
        op("dve", lambda e: e.tensor_scalar(out=cos2[:], in0=cos2[:], scalar1=-PI_LO, scalar2=PI_LO, op0=ALU.max, op1=ALU.min),
           reads=["cos2"], writes=["cos2"])
        op("act", lambda e: e.activation(out=cos2[:], in_=cos2[:], func=AF.Sin), reads=["cos2"], writes=["cos2"])
        op("pool", lambda e: e.tensor_scalar(out=cos2q[:], in0=cos2[:], scalar1=QSCALE, scalar2=1.0, op0=ALU.mult, op1=ALU.mult),
           reads=["cos2"], writes=["cos2q"])
        op("pool", lambda e: e.tensor_scalar(out=sin2q[:], in0=sin2[:], scalar1=QSCALE, scalar2=1.0, op0=ALU.mult, op1=ALU.mult),
           reads=["sin2"], writes=["sin2q"])
    TAB = ["cos2", "sin2", "cos2q", "sin2q"]

    xt = [P.sb("xt%d" % i, [128, 8, 512], F32) for i in range(2)]
    ub = [P.sb("ub%d" % i, [128, 8, 512], BF16) for i in range(2)]
    sq = P.sb("sq", [128, 3, 512], BF16)
    rstd = P.sb("rstd", [128, 2, 512], F32)
    cqn = P.sb("cqn", [128, 2, 512], BF16)
    ckvn = P.sb("ckvn", [128, 512], BF16)
    t1 = [P.sb("t1_%d" % i, [64, 512], F32) for i in range(2)]
    t2 = [P.sb("t2_%d" % i, [64, 512], F32) for i in range(2)]
    krs = P.sb("krs", [64, 512], BF16)
    qst = [P.sb("qst%d" % i, [128, 512], BF16) for i in range(2)]
    qrs = [P.sb("qrs%d" % i, [64, 512], BF16) for i in range(2)]
    kst = [P.sb("kst%d" % i, [128, 512], BF16) for i in range(2)]
    vst = [P.sb("vst%d" % i, [128, 1024], BF16) for i in range(2)]
    xv = x_d.rearrange("(c p) t -> p c t", p=128)
    NT = ntok // 512
    rot = [0]

    def nb():
        i = rot[0] % 8
        rot[0] += 1
        return i

    def load(t):
        tb = t % 2
        for half in range(2):
            P.dma("sp", xt[tb][:, half * 4:(half + 1) * 4, :], xv[:, half * 4:(half + 1) * 4, t * 512:(t + 1) * 512],
                  writes=[("xt", tb, dc) for dc in range(half * 4, half * 4 + 4)], semkey=("xld", tb, half))
    load(0)
    cnt = [0]
    def do_tile(t):
        tb = t % 2
        tsl = slice(t * 512, (t + 1) * 512)
        if t + 1 < NT:
            load(t + 1)
        rope_tables(t)
        for dc in range(8):
            op("act", lambda e, dc=dc, tb=tb: e.activation(out=ub[tb][:, dc, :], in_=xt[tb][:, dc, :], func=AF.Identity,
                                                        bias=sh[:, dc:dc + 1], scale=sc1[:, dc:dc + 1]),
               reads=[("xt", tb, dc), vk], writes=[("ub", tb, dc)])
        UB = [("ub", tb, dc) for dc in range(8)]

        def proj(col0, ncol, b, wt=None):
            wt_ = wi if wt is None else wt

            def mm(e):
                ins = None
                for kc in range(8):
                    ins = e.matmul(banks[b][0:ncol, :], lhsT=wt_[:, kc, col0:col0 + ncol], rhs=ub[tb][:, kc, :],
                                   start=(kc == 0), stop=(kc == 7))
                return ins
            op("pe", mm, reads=["wi", "wir"] + UB, writes=[("bank", b)])
        bq0, bq1, bkv, bkp, bkr = nb(), nb(), nb(), nb(), nb()
        proj(0, 128, bq0)
        proj(128, 128, bq1)
        proj(256, 128, bkv)
        proj(384, 64, bkp)
        proj(0, 64, bkr, wt=wir)
        for i, b in enumerate([bq0, bq1, bkv]):
            op("act", lambda e, i=i, b=b: e.activation(out=sq[:, i, :], in_=banks[b][:], func=AF.Square),
               reads=[("bank", b)], writes=[("sq", i)])
        bs_q, bs_kv = nb(), nb()

        def mmsq(e):
            e.matmul(banks[bs_q][:], lhsT=ones256[:], rhs=sq[:, 0, :], start=True, stop=False)
            return e.matmul(banks[bs_q][:], lhsT=ones256[:], rhs=sq[:, 1, :], start=False, stop=True)
        op("pe", mmsq, reads=["ones256", ("sq", 0), ("sq", 1)], writes=[("bank", bs_q)])
        op("pe", lambda e: e.matmul(banks[bs_kv][:], lhsT=ones128[:], rhs=sq[:, 2, :], start=True, stop=True),
           reads=["ones128", ("sq", 2)], writes=[("bank", bs_kv)])
        for i, b in enumerate([bs_q, bs_kv]):
            op("act", lambda e, i=i, b=b: e.activation(out=rstd[:, i, :], in_=banks[b][:], func=AF.Sqrt, bias=eps1[:, 0:1],
                                                       scale=1.0), reads=[("bank", b), "eps1"], writes=[("rstd", i)])
            op("dve", lambda e, i=i: e.reciprocal(out=rstd[:, i, :], in_=rstd[:, i, :]), reads=[("rstd", i)],
               writes=[("rstd", i)])
        for i, b in enumerate([bq0, bq1]):
            op("dve", lambda e, i=i, b=b: e.scalar_tensor_tensor(out=cqn[:, i, :], in0=banks[b][:], scalar=ng[:, i:i + 1],
                                                                 in1=rstd[:, 0, :], op0=ALU.mult, op1=ALU.mult),
               reads=[("bank", b), "ng", ("rstd", 0)], writes=[("cqn", i)])
        op("dve", lambda e: e.scalar_tensor_tensor(out=ckvn[:], in0=banks[bkv][:], scalar=ng[:, 2:3], in1=rstd[:, 1, :],
                                                   op0=ALU.mult, op1=ALU.mult),
           reads=[("bank", bkv), "ng", ("rstd", 1)], writes=["ckvn"])
        op("dve", lambda e: e.tensor_tensor(out=t1[0][:], in0=banks[bkp][0:64, :], in1=cos2[:], op=ALU.mult),
           reads=[("bank", bkp)] + TAB, writes=[("t1", 0)])
        op("dve", lambda e: e.tensor_tensor(out=t2[0][:], in0=banks[bkr][0:64, :], in1=sin2[:], op=ALU.mult),
           reads=[("bank", bkr)] + TAB, writes=[("t2", 0)])
        op("pool", lambda e: e.tensor_tensor(out=krs[:], in0=t1[0][:], in1=t2[0][:], op=ALU.add),
           reads=[("t1", 0), ("t2", 0)], writes=["krs"])
        P.dma("sp", kr_o[:, tsl], krs[:], reads=["krs"], semkey="krst")
        for h in range(8):
            i2 = cnt[0] % 2
            cnt[0] += 1
            bn, br, brr = nb(), nb(), nb()

            def mmq(e, h=h, bn=bn, br=br, brr=brr):
                for kc in range(2):
                    e.matmul(banks[bn][:], lhsT=wq[:, kc, h * 192:h * 192 + 128], rhs=cqn[:, kc, :], start=(kc == 0),
                             stop=(kc == 1))
                for kc in range(2):
                    e.matmul(banks[br][0:64, :], lhsT=wq[:, kc, h * 192 + 128:h * 192 + 192], rhs=cqn[:, kc, :],
                             start=(kc == 0), stop=(kc == 1))
                ins = None
                for kc in range(2):
                    ins = e.matmul(banks[brr][0:64, :], lhsT=wqr[:, kc, h, :], rhs=cqn[:, kc, :], start=(kc == 0),
                                   stop=(kc == 1))
                return ins
            op("pe", mmq, reads=["wq", "wqr", ("cqn", 0), ("cqn", 1)], writes=[("bank", bn), ("bank", br), ("bank", brr)])
            op("act", lambda e, bn=bn, i2=i2: e.activation(out=qst[i2][:], in_=banks[bn][:], func=AF.Identity, scale=QSCALE),
               reads=[("bank", bn)], writes=[("qst", i2)])
            op("dve", lambda e, br=br, i2=i2: e.tensor_tensor(out=t1[i2][:], in0=banks[br][0:64, :], in1=cos2q[:],
                                                            op=ALU.mult), reads=[("bank", br)] + TAB, writes=[("t1", i2)])
            op("dve", lambda e, brr=brr, i2=i2: e.tensor_tensor(out=t2[i2][:], in0=banks[brr][0:64, :], in1=sin2q[:],
                                                              op=ALU.mult), reads=[("bank", brr)] + TAB, writes=[("t2", i2)])
            op("pool", lambda e, i2=i2: e.tensor_tensor(out=qrs[i2][:], in0=t1[i2][:], in1=t2[i2][:], op=ALU.add),
               reads=[("t1", i2), ("t2", i2)], writes=[("qrs", i2)])
            P.dma("sp", q_o[h * 192:h * 192 + 128, tsl], qst[i2][:], reads=[("qst", i2)], semkey=("qst", i2))
            P.dma("sp", q_o[h * 192 + 128:h * 192 + 192, tsl], qrs[i2][:], reads=[("qrs", i2)], semkey=("qrs", i2))
            bk = nb()
            op("pe", lambda e, h=h, bk=bk: e.matmul(banks[bk][:], lhsT=wkv[:, h * 256:h * 256 + 128], rhs=ckvn[:],
                                                   start=True, stop=True), reads=["wkv", "ckvn"], writes=[("bank", bk)])
            op("act", lambda e, bk=bk, i2=i2: e.activation(out=kst[i2][:], in_=banks[bk][:], func=AF.Identity),
               reads=[("bank", bk)], writes=[("kst", i2)])
            P.dma("sp", kn_o[h * 128:(h + 1) * 128, tsl], kst[i2][:], reads=[("kst", i2)], semkey=("kst", i2))
        wkv3 = wkv[:].rearrange("p (h n) -> p h n", h=8)
        for g in range(4):
            i2 = g % 2
            b0, b1 = nb(), nb()

            def mmv(e, g=g, b0=b0, b1=b1):
                e.matmul(banks[b0][:].rearrange("p (h n) -> p h n", h=4), lhsT=ckvn[:, g * 128:(g + 1) * 128],
                         rhs=wkv3[:, 0:4, 128:256], start=True, stop=True)
                return e.matmul(banks[b1][:].rearrange("p (h n) -> p h n", h=4), lhsT=ckvn[:, g * 128:(g + 1) * 128],
                                rhs=wkv3[:, 4:8, 128:256], start=True, stop=True)
            op("pe", mmv, reads=["wkv", "ckvn"], writes=[("bank", b0), ("bank", b1)])
            op("act", lambda e, b0=b0, i2=i2: e.activation(out=vst[i2][:, 0:512], in_=banks[b0][:], func=AF.Identity),
               reads=[("bank", b0)], writes=[("vst", i2, 0)])
            op("dve", lambda e, b1=b1, i2=i2: e.tensor_copy(out=vst[i2][:, 512:1024], in_=banks[b1][:]),
               reads=[("bank", b1)], writes=[("vst", i2, 1)])
            P.dma("sp", v_o[t * 512 + g * 128:t * 512 + (g + 1) * 128, :], vst[i2][:], reads=[("vst", i2, 0), ("vst", i2, 1)],
                  semkey=("vst", i2))
    for t_ in range(NT):
        do_tile(t_)
    return P.build()


def rope_freqs():
    half = 32
    f = (10000.0 ** (-np.arange(half, dtype=np.float32) / half)).astype(np.float32)
    return np.ascontiguousarray(np.concatenate([f, f])[:, None])


def mlaproj_inputs(xT, c_b, mod_w_l, mod_b_l, pos, I):
    return {"xT": np.ascontiguousarray(xT), "cT": fm(c_b),
            "mw": np.ascontiguousarray(mod_w_l[:, 3 * D:6 * D]), "mb": fm(mod_b_l[3 * D:6 * D]),
            "win": np.ascontiguousarray(I["odd_w_in"][0]),
            "ng": np.ascontiguousarray(np.concatenate([fm(I["q_norm_g"][0]), fm(I["kv_norm_g"][0])], axis=1)),
            "wq": np.ascontiguousarray(I["w_q_up"][0]), "wkv": np.ascontiguousarray(I["w_kv_up"][0]),
            "pos": np.ascontiguousarray(pos.reshape(1, -1).astype(np.int32)), "fr": rope_freqs()}


def build_attn_prog(S, NH=2):
    P = Prog()
    op = P.op
    q_d = P.dram_in("qT", [NH, 192, S], BF16)
    kn_d = P.dram_in("kTn", [NH, 128, S], BF16)
    kr_d = P.dram_in("kTr", [64, S], BF16)
    v_d = P.dram_in("v", [NH, S, 128], BF16)
    mask_d = P.dram_in("maskT", [128, 128], F32)
    kpad_d = P.dram_in("kpad", [32, S], BF16)
    o_d = P.dram_out("o", [NH, S, 128], BF16)
    banks = [P.ps("bank%d" % i, [128, 512]) for i in range(8)]
    NKB = S // 128
    NQT = S // 512
    kA = P.sb("kA", [128, S], BF16)
    kB = P.sb("kB", [96, S], BF16)
    vS = P.sb("vS", [128, NKB, 130], BF16)
    maskf = P.sb("maskf", [128, 128], F32)
    maskb = P.sb("maskb", [128, 128], BF16)
    ones = P.sb("ones", [128, 128], BF16)
    P.dma("sp", maskf[:], mask_d, writes=["maskf"])
    op("dve", lambda e: e.tensor_copy(out=maskb[:], in_=maskf[:]), reads=["maskf"], writes=["maskb"])
    op("pool", lambda e: e.memset(ones[:], 1.0), writes=["ones"])
    P.dma("sp", kB[0:64, :], kr_d, writes=["kB"])
    P.dma("sp", kB[64:96, :], kpad_d, writes=["kBones"])
    op("pool", lambda e: e.memset(vS[:, :, 128:130], 1.0), writes=["vSones"])
    qA = [P.sb("qA%d" % i, [128, 512], BF16) for i in range(2)]
    qB = [P.sb("qB%d" % i, [96, 512], BF16) for i in range(2)]
    sqA = P.sb("sqA", [128, 512], BF16)
    sqB = P.sb("sqB", [64, 512], BF16)
    nrm = P.sb("nrm", [96, 512], F32)
    kmx = P.sb("kmx", [128, 1], F32)
    kmt = P.sb("kmt", [128, 1], F32)
    negK = P.sb("negK", [128, 1], F32)
    pT = [P.sb("pT%d" % i, [128, 512], BF16) for i in range(3)]
    ost = [P.sb("ost%d" % i, [128, 4, 128], BF16) for i in range(2)]
    rl = [P.sb("rl%d" % i, [128, 4], F32) for i in range(2)]
    step = [0]
    qcnt = [0]

    def do_head(hh):
        for half in range(2):
            sl = slice(half * (S // 2), (half + 1) * (S // 2))
            P.dma("sp", kA[:, sl], kn_d[hh, :, sl], writes=[("kA", half)], semkey=("kA", half))
        vv = v_d[hh].rearrange("(blk p) d -> p blk d", p=128)
        for qtr in range(4):
            bs = slice(qtr * (NKB // 4), (qtr + 1) * (NKB // 4))
            P.dma("sp", vS[:, bs, 0:128], vv[:, bs, :], writes=[("vS", qtr)], semkey=("vS", qtr))
        KA = [("kA", 0), ("kA", 1)]
        VS = [("vS", i) for i in range(4)] + ["vSones"]
        op("pool", lambda e: e.memset(kmx[:], 0.0), writes=["kmx"])

        def kmax_tile(kt):
            sl = slice(kt * 512, (kt + 1) * 512)
            op("act", lambda e: e.activation(out=sqA[:], in_=kA[:, sl], func=AF.Square), reads=KA, writes=["sqA"])
            op("act", lambda e: e.activation(out=sqB[:], in_=kB[0:64, sl], func=AF.Square), reads=["kB"], writes=["sqB"])

            def mm(e):
                e.matmul(banks[7][:], lhsT=ones[:, :], rhs=sqA[:], start=True, stop=False)
                return e.matmul(banks[7][:], lhsT=ones[0:64, :], rhs=sqB[:], start=False, stop=True)
            op("pe", mm, reads=["ones", "sqA", "sqB"], writes=[("bank", 7)])
            op("dve", lambda e: e.reduce_max(out=kmt[:], in_=banks[7][:], axis=AX.X), reads=[("bank", 7)], writes=["kmt"])
            op("dve", lambda e: e.tensor_max(out=kmx[:], in0=kmx[:], in1=kmt[:]), reads=["kmx", "kmt"], writes=["kmx"])
        for kt in range(S // 512):
            kmax_tile(kt)
        op("act", lambda e: e.activation(out=negK[:], in_=kmx[:], func=AF.Sqrt), reads=["kmx"], writes=["negK"])
        op("dve", lambda e: e.tensor_scalar(out=negK[:], in0=negK[:], scalar1=-1.0, scalar2=1.0, op0=ALU.mult, op1=ALU.mult),
           reads=["negK"], writes=["negK"])

        def load_q(qt):
            qb = qcnt[0] % 2
            qsl = slice(qt * 512, (qt + 1) * 512)
            P.dma("sp", qA[qb][:], q_d[hh, 0:128, qsl], writes=[("qA", qb)], semkey=("qA", qb))
            P.dma("sp", qB[qb][0:64, :], q_d[hh, 128:192, qsl], writes=[("qB", qb)], semkey=("qB", qb))
            op("act", lambda e: e.activation(out=sqA[:], in_=qA[qb][:], func=AF.Square), reads=[("qA", qb)], writes=["sqA"])
            op("act", lambda e: e.activation(out=sqB[:], in_=qB[qb][0:64, :], func=AF.Square), reads=[("qB", qb)],
               writes=["sqB"])

            def mm(e):
                e.matmul(banks[7][:], lhsT=ones[:, :], rhs=sqA[:], start=True, stop=False)
                return e.matmul(banks[7][:], lhsT=ones[0:64, :], rhs=sqB[:], start=False, stop=True)
            op("pe", mm, reads=["ones", "sqA", "sqB"], writes=[("bank", 7)])
            op("act", lambda e: e.activation(out=nrm[64:96, :], in_=banks[7][64:96, :], func=AF.Sqrt),
               reads=[("bank", 7)], writes=["nrm"])
            op("dve", lambda e: e.tensor_scalar(out=qB[qb][64:96, :], in0=nrm[64:96, :], scalar1=negK[64:96, 0:1], scalar2=1.0,
                                                op0=ALU.mult, op1=ALU.mult),
               reads=["nrm", "negK"], writes=[("qBm", qb)])

        def do_qtile(qt):
            qb = qcnt[0] % 2
            qcnt[0] += 1
            ob = (3, 4) if qb == 0 else (5, 6)
            last = 4 * qt + 3
            QK = [("qA", qb), ("qB", qb), ("qBm", qb)]

            def S_(kb):
                i3 = (step[0] + kb) % 3
                nq0 = 0 if kb < 4 * qt else kb - 4 * qt
                cs = slice(nq0 * 128, 512)
                ks = slice(kb * 128, (kb + 1) * 128)

                def mm(e):
                    e.matmul(banks[i3][:, cs], lhsT=kA[:, ks], rhs=qA[qb][:, cs], start=True, stop=False)
                    return e.matmul(banks[i3][:, cs], lhsT=kB[0:96, ks], rhs=qB[qb][0:96, cs], start=False, stop=True)
                op("pe", mm, reads=KA + ["kB", "kBones"] + QK, writes=[("bank", i3)])
                op("act", lambda e: e.activation(out=pT[i3][:, cs], in_=banks[i3][:, cs], func=AF.Exp),
                   reads=[("bank", i3)], writes=[("pT", i3)])
                if kb >= 4 * qt:
                    ds = slice(nq0 * 128, (nq0 + 1) * 128)
                    op("pool", lambda e: e.tensor_tensor(out=pT[i3][:, ds], in0=pT[i3][:, ds], in1=maskb[:], op=ALU.mult),
                       reads=[("pT", i3), "maskb"], writes=[("pT", i3)])

            def PV(kb):
                i3 = (step[0] + kb) % 3
                nq0 = 0 if kb < 4 * qt else kb - 4 * qt

                def mm(e):
                    ins = None
                    for sb in range(nq0, 4):
                        ins = e.matmul(banks[ob[sb // 2]][:, (sb % 2) * 130:(sb % 2) * 130 + 129],
                                       lhsT=pT[i3][:, sb * 128:(sb + 1) * 128], rhs=vS[:, kb, 0:129],
                                       start=(kb == 0 and sb % 2 == 0), stop=(kb == 4 * qt + sb))
                    return ins
                op("pe", mm, reads=[("pT", i3)] + VS, writes=[("obank", qb)])
            S_(0)
            if last >= 1:
                S_(1)
            for kb in range(last + 1):
                if kb + 2 <= last:
                    S_(kb + 2)
                PV(kb)
            step[0] += last + 1
            for sb in range(4):
                col = (sb % 2) * 130
                bk = banks[ob[sb // 2]]
                op("dve", lambda e, sb=sb, col=col, bk=bk: e.reciprocal(out=rl[qb][:, sb:sb + 1], in_=bk[:, col + 128:col + 129]),
                   reads=[("obank", qb)], writes=[("rl", qb, sb)])
                op("dve", lambda e, sb=sb, col=col, bk=bk: e.tensor_scalar(
                    out=ost[qb][:, sb, :], in0=bk[:, col:col + 128], scalar1=rl[qb][:, sb:sb + 1], scalar2=1.0,
                    op0=ALU.mult, op1=ALU.mult),
                   reads=[("obank", qb), ("rl", qb, sb)], writes=[("ost", qb, sb)])
            P.dma("sp", o_d[hh, qt * 512:(qt + 1) * 512, :].rearrange("(sb p) d -> p sb d", p=128), ost[qb][:],
                  reads=[("ost", qb, sb) for sb in range(4)], semkey=("ost", qb))

        load_q(0)
        for qt in range(NQT):
            if qt + 1 < NQT:
                qcnt[0] += 1
                load_q(qt + 1)
                qcnt[0] -= 1
            do_qtile(qt)

    for hh in range(NH):
        do_head(hh)
    return P.build()


def attn_kpad(S):
    import ml_dtypes
    a = np.zeros((32, S), np.float32)
    a[0, :] = 1.0
    return a.astype(ml_dtypes.bfloat16)


def attn_mask():
    j = np.arange(128)[:, None]
    i = np.arange(128)[None, :]
    return (j <= i).astype(np.float32)


B_, S_ = 2, 16384
TPC = 4096

_PROGS = {}


def _prog(key, fn):
    if key not in _PROGS:
        _PROGS[key] = fn
    return _PROGS[key]()


def _run(nc, maps):
    res = run_bass_kernel_spmd(nc, maps, core_ids=list(range(NCORES)))
    return res.results


def _tok(i):
    return i // 4, (i % 4) * TPC


def _ffn_launch(xT_sh, I, l, f, sub):
    nc = build_ffn_prog(TPC)
    maps = []
    for i in range(NCORES):
        b, _ = _tok(i)
        maps.append(ffn_inputs(xT_sh[i], I["c"][b], I["mod_w"][l], I["mod_b"][l], sub, I["ffn_in"][l, f], I["ffn_out"][l, f],
                               I["ln_g"][l, sub], I["ln_b"][l, sub]))
    r = _run(nc, maps)
    return [r[i]["yT"] for i in range(NCORES)]


def _mixout_launch(xT_sh, yT_sh, I, l, w_out, KC):
    nc = build_mixout_prog(TPC, KC)
    maps = []
    for i in range(NCORES):
        b, _ = _tok(i)
        maps.append(mixout_inputs(xT_sh[i], yT_sh[i], I["c"][b], I["mod_w"][l], I["mod_b"][l], w_out, I["ln_g"][l, 1],
                                  I["ln_b"][l, 1]))
    r = _run(nc, maps)
    return [r[i]["oT"] for i in range(NCORES)]


def kernel(**inputs):
    I = {k: np.asarray(v) for k, v in inputs.items()}
    x = I["x"]
    xT = [np.ascontiguousarray(x[b, t0:t0 + TPC].T) for b, t0 in (_tok(i) for i in range(NCORES))]
    x1 = _ffn_launch(xT, I, 0, 0, 0)
    x1_full = [np.concatenate(x1[b * 4:(b + 1) * 4], axis=1) for b in range(B_)]
    nc = build_even_prog(S_)
    maps = [even_inputs(x1_full[i // 4], I["c"][i // 4], I["mod_w"][0], I["mod_b"][0], i % 4, I) for i in range(NCORES)]
    r = _run(nc, maps)
    yT_sh = []
    for i in range(NCORES):
        b, t0 = _tok(i)
        ya = np.concatenate([r[b * 4 + q]["y"][0:256, t0:t0 + TPC] for q in range(4)], axis=0)
        yb = np.concatenate([r[b * 4 + q]["y"][256:512, t0:t0 + TPC] for q in range(4)], axis=0)
        yT_sh.append(np.ascontiguousarray(np.concatenate([ya, yb], axis=0)))
    x2 = _mixout_launch(x1, yT_sh, I, 0, I["even_w_out"][0], 16)
    x3 = _ffn_launch(x2, I, 0, 1, 2)
    x4 = _ffn_launch(x3, I, 1, 0, 0)
    nc = build_mlaproj_prog(TPC)
    maps = []
    for i in range(NCORES):
        b, t0 = _tok(i)
        maps.append(mlaproj_inputs(x4[i], I["c"][b], I["mod_w"][1], I["mod_b"][1], I["positions"][b, t0:t0 + TPC], I))
    r = _run(nc, maps)
    nc = build_attn_prog(S_)
    maps = []
    mask = attn_mask()
    kpad = attn_kpad(S_)
    for i in range(NCORES):
        b, hp = i // 4, i % 4
        sh = range(b * 4, (b + 1) * 4)
        qT = np.concatenate([r[j]["qT"] for j in sh], axis=1)
        kTn = np.concatenate([r[j]["kTn"] for j in sh], axis=1)
        kTr = np.concatenate([r[j]["kTr"] for j in sh], axis=1)
        v = np.concatenate([r[j]["v"] for j in sh], axis=0)
        hs = [2 * hp, 2 * hp + 1]
        maps.append({
            "qT": np.ascontiguousarray(np.stack([qT[h * 192:(h + 1) * 192] for h in hs])),
            "kTn": np.ascontiguousarray(np.stack([kTn[h * 128:(h + 1) * 128] for h in hs])),
            "kTr": np.ascontiguousarray(kTr),
            "v": np.ascontiguousarray(np.stack([v[:, h * 128:(h + 1) * 128] for h in hs])),
            "maskT": mask, "kpad": kpad})
    ro = _run(nc, maps)
    oT_sh = []
    for i in range(NCORES):
        b, t0 = _tok(i)
        o = np.concatenate([ro[b * 4 + hp]["o"][hh, t0:t0 + TPC, :] for hp in range(4) for hh in range(2)], axis=1)
        oT_sh.append(np.ascontiguousarray(o.T))
    x5 = _mixout_launch(x4, oT_sh, I, 1, I["odd_w_out"][0], 8)
    x6 = _ffn_launch(x5, I, 1, 1, 2)
    out = np.empty((B_, S_, D), np.float32)
    for i in range(NCORES):
        b, t0 = _tok(i)
        out[b, t0:t0 + TPC] = x6[i].T
    return out
```

```python
import contextlib
import math
import numpy as np
import concourse.bass as bass
import concourse.mybir as mybir
from concourse.bass_utils import run_bass_kernel_spmd

F32 = mybir.dt.float32
BF16 = mybir.dt.bfloat16
I32 = mybir.dt.int32
AF = mybir.ActivationFunctionType
ALU = mybir.AluOpType
AX = mybir.AxisListType

D = 1024
DFF = 2816
NJ = DFF // 128
DEPTH = 2
DN_ALPHA = (2.0 * DEPTH) ** 0.25
LN_EPS = 1e-5
NCORES = 8

ENGS = ("pe", "act", "dve", "pool", "sp")


class Op:
    __slots__ = ("eng", "fn", "reads", "writes", "is_dma", "semkey", "deps", "signal",
                 "idx", "sigval", "sem")

    def __init__(self, eng, fn, reads, writes, is_dma=False, semkey=None):
        self.eng = eng
        self.fn = fn
        self.reads = tuple(reads)
        self.writes = tuple(writes)
        self.is_dma = is_dma
        self.semkey = semkey
        self.deps = []
        self.signal = False


ARENA_F = 23600
ARENA_B = 57024
ARENA_I = 1024


class Prog:
    def __init__(self, arena=False):
        self.nc = bass.Bass("TRN2", target_bir_lowering=False)
        self.ops = []
        self.stack = contextlib.ExitStack()
        self.last_writer = {}
        self.readers = {}
        self.last_dma_on_key = {}
        self.last_on_eng = {}
        self.arena = None
        self.nbar = 0
        if arena:
            self.arena = {
                F32: self.stack.enter_context(self.nc.sbuf_tensor("arena_f", [128, ARENA_F], F32)),
                BF16: self.stack.enter_context(self.nc.sbuf_tensor("arena_b", [128, ARENA_B], BF16)),
                I32: self.stack.enter_context(self.nc.sbuf_tensor("arena_i", [128, ARENA_I], I32)),
            }
            self.off = {F32: 0, BF16: 0, I32: 0}
            self.cap = {F32: ARENA_F, BF16: ARENA_B, I32: ARENA_I}
            self.bscr = self.sb("bscr", [128, 16], F32)
            self.bscr_b = self.sb("bscr_b", [128, 16], BF16)

    def mark(self):
        return dict(self.off)

    def release(self, m):
        self.off = dict(m)

    def sb(self, name, shape, dt):
        if self.arena is None:
            return self.stack.enter_context(self.nc.sbuf_tensor("s_" + name, list(shape), dt))
        n = 1
        for d_ in shape[1:]:
            n *= d_
        n = (n + 15) // 16 * 16
        off = self.off[dt]
        assert off + n <= self.cap[dt], "arena overflow %s %s need %d have %d" % (name, dt, n, self.cap[dt] - off)
        self.off[dt] = off + n
        ap = self.arena[dt][0:shape[0], off:off + n]
        m = 1
        for d_ in shape[1:]:
            m *= d_
        ap = ap[:, 0:m]
        if len(shape) == 3:
            ap = ap.rearrange("p (a b) -> p a b", a=shape[1])
        elif len(shape) == 4:
            ap = ap.rearrange("p (a b c) -> p a b c", a=shape[1], b=shape[2])
        return ap

    def barrier(self):
        self.nbar += 1
        lasts = list(self.last_on_eng.values()) + list(self.last_dma_on_key.values())
        bops = []
        bs, bb, nc_ = self.bscr, self.bscr_b, self.nc
        fns = {
            "pe": lambda eng: eng.matmul(self._bar_bank[0:16, 0:16], lhsT=bb[0:16, 0:16], rhs=bb[0:16, 0:16], start=True, stop=True),
            "act": lambda eng: eng.activation(out=bs[0:1, 0:1], in_=bs[0:1, 1:2], func=AF.Identity),
            "dve": lambda eng: eng.tensor_copy(out=bs[0:1, 2:3], in_=bs[0:1, 3:4]),
            "pool": lambda eng: eng.tensor_copy(out=bs[0:1, 4:5], in_=bs[0:1, 5:6]),
            "sp": None,
        }
        for e in ENGS:
            if e == "sp":
                o = Op(e, lambda eng: eng.dma_start(out=bs[0:1, 8:9], in_=bs[0:1, 9:10]), (), (), is_dma=True, semkey="barrier_dma")
                prev = self.last_dma_on_key.get("barrier_dma")
            else:
                o = Op(e, fns[e], (), ())
                prev = None
            o.idx = len(self.ops)
            o.deps = [d for d in lasts if d.is_dma or d.eng != e]
            for d in o.deps:
                d.signal = True
            self.ops.append(o)
            bops.append(o)
        for o in bops:
            if o.is_dma:
                self.last_dma_on_key[o.semkey] = o
            else:
                self.last_on_eng[o.eng] = o
        self.last_writer = {}
        self.readers = {}

    def collective(self, kind, groups, in_ap, out_ap, reads, writes, semkey):
        def fn(e):
            return e.collective_compute(kind, ALU.bypass, replica_groups=groups, ins=[in_ap], outs=[out_ap])
        return self._add(Op("pool", fn, reads, writes, is_dma=True, semkey=semkey))

    def ps(self, name, shape, dt=F32):
        return self.stack.enter_context(self.nc.psum_tensor("p_" + name, list(shape), dt))

    def dram_in(self, name, shape, dt):
        return self.nc.dram_tensor(name, list(shape), dt, kind="ExternalInput").ap()

    def dram_out(self, name, shape, dt):
        return self.nc.dram_tensor(name, list(shape), dt, kind="ExternalOutput").ap()

    def dram_tmp(self, name, shape, dt):
        return self.nc.dram_tensor(name, list(shape), dt, kind="Internal").ap()

    def _add(self, op):
        deps = set()
        strong = set()
        for b in op.reads:
            w = self.last_writer.get(b)
            if w is not None:
                deps.add(w)
                strong.add(w)
        for b in op.writes:
            w = self.last_writer.get(b)
            if w is not None:
                deps.add(w)
                strong.add(w)
            for r in self.readers.get(b, ()):
                deps.add(r)
        if op.is_dma:
            prev = self.last_dma_on_key.get(op.semkey)
            if prev is not None:
                deps.add(prev)
            self.last_dma_on_key[op.semkey] = op
        deps.discard(op)
        op.idx = len(self.ops)
        op.deps = [d for d in deps if d.is_dma or op.is_dma or d.eng != op.eng
                   or (d in strong and op.eng != "pe")]
        for d in op.deps:
            d.signal = True
        for b in op.writes:
            self.last_writer[b] = op
            self.readers[b] = []
        for b in op.reads:
            self.readers.setdefault(b, []).append(op)
        self.ops.append(op)
        if not op.is_dma:
            self.last_on_eng[op.eng] = op
        return op

    def op(self, eng, fn, reads=(), writes=()):
        return self._add(Op(eng, fn, reads, writes))

    def dma(self, queue, out, in_, reads=(), writes=(), semkey=None):
        if semkey is None:
            semkey = ("dma", (writes[0] if len(writes) else reads[0]))

        def fn(e, out=out, in_=in_):
            return e.dma_start(out=out, in_=in_)
        return self._add(Op(queue, fn, reads, writes, is_dma=True, semkey=semkey))

    def build(self):
        nc = self.nc
        st = self.stack
        ops = self.ops
        eng_sem = {e: st.enter_context(nc.semaphore("s_" + e)) for e in ENGS}
        dma_sem = {}
        for o in ops:
            if o.is_dma and o.semkey not in dma_sem:
                dma_sem[o.semkey] = st.enter_context(nc.semaphore("d%d" % len(dma_sem)))
        eng_cnt = {e: 0 for e in ENGS}
        dma_cnt = {k: 0 for k in dma_sem}
        for o in ops:
            if o.is_dma:
                dma_cnt[o.semkey] += 16
                o.sigval = dma_cnt[o.semkey]
                o.sem = dma_sem[o.semkey]
            elif o.signal:
                eng_cnt[o.eng] += 1
                o.sigval = eng_cnt[o.eng]
                o.sem = eng_sem[o.eng]
        final_waits = [(dma_sem[k], v) for k, v in dma_cnt.items()]
        block = st.enter_context(nc.Block())
        per_eng = {e: [o for o in ops if o.eng == e] for e in ENGS}

        def emit(eng_obj, ename):
            seen = {}
            for o in per_eng[ename]:
                need = {}
                for d in o.deps:
                    key = id(d.sem)
                    if key not in need or need[key][1] < d.sigval:
                        need[key] = (d.sem, d.sigval)
                for key, (sem, val) in need.items():
                    if seen.get(key, 0) >= val:
                        continue
                    eng_obj.wait_ge(sem, val)
                    seen[key] = val
                ins = o.fn(eng_obj)
                if o.is_dma:
                    ins.then_inc(o.sem, 16)
                elif o.signal:
                    ins.then_inc(o.sem, 1)
            if ename == "sp":
                for sem, val in final_waits:
                    if val > 0:
                        eng_obj.wait_ge(sem, val)

        @block.tensor
        def _(e):
            emit(e, "pe")

        @block.scalar
        def _(e):
            emit(e, "act")

        @block.vector
        def _(e):
            emit(e, "dve")

        @block.gpsimd
        def _(e):
            emit(e, "pool")

        @block.sync
        def _(e):
            emit(e, "sp")

        st.close()
        return nc


class Ctx:
    def __init__(self, P, banks=None):
        self.P = P
        self.banks = banks if banks is not None else [P.ps("bank%d" % i, [128, 512]) for i in range(8)]
        self.ones_bf = P.sb("ones_bf", [128, 128], BF16)
        self.epsc = P.sb("epsc", [128, 1], F32)
        P.op("pool", lambda e: e.memset(self.ones_bf[:], 1.0 / D), writes=["ones_bf"])
        P.op("pool", lambda e: e.memset(self.epsc[:], LN_EPS / (DN_ALPHA * DN_ALPHA)), writes=["epsc"])


def emit_mod(P, C, tag, cT_d, mw_d, mb_d, wgt, stage_buf, bank_idx=7):
    cT = P.sb(tag + "cT", [128, 8], F32)
    cact = P.sb(tag + "cact", [128, 8], F32)
    mb = P.sb(tag + "mb", [128, 24], F32)
    modv = P.sb(tag + "modv", [128, 24], F32)
    sc1 = P.sb(tag + "sc1", [128, 8], F32)
    gs = P.sb(tag + "gs", [128, 8], F32)
    k = tag + "mod"
    P.dma("sp", cT[:], cT_d, writes=[k + "cT"])
    P.dma("sp", mb[:], mb_d, writes=[k + "mb"])
    P.op("act", lambda e: e.activation(out=cact[:], in_=cT[:], func=AF.Silu), reads=[k + "cT"], writes=[k + "cact"])
    pm = C.banks[bank_idx]
    mwv = mw_d.rearrange("(kc p) n -> p kc n", p=128)
    for piece in range(12):
        sbuf, skey = stage_buf[piece % 2]
        P.dma("sp", sbuf[:], mwv[:, :, piece * 256:(piece + 1) * 256], writes=[skey])
        for jj in range(2):
            jc = piece * 2 + jj

            def mm(e, sbuf=sbuf, jj=jj, jc=jc):
                ins = None
                for kc in range(8):
                    ins = e.matmul(pm[:, jc:jc + 1], lhsT=sbuf[:, kc, jj * 128:(jj + 1) * 128],
                                   rhs=cact[:, kc:kc + 1], start=(kc == 0), stop=(kc == 7))
                return ins
            P.op("pe", mm, reads=[skey, k + "cact"], writes=[("bank", bank_idx)])
    P.op("dve", lambda e: e.tensor_tensor(out=modv[:], in0=pm[:, 0:24], in1=mb[:], op=ALU.add),
         reads=[("bank", bank_idx), k + "mb"], writes=[k + "modv"])
    P.op("dve", lambda e: e.tensor_scalar(out=sc1[:], in0=modv[:, 8:16], scalar1=1.0, scalar2=1.0, op0=ALU.add, op1=ALU.mult),
         reads=[k + "modv"], writes=[k + "vec"])
    P.op("dve", lambda e: e.tensor_scalar(out=gs[:], in0=modv[:, 16:24], scalar1=1.0, scalar2=wgt / DN_ALPHA,
                                          op0=ALU.add, op1=ALU.mult),
         reads=[k + "modv"], writes=[k + "vec"])
    return modv, sc1, gs, k + "vec"


def emit_ln(P, C, tag, s, ysrc, ykeys, gT, bT, gbkey, dst, dkeys, tmp):
    ybf, ysq = tmp["ybf"], tmp["ysq"]
    kb = tag + "ln"
    for dc in range(8):
        P.op("pool", lambda e, dc=dc: e.tensor_copy(out=ybf[:, dc, :], in_=ysrc(dc)),
             reads=[ykeys(dc)], writes=[(kb, "ybf", dc)])
        P.op("act", lambda e, dc=dc: e.activation(out=ysq[:, dc, :], in_=ysrc(dc), func=AF.Square),
             reads=[ykeys(dc)], writes=[(kb, "ysq", dc)])
    b1, b2 = C.banks[6], C.banks[7]

    def mm1(e):
        ins = None
        for dc in range(8):
            ins = e.matmul(b1[:], lhsT=C.ones_bf[:], rhs=ybf[:, dc, :], start=(dc == 0), stop=(dc == 7))
        return ins

    def mm2(e):
        ins = None
        for dc in range(8):
            ins = e.matmul(b2[:], lhsT=C.ones_bf[:], rhs=ysq[:, dc, :], start=(dc == 0), stop=(dc == 7))
        return ins
    P.op("pe", mm1, reads=[(kb, "ybf", dc) for dc in range(8)] + ["ones_bf"], writes=[("bank", 6)])
    P.op("pe", mm2, reads=[(kb, "ysq", dc) for dc in range(8)] + ["ones_bf"], writes=[("bank", 7)])
    mean, var, rstd, nmr = tmp["mean"], tmp["var"], tmp["rstd"], tmp["nmr"]
    P.op("dve", lambda e: e.tensor_copy(out=mean[:], in_=b1[:]), reads=[("bank", 6)], writes=[(kb, "mean")])
    P.op("dve", lambda e: e.tensor_tensor(out=var[:], in0=mean[:], in1=mean[:], op=ALU.mult),
         reads=[(kb, "mean")], writes=[(kb, "var")])
    P.op("dve", lambda e: e.tensor_tensor(out=var[:], in0=b2[:], in1=var[:], op=ALU.subtract),
         reads=[("bank", 7), (kb, "var")], writes=[(kb, "var")])
    P.op("act", lambda e: e.activation(out=rstd[:], in_=var[:], func=AF.Sqrt, bias=C.epsc[:, 0:1], scale=1.0),
         reads=[(kb, "var"), "epsc"], writes=[(kb, "rstd")])
    P.op("dve", lambda e: e.reciprocal(out=rstd[:], in_=rstd[:]), reads=[(kb, "rstd")], writes=[(kb, "rstd")])
    P.op("dve", lambda e: e.scalar_tensor_tensor(out=nmr[:], in0=mean[:], scalar=-1.0, in1=rstd[:],
                                                 op0=ALU.mult, op1=ALU.mult),
         reads=[(kb, "mean"), (kb, "rstd")], writes=[(kb, "nmr")])
    t1r, t2r = tmp["t1"], tmp["t2"]
    for dc in range(8):
        t1 = t1r[dc % 2]
        t2 = t2r[dc % 2]
        P.op("dve", lambda e, dc=dc, t1=t1: e.tensor_tensor(out=t1[:], in0=ysrc(dc), in1=rstd[:], op=ALU.mult),
             reads=[ykeys(dc), (kb, "rstd")], writes=[(kb, "t1", dc % 2)])
        P.op("pool", lambda e, t1=t1, t2=t2: e.tensor_tensor(out=t2[:], in0=t1[:], in1=nmr[:], op=ALU.add),
             reads=[(kb, "t1", dc % 2), (kb, "nmr")], writes=[(kb, "t2", dc % 2)])
        P.op("act", lambda e, dc=dc, t2=t2: e.activation(out=dst(dc), in_=t2[:], func=AF.Identity,
                                                          bias=bT[:, dc:dc + 1], scale=gT[:, dc:dc + 1]),
             reads=[(kb, "t2", dc % 2), gbkey], writes=[dkeys(dc)])


def ln_tmp(P, tag):
    return {
        "ybf": P.sb(tag + "ybf", [128, 8, 512], BF16),
        "ysq": P.sb(tag + "ysq", [128, 8, 512], BF16),
        "mean": P.sb(tag + "mean", [128, 512], F32),
        "var": P.sb(tag + "var", [128, 512], F32),
        "rstd": P.sb(tag + "rstd", [128, 512], F32),
        "nmr": P.sb(tag + "nmr", [128, 512], F32),
        "t1": [P.sb(tag + "t1_%d" % i, [128, 512], F32) for i in range(2)],
        "t2": [P.sb(tag + "t2_%d" % i, [128, 512], F32) for i in range(2)],
    }


class FFNRes:
    def __init__(self, P):
        self.xt = P.sb("f_xt", [128, 8, 1024], F32)
        self.ub = P.sb("f_ub", [128, 8, 1024], BF16)
        self.hT = P.sb("f_hT", [128, NJ, 1024], BF16)
        self.win = [P.sb("f_win%d" % i, [128, 8, 2, 256], BF16) for i in range(3)]
        self.wout = [P.sb("f_wout%d" % i, [128, NJ, 128], BF16) for i in range(2)]
        self.sil = [P.sb("f_sil%d" % i, [128, 512], F32) for i in range(2)]
        self.lnt = ln_tmp(P, "f_")
        self.stage = [(P.sb("f_mst%d" % i, [128, 8, 256], F32), "f_mst%d" % i) for i in range(2)]
        self.g = P.sb("f_lng", [128, 8], F32)
        self.b = P.sb("f_lnb", [128, 8], F32)


def emit_ffn(P, C, R, tag, x_d, y_d, ntok, win_d, wout_d, sh, sc1, gs, veckey, lng_d, lnb_d, dbg=None):
    nsub = ntok // 512
    assert nsub % 2 == 0
    xv = x_d.rearrange("(c p) t -> p c t", p=128)
    yv = y_d.rearrange("(c p) t -> p c t", p=128)
    wv = win_d.rearrange("(kc p) (g n) -> p kc g n", p=128, g=2)
    wov = wout_d.rearrange("(j p) n -> p j n", p=128)
    P.dma("sp", R.g[:], lng_d, writes=["f_gb"], semkey="f_g")
    P.dma("sp", R.b[:], lnb_d, writes=["f_gb"], semkey="f_b")
    xt, ub, hT = R.xt, R.ub, R.hT
    win_cnt = [0]
    wout_cnt = [0]
    pair_cnt = [0]
    obank_cnt = [0]
    sil_cnt = [0]

    def load_win(blk):
        i = win_cnt[0] % 3
        win_cnt[0] += 1
        for g in range(2):
            P.dma("pool", R.win[i][:, :, g, :], wv[:, :, g, blk * 256:(blk + 1) * 256], writes=[("f_win", i, g)])
        return i

    def load_wout(dc):
        i = wout_cnt[0] % 2
        wout_cnt[0] += 1
        P.dma("pool", R.wout[i][:], wov[:, :, dc * 128:(dc + 1) * 128], writes=[("f_wout", i)])
        return i

    NBLK = NJ // 2
    for tp in range(nsub // 2):
        subs = [0, 1]
        t0 = tp * 1024
        for s in subs:
            for half in range(2):
                P.dma("sp", xt[:, half * 4:(half + 1) * 4, s * 512:(s + 1) * 512],
                      xv[:, half * 4:(half + 1) * 4, t0 + s * 512:t0 + (s + 1) * 512],
                      writes=[("f_xt", s, dc) for dc in range(half * 4, half * 4 + 4)],
                      semkey=("f_xtld", s, half))
        pend = [load_win(0), load_win(1)]
        for s in subs:
            for dc in range(8):
                P.op("act", lambda e, s=s, dc=dc: e.activation(
                    out=ub[:, dc, s * 512:(s + 1) * 512], in_=xt[:, dc, s * 512:(s + 1) * 512],
                    func=AF.Identity, bias=sh[:, dc:dc + 1], scale=sc1[:, dc:dc + 1]),
                    reads=[("f_xt", s, dc), veckey], writes=[("f_ub", s, dc)])
        wo_pend = []
        for blk in range(NBLK):
            wi = pend.pop(0)
            if blk + 2 < NBLK:
                pend.append(load_win(blk + 2))
            if blk == NBLK - 2:
                wo_pend.append(load_wout(0))
            if blk == NBLK - 1:
                wo_pend.append(load_wout(1))
            wt = R.win[wi]
            for jj in range(2):
                j = blk * 2 + jj
                for s in subs:
                    pr = pair_cnt[0] % 2
                    pair_cnt[0] += 1
                    bg, bu = C.banks[2 * pr], C.banks[2 * pr + 1]

                    def mmg(e, wt=wt, jj=jj, s=s, bg=bg, g=0):
                        ins = None
                        for kc in range(8):
                            ins = e.matmul(bg[:], lhsT=wt[:, kc, g, jj * 128:(jj + 1) * 128],
                                           rhs=ub[:, kc, s * 512:(s + 1) * 512], start=(kc == 0), stop=(kc == 7))
                        return ins

                    def mmu(e, wt=wt, jj=jj, s=s, bu=bu, g=1):
                        ins = None
                        for kc in range(8):
                            ins = e.matmul(bu[:], lhsT=wt[:, kc, g, jj * 128:(jj + 1) * 128],
                                           rhs=ub[:, kc, s * 512:(s + 1) * 512], start=(kc == 0), stop=(kc == 7))
                        return ins
                    ubk = [("f_ub", s, dc) for dc in range(8)]
                    P.op("pe", mmg, reads=[("f_win", wi, 0)] + ubk, writes=[("bank", 2 * pr)])
                    P.op("pe", mmu, reads=[("f_win", wi, 1)] + ubk, writes=[("bank", 2 * pr + 1)])
                    si = sil_cnt[0] % 2
                    sil_cnt[0] += 1
                    sl = R.sil[si]
                    P.op("act", lambda e, sl=sl, bg=bg: e.activation(out=sl[:], in_=bg[:], func=AF.Silu),
                         reads=[("bank", 2 * pr)], writes=[("f_sil", si)])
                    P.op("dve", lambda e, sl=sl, bu=bu, j=j, s=s: e.tensor_tensor(
                        out=hT[:, j, s * 512:(s + 1) * 512], in0=sl[:], in1=bu[:], op=ALU.mult),
                        reads=[("f_sil", si), ("bank", 2 * pr + 1)], writes=[("f_hT", s, j)])
        for dc in range(8):
            wi = wo_pend.pop(0)
            wt = R.wout[wi]
            for s in subs:
                ob = 4 + obank_cnt[0] % 2
                obank_cnt[0] += 1
                bo = C.banks[ob]

                def mmo(e, wt=wt, s=s, bo=bo):
                    ins = None
                    for j in range(NJ):
                        ins = e.matmul(bo[:], lhsT=wt[:, j, :], rhs=hT[:, j, s * 512:(s + 1) * 512],
                                       start=(j == 0), stop=(j == NJ - 1))
                    return ins
                P.op("pe", mmo, reads=[("f_wout", wi)] + [("f_hT", s, j) for j in range(NJ)], writes=[("bank", ob)])
                P.op("dve", lambda e, dc=dc, s=s, bo=bo: e.scalar_tensor_tensor(
                    out=xt[:, dc, s * 512:(s + 1) * 512], in0=bo[:], scalar=gs[:, dc:dc + 1],
                    in1=xt[:, dc, s * 512:(s + 1) * 512], op0=ALU.mult, op1=ALU.add),
                    reads=[("bank", ob), veckey, ("f_xt", s, dc)], writes=[("f_xt", s, dc)])
            if dc + 2 < 8:
                wo_pend.append(load_wout(dc + 2))
        if dbg is not None and tp == 0:
            P.dma("sp", dbg["ub"], ub[:, :, 0:512], reads=[("f_ub", 0, dc) for dc in range(8)], semkey="dbg1")
            P.dma("sp", dbg["hT"], hT[:, :, 0:512], reads=[("f_hT", 0, j) for j in range(NJ)], semkey="dbg2")
            P.dma("sp", dbg["y"], xt[:, :, 0:512], reads=[("f_xt", 0, dc) for dc in range(8)], semkey="dbg3")
        for s in subs:
            emit_ln(P, C, "f_", s,
                    lambda dc, s=s: xt[:, dc, s * 512:(s + 1) * 512], lambda dc, s=s: ("f_xt", s, dc),
                    R.g, R.b, "f_gb",
                    lambda dc, s=s: xt[:, dc, s * 512:(s + 1) * 512], lambda dc, s=s: ("f_xt", s, dc),
                    R.lnt)
            for half in range(2):
                P.dma("sp", yv[:, half * 4:(half + 1) * 4, t0 + s * 512:t0 + (s + 1) * 512],
                      xt[:, half * 4:(half + 1) * 4, s * 512:(s + 1) * 512],
                      reads=[("f_xt", s, dc) for dc in range(half * 4, half * 4 + 4)],
                      semkey=("f_xtst", s, half))


def emit_ffn_stage(P, banks, T, ntok, wgt=0.5):
    C = Ctx(P, banks)
    R = FFNRes(P)
    modv, sc1, gs, vk = emit_mod(P, C, "m_", T["cT"], T["mw"], T["mb"], wgt, R.stage)
    emit_ffn(P, C, R, "f", T["xT"], T["yT"], ntok, T["win"], T["wout"], modv[:, 0:8], sc1, gs, vk, T["lng"], T["lnb"])


FFN_IN = [("cT", [128, 8], F32), ("mw", [D, 3 * D], F32), ("mb", [128, 24], F32), ("win", [D, 2 * DFF], F32),
          ("wout", [DFF, D], F32), ("lng", [128, 8], F32), ("lnb", [128, 8], F32)]


def build_ffn_prog(ntok, wgt=0.5):
    P = Prog()
    T = {n: P.dram_in(n, sh, dt) for n, sh, dt in FFN_IN}
    T["xT"] = P.dram_in("xT", [D, ntok], F32)
    T["yT"] = P.dram_out("yT", [D, ntok], F32)
    emit_ffn_stage(P, None, T, ntok, wgt)
    return P.build()


def fm(v):
    v = np.asarray(v)
    return np.ascontiguousarray(v.reshape(-1, 128).T)


def ffn_inputs(xT, c_b, mod_w_l, mod_b_l, sub, ffn_in_lf, ffn_out_lf, ln_g, ln_b):
    return {
        "xT": None if xT is None else np.ascontiguousarray(xT),
        "cT": fm(c_b),
        "mw": np.ascontiguousarray(mod_w_l[:, sub * 3 * D:(sub + 1) * 3 * D]),
        "mb": fm(mod_b_l[sub * 3 * D:(sub + 1) * 3 * D]),
        "win": np.ascontiguousarray(ffn_in_lf),
        "wout": np.ascontiguousarray(ffn_out_lf),
        "lng": fm(ln_g),
        "lnb": fm(ln_b),
    }


EV_COLS = 1296


EVEN_IN = [("cT", [128, 8], F32), ("mw", [D, 3 * D], F32), ("mb", [128, 24], F32), ("win", [D, EV_COLS], F32),
           ("convw", [128, 2, 4], F32), ("rgvec", [128, 5, 2], F32), ("rgw", [2, 4, 64, 64], F32), ("wa2", [16, 128], F32),
           ("gba", [128, 1], F32), ("ident", [128, 128], F32), ("maskT", [128, 128], F32), ("cmask", [128, 512], F32)]


def build_even_prog(S):
    P = Prog()
    T = {n: P.dram_in(n, sh, dt) for n, sh, dt in EVEN_IN}
    x_d = P.dram_in("xT", [D, S], F32)
    y_d = P.dram_out("y", [512, S], BF16)
    xv = x_d.rearrange("(c p) t -> p c t", p=128)
    yv = y_d.rearrange("(c p) t -> p c t", p=128)
    banks = [P.ps("bank%d" % i, [128, 512]) for i in range(8)]
    emit_even_stage(P, banks, T, S,
                    lambda t, half: xv[:, half * 4:(half + 1) * 4, t * 512:(t + 1) * 512],
                    lambda t: yv[:, :, t * 512:(t + 1) * 512])
    return P.build()


def emit_even_stage(P, banks, T, S, xtile, ytile):
    op = P.op
    cT_d, mw_d, mb_d, win_d, cw_d, vec_d = T["cT"], T["mw"], T["mb"], T["win"], T["convw"], T["rgvec"]
    rgw_d, wa2_d, gba_d, ident_d, mask_d, cm_d = T["rgw"], T["wa2"], T["gba"], T["ident"], T["maskT"], T["cmask"]
    tps = banks[7][:].bitcast(BF16)

    class C_:
        pass
    C = C_()
    C.banks = banks
    stage = [(P.sb("mst%d" % i, [128, 8, 256], F32), "mst%d" % i) for i in range(2)]
    modv, sc1, gs_unused, vk = emit_mod(P, C, "m_", cT_d, mw_d, mb_d, 1.0, stage, bank_idx=2)
    sh = modv[:, 0:8]

    wi = P.sb("wi", [128, 8, EV_COLS], BF16)
    wv = win_d.rearrange("(kc p) n -> p kc n", p=128)
    for h in range(2):
        P.dma("pool", wi[:, h * 4:(h + 1) * 4, :], wv[:, h * 4:(h + 1) * 4, :], writes=[("wi", h)])
    WI = [("wi", 0), ("wi", 1)]
    cw = P.sb("cw", [128, 2, 4], F32)
    vec = P.sb("vec", [128, 5, 2], F32)
    P.dma("sp", cw[:], cw_d, writes=["cw"])
    P.dma("sp", vec[:], vec_d, writes=["vec"])
    rgf = P.sb("rgf", [128, 2, 2, 128], F32)
    rgb = P.sb("rgb", [128, 2, 2, 128], BF16)
    op("pool", lambda e: e.memset(rgf[:], 0.0), writes=["rgf"])
    for ax in range(2):
        for blk in range(4):
            c, hf = blk // 2, blk % 2
            P.dma("sp", rgf[hf * 64:(hf + 1) * 64, ax, c, hf * 64:(hf + 1) * 64], rgw_d[ax, blk],
                  writes=["rgf"], semkey=("rgf", ax, blk))
    op("dve", lambda e: e.tensor_copy(out=rgb[:], in_=rgf[:]), reads=["rgf"], writes=["rgb"])
    wa2f = P.sb("wa2f", [16, 128], F32)
    wa2b = P.sb("wa2b", [16, 128], BF16)
    P.dma("sp", wa2f[:], wa2_d, writes=["wa2f"])
    op("dve", lambda e: e.tensor_copy(out=wa2b[:], in_=wa2f[:]), reads=["wa2f"], writes=["wa2b"])
    nba = P.sb("nba", [128, 1], F32)
    P.dma("sp", nba[:], gba_d, writes=["nba"])
    op("dve", lambda e: e.tensor_scalar(out=nba[:], in0=nba[:], scalar1=-1.0, scalar2=1.0, op0=ALU.mult, op1=ALU.mult),
       reads=["nba"], writes=["nba"])
    identf = P.sb("identf", [128, 128], F32)
    ident = P.sb("ident", [128, 128], BF16)
    maskT = P.sb("maskT", [128, 128], F32)
    cmask = P.sb("cmask", [128, 512], F32)
    P.dma("sp", identf[:], ident_d, writes=["identf"])
    P.dma("sp", maskT[:], mask_d, writes=["maskT"])
    P.dma("sp", cmask[:], cm_d, writes=["cmask"])
    op("dve", lambda e: e.tensor_copy(out=ident[:], in_=identf[:]), reads=["identf"], writes=["ident"])
    ones256 = P.sb("ones256", [128, 128], BF16)
    op("pool", lambda e: e.memset(ones256[:], 1.0 / 256.0), writes=["ones256"])
    eps1 = P.sb("eps1", [128, 1], F32)
    op("pool", lambda e: e.memset(eps1[:], LN_EPS), writes=["eps1"])
    cl = P.sb("cl", [128, 2, 2], F32)
    tmpl = P.sb("tmpl", [128, 2], F32)
    op("act", lambda e: e.activation(out=tmpl[:], in_=vec[:, 3, :], func=AF.Exp, scale=-1.0), reads=["vec"], writes=["tmpl"])
    op("act", lambda e: e.activation(out=tmpl[:], in_=tmpl[:], func=AF.Ln, bias=1.0, scale=1.0), reads=["tmpl"], writes=["tmpl"])
    op("dve", lambda e: e.tensor_scalar(out=cl[:, 0, :], in0=tmpl[:], scalar1=-8.0, scalar2=1.0, op0=ALU.mult, op1=ALU.mult),
       reads=["tmpl"], writes=["cl"])
    op("dve", lambda e: e.tensor_scalar(out=cl[:, 1, :], in0=tmpl[:], scalar1=-16.0, scalar2=1.0, op0=ALU.mult, op1=ALU.mult),
       reads=["tmpl"], writes=["cl"])

    def two(name, shape, dt):
        return [P.sb("%s%d" % (name, i), shape, dt) for i in range(2)]
    xt = two("xt", [128, 8, 512], F32)
    ub = two("ub", [128, 8, 512], BF16)
    xbuf = P.sb("xbuf", [128, 2, 515], F32)
    op("pool", lambda e: e.memset(xbuf[:], 0.0), writes=[("xbuf", 0), ("xbuf", 1)])
    acc = [P.sb("acc", [128, 2, 512], F32)]
    xcb = [P.sb("xcb", [128, 2, 512], BF16)]
    rr = [P.sb("rr", [128, 2, 512], F32)]
    ii = [P.sb("ii", [128, 2, 512], F32)]
    aa = [P.sb("aa", [128, 2, 512], F32)]
    ss = [P.sb("ss", [128, 2, 512], F32)]
    uu = [P.sb("uu", [128, 2, 512], F32)]
    hh = two("hh", [128, 2, 512], F32)
    gel = [P.sb("gel", [128, 2, 512], F32)]
    yo = two("yo", [128, 4, 512], BF16)
    zab = [P.sb("zab", [16, 512], BF16)]
    sp_ = [P.sb("sp", [128, 512], F32)]
    cs = [P.sb("cs", [128, 512], F32)]
    eb = [P.sb("eb", [128, 512], F32)]
    einv = [P.sb("einv", [128, 512], F32)]
    E8 = two("E8", [128, 8], F32)
    qd = [P.sb("qd", [128, 512], BF16)]
    kinv = [P.sb("kinv", [128, 512], BF16)]
    kend = [P.sb("kend", [128, 512], BF16)]
    vtok = [P.sb("vtok", [128, 4, 256], BF16)]
    sT = [P.sb("sT", [128, 4, 128], BF16)]
    ktok = [P.sb("ktok", [128, 4, 128], BF16)]
    sg = [P.sb("sg", [128, 2, 512], F32)]
    osq = [P.sb("osq", [128, 2, 512], BF16)]
    rs = [P.sb("rs", [128, 512], F32)]
    tt = [P.sb("tt", [128, 2, 512], F32)]
    Sf = P.sb("Sf", [128, 256], F32)
    Sb = two("Sb", [128, 256], BF16)
    op("pool", lambda e: e.memset(Sf[:], 0.0), writes=["Sf"])
    op("pool", lambda e: e.memset(Sb[1][:], 0.0), writes=[("Sb", 1)])

    NT = S // 512
    rot = [0]

    def nb():
        i = rot[0] % 3
        rot[0] += 1
        return i

    def proj(col0, ncol, tb_, dst_bank, bkey, cols=slice(0, 512)):
        def mm(e):
            ins = None
            for kc in range(8):
                ins = e.matmul(banks[dst_bank][0:ncol, :], lhsT=wi[:, kc, col0:col0 + ncol], rhs=ub[tb_][:, kc, :],
                               start=(kc == 0), stop=(kc == 7))
            return ins
        op("pe", mm, reads=WI + [("ub", tb_, dc) for dc in range(8)], writes=[("bank", dst_bank)])

    def load_x(t):
        tb_ = t % 2
        for half in range(2):
            P.dma("sp", xt[tb_][:, half * 4:(half + 1) * 4, :], xtile(t, half),
                  writes=[("xt", tb_, dc) for dc in range(half * 4, half * 4 + 4)], semkey=("xtld", tb_, half))

    load_x(0)
    sbi = 1
    for t in range(NT):
        tb_ = t % 2
        if t + 1 < NT:
            load_x(t + 1)
        for dc in range(8):
            op("act", lambda e, dc=dc, tb_=tb_: e.activation(out=ub[tb_][:, dc, :], in_=xt[tb_][:, dc, :], func=AF.Identity,
                                                          bias=sh[:, dc:dc + 1], scale=sc1[:, dc:dc + 1]),
               reads=[("xt", tb_, dc), vk], writes=[("ub", tb_, dc)])
        UB = [("ub", tb_, dc) for dc in range(8)]
        b = nb()
        proj(1280, 16, tb_, b, None)
        op("act", lambda e, b=b, tb_=tb_: e.activation(out=zab[0][:], in_=banks[b][0:16, :], func=AF.Identity),
           reads=[("bank", b)], writes=[("zab", 0)])
        b = nb()
        op("pe", lambda e, b=b, tb_=tb_: e.matmul(banks[b][:], lhsT=wa2b[:], rhs=zab[0][:], start=True, stop=True),
           reads=["wa2b", ("zab", 0)], writes=[("bank", b)])
        op("act", lambda e, b=b, tb_=tb_: e.activation(out=sp_[0][:], in_=banks[b][:], func=AF.Exp, scale=-1.0, bias=nba[:, 0:1]),
           reads=[("bank", b), "nba"], writes=[("sp", 0)])
        op("act", lambda e, tb_=tb_: e.activation(out=sp_[0][:], in_=sp_[0][:], func=AF.Ln, bias=1.0, scale=1.0),
           reads=[("sp", 0)], writes=[("sp", 0)])
        op("dve", lambda e, tb_=tb_: e.tensor_tensor_scan(out=cs[0][:], data0=cmask[:], data1=sp_[0][:], initial=0.0,
                                                          op0=ALU.mult, op1=ALU.add),
           reads=["cmask", ("sp", 0)], writes=[("cs", 0)])
        op("act", lambda e, tb_=tb_: e.activation(out=eb[0][:], in_=cs[0][:], func=AF.Exp, scale=-1.0 / 16.0),
           reads=[("cs", 0)], writes=[("eb", 0)])
        op("act", lambda e, tb_=tb_: e.activation(out=einv[0][:], in_=cs[0][:], func=AF.Exp, scale=1.0 / 16.0),
           reads=[("cs", 0)], writes=[("einv", 0)])
        op("act", lambda e, tb_=tb_: e.activation(out=E8[tb_][:, 0:4], in_=cs[0][:, 127::128], func=AF.Exp, scale=-1.0 / 16.0),
           reads=[("cs", 0)], writes=[("E8", tb_)])
        b = nb()
        proj(640, 128, tb_, b, None)
        op("dve", lambda e, b=b, tb_=tb_: e.tensor_tensor(out=kinv[0][:], in0=banks[b][:], in1=einv[0][:], op=ALU.mult),
           reads=[("bank", b), ("einv", 0)], writes=[("kinv", 0)])
        op("pool", lambda e, tb_=tb_: e.tensor_tensor(out=kend[0][:].rearrange("p (c t) -> p c t", c=4),
                                                      in0=kinv[0][:].rearrange("p (c t) -> p c t", c=4),
                                                      in1=E8[tb_][:, 0:4].unsqueeze(2).to_broadcast([128, 4, 128]), op=ALU.mult),
           reads=[("kinv", 0), ("E8", tb_)], writes=[("kend", 0)])
        b = nb()
        proj(512, 128, tb_, b, None)
        op("dve", lambda e, b=b, tb_=tb_: e.scalar_tensor_tensor(out=qd[0][:], in0=banks[b][:], scalar=128.0 ** -0.5,
                                                                in1=eb[0][:], op0=ALU.mult, op1=ALU.mult),
           reads=[("bank", b), ("eb", 0)], writes=[("qd", 0)])
        for gp in range(2):
            b = nb()

            def mmv(e, b=b, gp=gp, tb_=tb_):
                ins = None
                for g2 in range(2):
                    gi = gp * 2 + g2
                    for kc in range(8):
                        ins = e.matmul(banks[b][:, g2 * 256:(g2 + 1) * 256], lhsT=ub[tb_][:, kc, gi * 128:(gi + 1) * 128],
                                       rhs=wi[:, kc, 768:1024], start=(kc == 0), stop=(kc == 7))
                return ins
            op("pe", mmv, reads=WI + UB, writes=[("bank", b)])
            op("act", lambda e, b=b, gp=gp, tb_=tb_: e.activation(
                out=vtok[0][:, gp * 2:gp * 2 + 2, :].rearrange("p g v -> p (g v)"), in_=banks[b][:], func=AF.Identity),
               reads=[("bank", b)], writes=[("vtok", 0, gp)])
        for vc in range(2):
            b = nb()
            proj(1024 + vc * 128, 128, tb_, b, None)
            op("act", lambda e, b=b, vc=vc, tb_=tb_: e.activation(out=sg[0][:, vc, :], in_=banks[b][:], func=AF.Silu),
               reads=[("bank", b)], writes=[("sg", 0, vc)])
        for c in range(2):
            b = nb()
            proj(c * 128, 128, tb_, b, None)
            op("pool", lambda e, c=c: e.tensor_copy(out=xbuf[:, c, 0:3], in_=xbuf[:, c, 512:515]),
               reads=[("xbuf", c)], writes=[("xbuf", c)])
            op("act", lambda e, b=b, c=c: e.activation(out=xbuf[:, c, 3:515], in_=banks[b][:], func=AF.Identity),
               reads=[("bank", b)], writes=[("xbuf", c)])
            op("act", lambda e, c=c, tb_=tb_: e.activation(out=acc[0][:, c, :], in_=xbuf[:, c, 3:515], func=AF.Identity,
                                                        scale=cw[:, c, 3:4], bias=vec[:, 0, c:c + 1]),
               reads=[("xbuf", c), "cw", "vec"], writes=[("acc", 0, c)])
            for k in range(3):
                op("dve", lambda e, c=c, k=k, tb_=tb_: e.scalar_tensor_tensor(
                    out=acc[0][:, c, :], in0=xbuf[:, c, k:k + 512], scalar=cw[:, c, k:k + 1], in1=acc[0][:, c, :],
                    op0=ALU.mult, op1=ALU.add),
                   reads=[("xbuf", c), "cw", ("acc", 0, c)], writes=[("acc", 0, c)])
            op("pool", lambda e, c=c, tb_=tb_: e.tensor_copy(out=xcb[0][:, c, :], in_=acc[0][:, c, :]),
               reads=[("acc", 0, c)], writes=[("xcb", 0, c)])
            ba_ = nb()
            op("pe", lambda e, c=c, ba_=ba_, tb_=tb_: e.matmul(banks[ba_][:], lhsT=rgb[:, 0, c, :], rhs=xcb[0][:, c, :],
                                                           start=True, stop=True),
               reads=["rgb", ("xcb", 0, c)], writes=[("bank", ba_)])
            op("act", lambda e, c=c, ba_=ba_, tb_=tb_: e.activation(out=rr[0][:, c, :], in_=banks[ba_][:], func=AF.Sigmoid,
                                                                bias=vec[:, 1, c:c + 1], scale=1.0),
               reads=[("bank", ba_), "vec"], writes=[("rr", 0, c)])
            bx_ = nb()
            op("pe", lambda e, c=c, bx_=bx_, tb_=tb_: e.matmul(banks[bx_][:], lhsT=rgb[:, 1, c, :], rhs=xcb[0][:, c, :],
                                                           start=True, stop=True),
               reads=["rgb", ("xcb", 0, c)], writes=[("bank", bx_)])
            op("act", lambda e, c=c, bx_=bx_, tb_=tb_: e.activation(out=ii[0][:, c, :], in_=banks[bx_][:], func=AF.Sigmoid,
                                                                bias=vec[:, 2, c:c + 1], scale=1.0),
               reads=[("bank", bx_), "vec"], writes=[("ii", 0, c)])
            op("act", lambda e, c=c, tb_=tb_: e.activation(out=aa[0][:, c, :], in_=rr[0][:, c, :], func=AF.Exp,
                                                        scale=cl[:, 0, c:c + 1]),
               reads=[("rr", 0, c), "cl"], writes=[("aa", 0, c)])
            op("act", lambda e, c=c, tb_=tb_: e.activation(out=ss[0][:, c, :], in_=rr[0][:, c, :], func=AF.Exp,
                                                        scale=cl[:, 1, c:c + 1]),
               reads=[("rr", 0, c), "cl"], writes=[("ss", 0, c)])
            op("dve", lambda e, c=c, tb_=tb_: e.tensor_scalar(out=ss[0][:, c, :], in0=ss[0][:, c, :], scalar1=1.0, scalar2=0.0,
                                                           op0=ALU.subtract, op1=ALU.min),
               reads=[("ss", 0, c)], writes=[("ss", 0, c)])
            op("act", lambda e, c=c, tb_=tb_: e.activation(out=ss[0][:, c, :], in_=ss[0][:, c, :], func=AF.Sqrt,
                                                        scale=-1.0),
               reads=[("ss", 0, c)], writes=[("ss", 0, c)])
            op("dve", lambda e, c=c, tb_=tb_: e.tensor_tensor(out=uu[0][:, c, :], in0=ii[0][:, c, :], in1=acc[0][:, c, :],
                                                           op=ALU.mult),
               reads=[("ii", 0, c), ("acc", 0, c)], writes=[("uu", 0, c)])
            op("pool", lambda e, c=c, tb_=tb_: e.tensor_tensor(out=uu[0][:, c, :], in0=uu[0][:, c, :], in1=ss[0][:, c, :],
                                                            op=ALU.mult),
               reads=[("uu", 0, c), ("ss", 0, c)], writes=[("uu", 0, c)])
            init = 0.0 if t == 0 else hh[1 - tb_][:, c, 511:512]
            op("dve", lambda e, c=c, tb_=tb_, init=init: e.tensor_tensor_scan(
                out=hh[tb_][:, c, :], data0=aa[0][:, c, :], data1=uu[0][:, c, :], initial=init, op0=ALU.mult, op1=ALU.add),
               reads=[("aa", 0, c), ("uu", 0, c), ("hh", 1 - tb_, c)], writes=[("hh", tb_, c)])
            bg_ = nb()
            proj(256 + c * 128, 128, tb_, bg_, None)
            op("act", lambda e, c=c, bg_=bg_, tb_=tb_: e.activation(out=gel[0][:, c, :], in_=banks[bg_][:],
                                                                func=AF.Gelu_apprx_tanh),
               reads=[("bank", bg_)], writes=[("gel", 0, c)])
            op("dve", lambda e, c=c, tb_=tb_: e.tensor_tensor(out=yo[tb_][:, c, :], in0=hh[tb_][:, c, :], in1=gel[0][:, c, :],
                                                           op=ALU.mult),
               reads=[("hh", tb_, c), ("gel", 0, c)], writes=[("yo", tb_, c)])
        for gi in range(4):
            gsl = slice(gi * 128, (gi + 1) * 128)
            op("pe", lambda e, gsl=gsl, tb_=tb_: e.matmul(banks[3][:, gsl], lhsT=kinv[0][:, gsl], rhs=qd[0][:, gsl],
                                                      start=True, stop=True),
               reads=[("kinv", 0), ("qd", 0)], writes=[("bank3", gi)])
            op("dve", lambda e, gi=gi, gsl=gsl, tb_=tb_: e.tensor_tensor(out=sT[0][:, gi, :], in0=banks[3][:, gsl], in1=maskT[:],
                                                                    op=ALU.mult),
               reads=[("bank3", gi), "maskT"], writes=[("sT", 0, gi)])
            op("pe", lambda e, gsl=gsl, tb_=tb_: e.transpose(tps[:, gsl], kend[0][:, gsl], ident[:]),
               reads=[("kend", 0), "ident"], writes=[("tps", gi)])
            op("pool" if False else "act", lambda e, gi=gi, gsl=gsl, tb_=tb_: e.activation(out=ktok[0][:, gi, :], in_=tps[:, gsl],
                                                                                       func=AF.Identity),
               reads=[("tps", gi)], writes=[("ktok", 0, gi)])
            c = gi
            for vc in range(2):
                def mmo(e, vc=vc, gsl=gsl, gi=gi, tb_=tb_, sbi=sbi):
                    e.matmul(banks[5 + vc][:, gsl], lhsT=Sb[sbi][:, vc * 128:(vc + 1) * 128], rhs=qd[0][:, gsl],
                             start=True, stop=False)
                    return e.matmul(banks[5 + vc][:, gsl], lhsT=vtok[0][:, gi, vc * 128:(vc + 1) * 128],
                                    rhs=sT[0][:, gi, :], start=False, stop=True)
                op("pe", mmo, reads=[("Sb", sbi), ("qd", 0), ("vtok", 0, gi // 2), ("sT", 0, gi)],
                   writes=[("obank", vc, c)])
            slot = c % 2
            op("pe", lambda e, gi=gi, tb_=tb_, slot=slot: e.matmul(
                banks[4][:, slot * 256:(slot + 1) * 256], lhsT=ktok[0][:, gi, :], rhs=vtok[0][:, gi, :],
                start=True, stop=True),
               reads=[("ktok", 0, gi), ("vtok", 0, gi // 2)], writes=[("bank4", slot)])
            op("dve", lambda e, c=c, tb_=tb_, slot=slot: e.scalar_tensor_tensor(
                out=Sf[:], in0=Sf[:], scalar=E8[tb_][:, c:c + 1], in1=banks[4][:, slot * 256:(slot + 1) * 256],
                op0=ALU.mult, op1=ALU.add),
               reads=["Sf", ("E8", tb_), ("bank4", slot)], writes=["Sf"])
            sbi = 1 - sbi
            op("act", lambda e, sbi=sbi: e.activation(out=Sb[sbi][:], in_=Sf[:], func=AF.Identity),
               reads=["Sf"], writes=[("Sb", sbi)])
        OB = lambda vc: [("obank", vc, c) for c in range(4)]
        for vc in range(2):
            op("act", lambda e, vc=vc, tb_=tb_: e.activation(out=osq[0][:, vc, :], in_=banks[5 + vc][:], func=AF.Square),
               reads=OB(vc), writes=[("osq", 0, vc)])
        b = nb()

        def mms(e, b=b, tb_=tb_):
            e.matmul(banks[b][:], lhsT=ones256[:], rhs=osq[0][:, 0, :], start=True, stop=False)
            return e.matmul(banks[b][:], lhsT=ones256[:], rhs=osq[0][:, 1, :], start=False, stop=True)
        op("pe", mms, reads=["ones256", ("osq", 0, 0), ("osq", 0, 1)], writes=[("bank", b)])
        op("act", lambda e, b=b, tb_=tb_: e.activation(out=rs[0][:], in_=banks[b][:], func=AF.Sqrt, bias=eps1[:, 0:1], scale=1.0),
           reads=[("bank", b), "eps1"], writes=[("rs", 0)])
        op("dve", lambda e, tb_=tb_: e.reciprocal(out=rs[0][:], in_=rs[0][:]), reads=[("rs", 0)], writes=[("rs", 0)])
        for vc in range(2):
            op("dve", lambda e, vc=vc, tb_=tb_: e.tensor_tensor(out=tt[0][:, vc, :], in0=banks[5 + vc][:], in1=rs[0][:],
                                                             op=ALU.mult),
               reads=OB(vc) + [("rs", 0)], writes=[("tt", 0, vc)])
            op("dve", lambda e, vc=vc, tb_=tb_: e.scalar_tensor_tensor(
                out=yo[tb_][:, 2 + vc, :], in0=tt[0][:, vc, :], scalar=vec[:, 4, vc:vc + 1], in1=sg[0][:, vc, :],
                op0=ALU.mult, op1=ALU.mult),
               reads=[("tt", 0, vc), "vec", ("sg", 0, vc)], writes=[("yo", tb_, 2 + vc)])
        P.dma("sp", ytile(t), yo[tb_][:], reads=[("yo", tb_, i) for i in range(4)], writes=[("even_y_dram", t)],
              semkey=("yst", tb_))


def even_consts():
    ident = np.eye(128, dtype=np.float32)
    j = np.arange(128)[:, None]
    i = np.arange(128)[None, :]
    maskT = (j <= i).astype(np.float32)
    cm = np.ones((128, 512), np.float32)
    cm[:, ::128] = 0.0
    return ident, maskT, cm


def even_inputs(x1T_b, c_b, mod_w_l, mod_b_l, q, I):
    ident, maskT, cm = even_consts()
    w = I["even_w_in"][0]
    cols = np.concatenate([
        np.arange(q * 256, (q + 1) * 256),
        1024 + np.arange(q * 256, (q + 1) * 256),
        2048 + np.arange(q * 128, (q + 1) * 128),
        2560 + np.arange(q * 128, (q + 1) * 128),
        3072 + np.arange(q * 256, (q + 1) * 256),
        4096 + np.arange(q * 256, (q + 1) * 256),
        5120 + np.arange(16)])
    ch = slice(q * 256, (q + 1) * 256)
    convw = np.ascontiguousarray(I["conv_w"][0][:, ch].reshape(4, 2, 128).transpose(2, 1, 0))
    vec = np.stack([fm(I["conv_b"][0][ch]), fm(I["rg_ba"][0][ch]), fm(I["rg_bx"][0][ch]), fm(I["rg_lam"][0][ch]),
                    fm(I["gla_norm_g"][0])], axis=1)
    rgw = np.stack([I["rg_wa"][0][q * 4:(q + 1) * 4], I["rg_wx"][0][q * 4:(q + 1) * 4]], axis=0)
    return {
        "xT": np.ascontiguousarray(x1T_b), "cT": fm(c_b),
        "mw": np.ascontiguousarray(mod_w_l[:, 3 * D:6 * D]), "mb": fm(mod_b_l[3 * D:6 * D]),
        "win": np.ascontiguousarray(w[:, cols]), "convw": convw, "rgvec": np.ascontiguousarray(vec),
        "rgw": np.ascontiguousarray(rgw), "wa2": np.ascontiguousarray(I["gla_wa2"][0][:, q * 128:(q + 1) * 128]),
        "gba": fm(I["gla_ba"][0][q * 128:(q + 1) * 128]),
        "ident": ident, "maskT": maskT, "cmask": cm,
    }


MIX_IN = [("cT", [128, 8], F32), ("mw", [D, 3 * D], F32), ("mb", [128, 24], F32), ("lng", [128, 8], F32),
          ("lnb", [128, 8], F32)]


def build_mixout_prog(ntok, KC):
    P = Prog()
    T = {n: P.dram_in(n, sh, dt) for n, sh, dt in MIX_IN}
    T["wout"] = P.dram_in("wout", [KC * 128, D], F32)
    x_d = P.dram_in("xT", [D, ntok], F32)
    y_d = P.dram_in("yT", [KC * 128, ntok], BF16)
    o_d = P.dram_out("oT", [D, ntok], F32)
    yv = y_d.rearrange("(c p) t -> p c t", p=128)

    def ysrc(t):
        return [(4 * h, 4, yv[:, h * 4:(h + 1) * 4, t * 512:(t + 1) * 512]) for h in range(KC // 4)]
    emit_mixout_stage(P, None, T, ntok, KC, x_d, o_d, ysrc, [])
    return P.build()


def emit_mixout_stage(P, banks, T, ntok, KC, x_d, o_d, ysrc, yreads):
    op = P.op
    cT_d, mw_d, mb_d, w_d, lng_d, lnb_d = T["cT"], T["mw"], T["mb"], T["wout"], T["lng"], T["lnb"]
    C = Ctx(P, banks)
    stage = [(P.sb("mst%d" % i, [128, 8, 256], F32), "mst%d" % i) for i in range(2)]
    modv, sc1, gs, vk = emit_mod(P, C, "m_", cT_d, mw_d, mb_d, 1.0, stage)
    g = P.sb("lng", [128, 8], F32)
    b = P.sb("lnb", [128, 8], F32)
    P.dma("sp", g[:], lng_d, writes=["gb"], semkey="g")
    P.dma("sp", b[:], lnb_d, writes=["gb"], semkey="b")
    w = P.sb("w", [128, KC, D], BF16)
    wv = w_d.rearrange("(kc p) n -> p kc n", p=128)
    for h in range(KC // 4):
        P.dma("pool", w[:, h * 4:(h + 1) * 4, :], wv[:, h * 4:(h + 1) * 4, :], writes=[("w", h)])
    WK = [("w", h) for h in range(KC // 4)]
    xt = [P.sb("xt%d" % i, [128, 8, 512], F32) for i in range(2)]
    yt = [P.sb("yt%d" % i, [128, KC, 512], BF16) for i in range(2)]
    lnt = ln_tmp(P, "l_")
    xv = x_d.rearrange("(c p) t -> p c t", p=128)
    ov = o_d.rearrange("(c p) t -> p c t", p=128)
    NT = ntok // 512

    def load(t):
        tb = t % 2
        for half in range(2):
            P.dma("sp", xt[tb][:, half * 4:(half + 1) * 4, :], xv[:, half * 4:(half + 1) * 4, t * 512:(t + 1) * 512],
                  writes=[("xt", tb, dc) for dc in range(half * 4, half * 4 + 4)], semkey=("xld", tb, half))
        for k0, nk, src in ysrc(t):
            P.dma("sp", yt[tb][:, k0:k0 + nk, :], src, reads=yreads, writes=[("yt", tb, k0)], semkey=("yld", tb, k0))
    load(0)
    cnt = 0
    for t in range(NT):
        tb = t % 2
        if t + 1 < NT:
            load(t + 1)
        for dc in range(8):
            bi = cnt % 4
            cnt += 1

            def mm(e, dc=dc, bi=bi, tb=tb):
                ins = None
                for kc in range(KC):
                    ins = e.matmul(C.banks[bi][:], lhsT=w[:, kc, dc * 128:(dc + 1) * 128], rhs=yt[tb][:, kc, :],
                                   start=(kc == 0), stop=(kc == KC - 1))
                return ins
            op("pe", mm, reads=WK + [("yt", tb, k0) for k0, _, _ in ysrc(0)], writes=[("bank", bi)])
            op("dve", lambda e, dc=dc, bi=bi, tb=tb: e.scalar_tensor_tensor(
                out=xt[tb][:, dc, :], in0=C.banks[bi][:], scalar=gs[:, dc:dc + 1], in1=xt[tb][:, dc, :],
                op0=ALU.mult, op1=ALU.add),
               reads=[("bank", bi), vk, ("xt", tb, dc)], writes=[("xt", tb, dc)])
        emit_ln(P, C, "l_", 0, lambda dc, tb=tb: xt[tb][:, dc, :], lambda dc, tb=tb: ("xt", tb, dc), g, b, "gb",
                lambda dc, tb=tb: xt[tb][:, dc, :], lambda dc, tb=tb: ("xt", tb, dc), lnt)
        for half in range(2):
            P.dma("sp", ov[:, half * 4:(half + 1) * 4, t * 512:(t + 1) * 512], xt[tb][:, half * 4:(half + 1) * 4, :],
                  reads=[("xt", tb, dc) for dc in range(half * 4, half * 4 + 4)], writes=[("mix_o_dram", t, half)],
                  semkey=("ost", tb, half))


def mixout_inputs(xT, yT, c_b, mod_w_l, mod_b_l, w_out, ln_g, ln_b):
    return {"xT": None if xT is None else np.ascontiguousarray(xT), "yT": None if yT is None else np.ascontiguousarray(yT), "cT": fm(c_b),
            "mw": np.ascontiguousarray(mod_w_l[:, 3 * D:6 * D]), "mb": fm(mod_b_l[3 * D:6 * D]),
            "wout": np.ascontiguousarray(w_out), "lng": fm(ln_g), "lnb": fm(ln_b)}


QSCALE = 192.0 ** -0.5
C1_2PI = 6.28125
C2_2PI = 2.0 * math.pi - 6.28125
PI_LO = 3.141592


MLA_IN = [("cT", [128, 8], F32), ("mw", [D, 3 * D], F32), ("mb", [128, 24], F32), ("win", [D, 448], F32),
          ("ng", [128, 3], F32), ("wq", [256, 1536], F32), ("wkv", [128, 2048], F32), ("fr", [64, 1], F32)]


def build_mlaproj_prog(ntok):
    P = Prog()
    T = {n: P.dram_in(n, sh, dt) for n, sh, dt in MLA_IN}
    T["pos"] = P.dram_in("pos", [1, ntok], I32)
    x_d = P.dram_in("xT", [D, ntok], F32)
    q_o = P.dram_out("qT", [1536, ntok], BF16)
    kn_o = P.dram_out("kTn", [1024, ntok], BF16)
    kr_o = P.dram_out("kTr", [64, ntok], BF16)
    v_o = P.dram_out("v", [ntok, 1024], BF16)
    banks = [P.ps("bank%d" % i, [128, 512]) for i in range(8)]
    emit_mlaproj_stage(P, banks, T, ntok, x_d, q_o, kn_o, kr_o, v_o.rearrange("t (h d) -> t h d", h=4))
    return P.build()


def emit_mlaproj_stage(P, banks, T, ntok, x_d, q_o, kn_o, kr_o, v_view):
    op = P.op
    cT_d, mw_d, mb_d, win_d, ng_d, wq_d, wkv_d, pos_d, fr_d = (T["cT"], T["mw"], T["mb"], T["win"], T["ng"], T["wq"],
                                                                T["wkv"], T["pos"], T["fr"])

    class C_:
        pass
    C = C_()
    C.banks = banks
    stage = [(P.sb("mst%d" % i, [128, 8, 256], F32), "mst%d" % i) for i in range(2)]
    modv, sc1, gs_unused, vk = emit_mod(P, C, "m_", cT_d, mw_d, mb_d, 1.0, stage, bank_idx=7)
    sh = modv[:, 0:8]

    wi = P.sb("wi", [128, 8, 448], BF16)
    wir = P.sb("wir", [128, 8, 64], BF16)
    P.dma("pool", wi[:], win_d.rearrange("(kc p) n -> p kc n", p=128), writes=["wi"])
    op("dve", lambda e: e.tensor_scalar(out=wir[:, :, 0:32], in0=wi[:, :, 416:448], scalar1=-1.0, scalar2=1.0,
                                        op0=ALU.mult, op1=ALU.mult), reads=["wi"], writes=["wir"])
    op("dve", lambda e: e.tensor_copy(out=wir[:, :, 32:64], in_=wi[:, :, 384:416]), reads=["wi"], writes=["wir"])
    wq = P.sb("wq", [128, 2, 1536], BF16)
    P.dma("pool", wq[:], wq_d.rearrange("(kc p) n -> p kc n", p=128), writes=["wq"])
    wqr = P.sb("wqr", [128, 2, 8, 64], BF16)
    for kc in range(2):
        wq4 = wq[:, kc, :].rearrange("p (h n) -> p h n", h=8)
        op("dve", lambda e, kc=kc, wq4=wq4: e.tensor_scalar(out=wqr[:, kc, :, 0:32], in0=wq4[:, :, 160:192], scalar1=-1.0,
                                                          scalar2=1.0, op0=ALU.mult, op1=ALU.mult),
           reads=["wq"], writes=["wqr"])
        op("dve", lambda e, kc=kc, wq4=wq4: e.tensor_copy(out=wqr[:, kc, :, 32:64], in_=wq4[:, :, 128:160]),
           reads=["wq"], writes=["wqr"])
    wkv = P.sb("wkv", [128, 2048], BF16)
    P.dma("pool", wkv[:], wkv_d, writes=["wkv"])
    ng = P.sb("ng", [128, 3], F32)
    P.dma("sp", ng[:], ng_d, writes=["ng"])
    ones256 = P.sb("ones256", [128, 128], BF16)
    ones128 = P.sb("ones128", [128, 128], BF16)
    op("pool", lambda e: e.memset(ones256[:], 1.0 / 256.0), writes=["ones256"])
    op("pool", lambda e: e.memset(ones128[:], 1.0 / 128.0), writes=["ones128"])
    eps1 = P.sb("eps1", [128, 1], F32)
    op("pool", lambda e: e.memset(eps1[:], LN_EPS), writes=["eps1"])

    pi_ = P.sb("posi", [64, 512], I32)
    fr = P.sb("fr", [64, 1], F32)
    ang = P.sb("ang", [64, 512], F32)
    kf = P.sb("kf", [64, 512], F32)
    ki = P.sb("ki", [64, 512], I32)
    cos2 = P.sb("cos2", [64, 512], F32)
    sin2 = P.sb("sin2", [64, 512], F32)
    cos2q = P.sb("cos2q", [64, 512], F32)
    sin2q = P.sb("sin2q", [64, 512], F32)
    P.dma("sp", fr[:], fr_d, writes=["fr"])

    def rope_tables(t):
        P.dma("sp", pi_[:], pos_d[0:1, t * 512:(t + 1) * 512].partition_broadcast(64), writes=["posi"])
        op("dve", lambda e: e.tensor_copy(out=ang[:], in_=pi_[:]), reads=["posi"], writes=["ang"])
        op("dve", lambda e: e.tensor_scalar(out=ang[:], in0=ang[:], scalar1=fr[:, 0:1], scalar2=1.0, op0=ALU.mult, op1=ALU.mult),
           reads=["ang", "fr"], writes=["ang"])
        op("dve", lambda e: e.tensor_scalar(out=kf[:], in0=ang[:], scalar1=1.0 / (2.0 * math.pi), scalar2=1.0, op0=ALU.mult,
                                            op1=ALU.mult), reads=["ang"], writes=["kf"])
        op("dve", lambda e: e.tensor_copy(out=ki[:], in_=kf[:]), reads=["kf"], writes=["ki"])
        op("dve", lambda e: e.tensor_copy(out=kf[:], in_=ki[:]), reads=["ki"], writes=["kf"])
        op("dve", lambda e: e.scalar_tensor_tensor(out=ang[:], in0=kf[:], scalar=-C1_2PI, in1=ang[:], op0=ALU.mult, op1=ALU.add),
           reads=["kf", "ang"], writes=["ang"])
        op("dve", lambda e: e.scalar_tensor_tensor(out=ang[:], in0=kf[:], scalar=-C2_2PI, in1=ang[:], op0=ALU.mult, op1=ALU.add),
           reads=["kf", "ang"], writes=["ang"])
        op("dve", lambda e: e.tensor_scalar(out=sin2[:], in0=ang[:], scalar1=-PI_LO, scalar2=PI_LO, op0=ALU.max, op1=ALU.min),
           reads=["ang"], writes=["sin2"])
        op("act", lambda e: e.activation(out=sin2[:], in_=sin2[:], func=AF.Sin), reads=["sin2"], writes=["sin2"])
        op("dve", lambda e: e.tensor_scalar(out=cos2[:], in0=ang[:], scalar1=math.pi / 2.0, scalar2=1.0, op0=ALU.add, op1=ALU.mult),
           reads=["ang"], writes=["cos2"])
        op("dve", lambda e: e.tensor_scalar(out=kf[:], in0=cos2[:], scalar1=math.pi, scalar2=2.0 * math.pi, op0=ALU.is_gt,
                                            op1=ALU.mult), reads=["cos2"], writes=["kf"])
        op("dve", lambda e: e.tensor_tensor(out=cos2[:], in0=cos2[:], in1=kf[:], op=ALU.subtract), reads=["cos2", "kf"],
           writes=["cos2"])
        op("dve", lambda e: e.tensor_scalar(out=cos2[:], in0=cos2[:], scalar1=-PI_LO, scalar2=PI_LO, op0=ALU.max, op1=ALU.min),
           reads=["cos2"], writes=["cos2"])
        op("act", lambda e: e.activation(out=cos2[:], in_=cos2[:], func=AF.Sin), reads=["cos2"], writes=["cos2"])
        op("pool", lambda e: e.tensor_scalar(out=cos2q[:], in0=cos2[:], scalar1=QSCALE, scalar2=1.0, op0=ALU.mult, op1=ALU.mult),
           reads=["cos2"], writes=["cos2q"])
        op("pool", lambda e: e.tensor_scalar(out=sin2q[:], in0=sin2[:], scalar1=QSCALE, scalar2=1.0, op0=ALU.mult, op1=ALU.mult),
           reads=["sin2"], writes=["sin2q"])
    TAB = ["cos2", "sin2", "cos2q", "sin2q"]

    xt = [P.sb("xt%d" % i, [128, 8, 512], F32) for i in range(2)]
    ub = [P.sb("ub%d" % i, [128, 8, 512], BF16) for i in range(2)]
    sq = P.sb("sq", [128, 3, 512], BF16)
    rstd = P.sb("rstd", [128, 2, 512], F32)
    cqn = P.sb("cqn", [128, 2, 512], BF16)
    ckvn = P.sb("ckvn", [128, 512], BF16)
    t1 = [P.sb("t1_%d" % i, [64, 512], F32) for i in range(2)]
    t2 = [P.sb("t2_%d" % i, [64, 512], F32) for i in range(2)]
    krs = P.sb("krs", [64, 512], BF16)
    qst = [P.sb("qst%d" % i, [128, 512], BF16) for i in range(2)]
    qrs = [P.sb("qrs%d" % i, [64, 512], BF16) for i in range(2)]
    kst = [P.sb("kst%d" % i, [128, 512], BF16) for i in range(2)]
    vst = [P.sb("vst%d" % i, [128, 1024], BF16) for i in range(2)]
    xv = x_d.rearrange("(c p) t -> p c t", p=128)
    NT = ntok // 512
    rot = [0]

    def nb():
        i = rot[0] % 8
        rot[0] += 1
        return i

    def load(t):
        tb = t % 2
        for half in range(2):
            P.dma("sp", xt[tb][:, half * 4:(half + 1) * 4, :], xv[:, half * 4:(half + 1) * 4, t * 512:(t + 1) * 512],
                  writes=[("xt", tb, dc) for dc in range(half * 4, half * 4 + 4)], semkey=("xld", tb, half))
    load(0)
    cnt = [0]
    def do_tile(t):
        tb = t % 2
        tsl = slice(t * 512, (t + 1) * 512)
        if t + 1 < NT:
            load(t + 1)
        rope_tables(t)
        for dc in range(8):
            op("act", lambda e, dc=dc, tb=tb: e.activation(out=ub[tb][:, dc, :], in_=xt[tb][:, dc, :], func=AF.Identity,
                                                        bias=sh[:, dc:dc + 1], scale=sc1[:, dc:dc + 1]),
               reads=[("xt", tb, dc), vk], writes=[("ub", tb, dc)])
        UB = [("ub", tb, dc) for dc in range(8)]

        def proj(col0, ncol, b, wt=None):
            wt_ = wi if wt is None else wt

            def mm(e):
                ins = None
                for kc in range(8):
                    ins = e.matmul(banks[b][0:ncol, :], lhsT=wt_[:, kc, col0:col0 + ncol], rhs=ub[tb][:, kc, :],
                                   start=(kc == 0), stop=(kc == 7))
                return ins
            op("pe", mm, reads=["wi", "wir"] + UB, writes=[("bank", b)])
        bq0, bq1, bkv, bkp, bkr = nb(), nb(), nb(), nb(), nb()
        proj(0, 128, bq0)
        proj(128, 128, bq1)
        proj(256, 128, bkv)
        proj(384, 64, bkp)
        proj(0, 64, bkr, wt=wir)
        for i, b in enumerate([bq0, bq1, bkv]):
            op("act", lambda e, i=i, b=b: e.activation(out=sq[:, i, :], in_=banks[b][:], func=AF.Square),
               reads=[("bank", b)], writes=[("sq", i)])
        bs_q, bs_kv = nb(), nb()

        def mmsq(e):
            e.matmul(banks[bs_q][:], lhsT=ones256[:], rhs=sq[:, 0, :], start=True, stop=False)
            return e.matmul(banks[bs_q][:], lhsT=ones256[:], rhs=sq[:, 1, :], start=False, stop=True)
        op("pe", mmsq, reads=["ones256", ("sq", 0), ("sq", 1)], writes=[("bank", bs_q)])
        op("pe", lambda e: e.matmul(banks[bs_kv][:], lhsT=ones128[:], rhs=sq[:, 2, :], start=True, stop=True),
           reads=["ones128", ("sq", 2)], writes=[("bank", bs_kv)])
        for i, b in enumerate([bs_q, bs_kv]):
            op("act", lambda e, i=i, b=b: e.activation(out=rstd[:, i, :], in_=banks[b][:], func=AF.Sqrt, bias=eps1[:, 0:1],
                                                       scale=1.0), reads=[("bank", b), "eps1"], writes=[("rstd", i)])
            op("dve", lambda e, i=i: e.reciprocal(out=rstd[:, i, :], in_=rstd[:, i, :]), reads=[("rstd", i)],
               writes=[("rstd", i)])
        for i, b in enumerate([bq0, bq1]):
            op("dve", lambda e, i=i, b=b: e.scalar_tensor_tensor(out=cqn[:, i, :], in0=banks[b][:], scalar=ng[:, i:i + 1],
                                                                 in1=rstd[:, 0, :], op0=ALU.mult, op1=ALU.mult),
               reads=[("bank", b), "ng", ("rstd", 0)], writes=[("cqn", i)])
        op("dve", lambda e: e.scalar_tensor_tensor(out=ckvn[:], in0=banks[bkv][:], scalar=ng[:, 2:3], in1=rstd[:, 1, :],
                                                   op0=ALU.mult, op1=ALU.mult),
           reads=[("bank", bkv), "ng", ("rstd", 1)], writes=["ckvn"])
        op("dve", lambda e: e.tensor_tensor(out=t1[0][:], in0=banks[bkp][0:64, :], in1=cos2[:], op=ALU.mult),
           reads=[("bank", bkp)] + TAB, writes=[("t1", 0)])
        op("dve", lambda e: e.tensor_tensor(out=t2[0][:], in0=banks[bkr][0:64, :], in1=sin2[:], op=ALU.mult),
           reads=[("bank", bkr)] + TAB, writes=[("t2", 0)])
        op("pool", lambda e: e.tensor_tensor(out=krs[:], in0=t1[0][:], in1=t2[0][:], op=ALU.add),
           reads=[("t1", 0), ("t2", 0)], writes=["krs"])
        P.dma("sp", kr_o[:, tsl], krs[:], reads=["krs"], semkey="krst")
        for h in range(8):
            i2 = cnt[0] % 2
            cnt[0] += 1
            bn, br, brr = nb(), nb(), nb()

            def mmq(e, h=h, bn=bn, br=br, brr=brr):
                for kc in range(2):
                    e.matmul(banks[bn][:], lhsT=wq[:, kc, h * 192:h * 192 + 128], rhs=cqn[:, kc, :], start=(kc == 0),
                             stop=(kc == 1))
                for kc in range(2):
                    e.matmul(banks[br][0:64, :], lhsT=wq[:, kc, h * 192 + 128:h * 192 + 192], rhs=cqn[:, kc, :],
                             start=(kc == 0), stop=(kc == 1))
                ins = None
                for kc in range(2):
                    ins = e.matmul(banks[brr][0:64, :], lhsT=wqr[:, kc, h, :], rhs=cqn[:, kc, :], start=(kc == 0),
                                   stop=(kc == 1))
                return ins
            op("pe", mmq, reads=["wq", "wqr", ("cqn", 0), ("cqn", 1)], writes=[("bank", bn), ("bank", br), ("bank", brr)])
            op("act", lambda e, bn=bn, i2=i2: e.activation(out=qst[i2][:], in_=banks[bn][:], func=AF.Identity, scale=QSCALE),
               reads=[("bank", bn)], writes=[("qst", i2)])
            op("dve", lambda e, br=br, i2=i2: e.tensor_tensor(out=t1[i2][:], in0=banks[br][0:64, :], in1=cos2q[:],
                                                            op=ALU.mult), reads=[("bank", br)] + TAB, writes=[("t1", i2)])
            op("dve", lambda e, brr=brr, i2=i2: e.tensor_tensor(out=t2[i2][:], in0=banks[brr][0:64, :], in1=sin2q[:],
                                                              op=ALU.mult), reads=[("bank", brr)] + TAB, writes=[("t2", i2)])
            op("pool", lambda e, i2=i2: e.tensor_tensor(out=qrs[i2][:], in0=t1[i2][:], in1=t2[i2][:], op=ALU.add),
               reads=[("t1", i2), ("t2", i2)], writes=[("qrs", i2)])
            P.dma("sp", q_o[h * 192:h * 192 + 128, tsl], qst[i2][:], reads=[("qst", i2)], semkey=("qst", i2))
            P.dma("sp", q_o[h * 192 + 128:h * 192 + 192, tsl], qrs[i2][:], reads=[("qrs", i2)], semkey=("qrs", i2))
            bk = nb()
            op("pe", lambda e, h=h, bk=bk: e.matmul(banks[bk][:], lhsT=wkv[:, h * 256:h * 256 + 128], rhs=ckvn[:],
                                                   start=True, stop=True), reads=["wkv", "ckvn"], writes=[("bank", bk)])
            op("act", lambda e, bk=bk, i2=i2: e.activation(out=kst[i2][:], in_=banks[bk][:], func=AF.Identity),
               reads=[("bank", bk)], writes=[("kst", i2)])
            P.dma("sp", kn_o[h * 128:(h + 1) * 128, tsl], kst[i2][:], reads=[("kst", i2)], semkey=("kst", i2))
        wkv3 = wkv[:].rearrange("p (h n) -> p h n", h=8)
        for g in range(4):
            i2 = g % 2
            b0, b1 = nb(), nb()

            def mmv(e, g=g, b0=b0, b1=b1):
                e.matmul(banks[b0][:].rearrange("p (h n) -> p h n", h=4), lhsT=ckvn[:, g * 128:(g + 1) * 128],
                         rhs=wkv3[:, 0:4, 128:256], start=True, stop=True)
                return e.matmul(banks[b1][:].rearrange("p (h n) -> p h n", h=4), lhsT=ckvn[:, g * 128:(g + 1) * 128],
                                rhs=wkv3[:, 4:8, 128:256], start=True, stop=True)
            op("pe", mmv, reads=["wkv", "ckvn"], writes=[("bank", b0), ("bank", b1)])
            op("act", lambda e, b0=b0, i2=i2: e.activation(out=vst[i2][:, 0:512], in_=banks[b0][:], func=AF.Identity),
               reads=[("bank", b0)], writes=[("vst", i2, 0)])
            op("dve", lambda e, b1=b1, i2=i2: e.tensor_copy(out=vst[i2][:, 512:1024], in_=banks[b1][:]),
               reads=[("bank", b1)], writes=[("vst", i2, 1)])
            P.dma("sp", v_view[t * 512 + g * 128:t * 512 + (g + 1) * 128, :, :], vst[i2][:].rearrange("p (h d) -> p h d", h=4),
                  reads=[("vst", i2, 0), ("vst", i2, 1)], semkey=("vst", i2))
    for t_ in range(NT):
        do_tile(t_)


def rope_freqs():
    half = 32
    f = (10000.0 ** (-np.arange(half, dtype=np.float32) / half)).astype(np.float32)
    return np.ascontiguousarray(np.concatenate([f, f])[:, None])


def mlaproj_inputs(xT, c_b, mod_w_l, mod_b_l, pos, I):
    return {"xT": None if xT is None else np.ascontiguousarray(xT), "cT": fm(c_b),
            "mw": np.ascontiguousarray(mod_w_l[:, 3 * D:6 * D]), "mb": fm(mod_b_l[3 * D:6 * D]),
            "win": np.ascontiguousarray(I["odd_w_in"][0]),
            "ng": np.ascontiguousarray(np.concatenate([fm(I["q_norm_g"][0]), fm(I["kv_norm_g"][0])], axis=1)),
            "wq": np.ascontiguousarray(I["w_q_up"][0]), "wkv": np.ascontiguousarray(I["w_kv_up"][0]),
            "pos": np.ascontiguousarray(pos.reshape(1, -1).astype(np.int32)), "fr": rope_freqs()}


def build_attn_prog(S, NH=2):
    P = Prog()
    q_d = P.dram_in("qT", [NH, 192, S], BF16)
    kn_d = P.dram_in("kTn", [NH, 128, S], BF16)
    kr_d = P.dram_in("kTr", [64, S], BF16)
    v_d = P.dram_in("v", [NH, S, 128], BF16)
    T = {"maskT": P.dram_in("maskT", [128, 128], F32), "kpad": P.dram_in("kpad", [32, S], BF16),
         "ident": P.dram_in("ident", [128, 128], F32)}
    o_d = P.dram_out("oT", [NH, 128, S], BF16)
    banks = [P.ps("bank%d" % i, [128, 512]) for i in range(8)]
    NKB = S // 128
    src = {
        "q": lambda hh, qt: (q_d[hh, 0:128, qt * 512:(qt + 1) * 512], q_d[hh, 128:192, qt * 512:(qt + 1) * 512]),
        "kA": lambda hh: [(slice(h2 * (S // 2), (h2 + 1) * (S // 2)), kn_d[hh, :, h2 * (S // 2):(h2 + 1) * (S // 2)])
                          for h2 in range(2)],
        "kB": lambda: [(slice(0, S), kr_d)],
        "v": lambda hh: [(slice(qr * (NKB // 4), (qr + 1) * (NKB // 4)),
                          v_d[hh].rearrange("(blk p) d -> p blk d", p=128)[:, qr * (NKB // 4):(qr + 1) * (NKB // 4), :])
                         for qr in range(4)],
        "o": lambda hh, qt: o_d[hh, :, qt * 512:(qt + 1) * 512],
    }
    emit_attn_stage(P, banks, T, S, NH, src)
    return P.build()


def emit_attn_stage(P, banks, T, S, NH, src):
    op = P.op
    mask_d, kpad_d, ident_d = T["maskT"], T["kpad"], T["ident"]
    pb7 = banks[7][:].bitcast(BF16)
    NKB = S // 128
    NQT = S // 512
    kA = P.sb("kA", [128, S], BF16)
    kB = P.sb("kB", [96, S], BF16)
    vS = P.sb("vS", [128, NKB, 130], BF16)
    maskf = P.sb("maskf", [128, 128], F32)
    maskb = P.sb("maskb", [128, 128], BF16)
    ones = P.sb("ones", [128, 128], BF16)
    identf = P.sb("identf", [128, 128], F32)
    identb = P.sb("identb", [128, 128], BF16)
    P.dma("sp", identf[:], ident_d, writes=["identf"])
    op("dve", lambda e: e.tensor_copy(out=identb[:], in_=identf[:]), reads=["identf"], writes=["identb"])
    oTs = [P.sb("oTs%d" % i, [128, 512], BF16) for i in range(2)]
    P.dma("sp", maskf[:], mask_d, writes=["maskf"])
    op("dve", lambda e: e.tensor_copy(out=maskb[:], in_=maskf[:]), reads=["maskf"], writes=["maskb"])
    op("pool", lambda e: e.memset(ones[:], 1.0), writes=["ones"])
    for ki_, (csl_, ap_) in enumerate(src["kB"]()):
        P.dma("sp", kB[0:64, csl_], ap_, writes=[("kB", ki_)], semkey=("kB", ki_))
    KBK = [("kB", ki_) for ki_ in range(len(src["kB"]()))]
    P.dma("sp", kB[64:96, :], kpad_d, writes=["kBones"])
    op("pool", lambda e: e.memset(vS[:, :, 128:130], 1.0), writes=["vSones"])
    qA = [P.sb("qA%d" % i, [128, 512], BF16) for i in range(2)]
    qB = [P.sb("qB%d" % i, [96, 512], BF16) for i in range(2)]
    sqA = P.sb("sqA", [128, 512], BF16)
    sqB = P.sb("sqB", [64, 512], BF16)
    nrm = P.sb("nrm", [96, 512], F32)
    kmx = P.sb("kmx", [128, 1], F32)
    kmt = P.sb("kmt", [128, 1], F32)
    negK = P.sb("negK", [128, 1], F32)
    pT = [P.sb("pT%d" % i, [128, 512], BF16) for i in range(3)]
    ost = [P.sb("ost%d" % i, [128, 4, 128], BF16) for i in range(2)]
    rl = [P.sb("rl%d" % i, [128, 4], F32) for i in range(2)]
    step = [0]
    qcnt = [0]

    def do_head(hh):
        kal = src["kA"](hh)
        for ki_, (csl_, ap_) in enumerate(kal):
            P.dma("sp", kA[:, csl_], ap_, writes=[("kA", ki_)], semkey=("kA", ki_))
        vl = src["v"](hh)
        for vi_, (bs, ap_) in enumerate(vl):
            P.dma("sp", vS[:, bs, 0:128], ap_, writes=[("vS", vi_)], semkey=("vS", vi_))
        KA = [("kA", ki_) for ki_ in range(len(kal))]
        VS = [("vS", vi_) for vi_ in range(len(vl))] + ["vSones"]
        op("pool", lambda e: e.memset(kmx[:], 0.0), writes=["kmx"])

        def kmax_tile(kt):
            sl = slice(kt * 512, (kt + 1) * 512)
            op("act", lambda e: e.activation(out=sqA[:], in_=kA[:, sl], func=AF.Square), reads=KA, writes=["sqA"])
            op("act", lambda e: e.activation(out=sqB[:], in_=kB[0:64, sl], func=AF.Square), reads=KBK, writes=["sqB"])

            def mm(e):
                e.matmul(banks[7][:], lhsT=ones[:, :], rhs=sqA[:], start=True, stop=False)
                return e.matmul(banks[7][:], lhsT=ones[0:64, :], rhs=sqB[:], start=False, stop=True)
            op("pe", mm, reads=["ones", "sqA", "sqB"], writes=[("bank", 7)])
            op("dve", lambda e: e.reduce_max(out=kmt[:], in_=banks[7][:], axis=AX.X), reads=[("bank", 7)], writes=["kmt"])
            op("dve", lambda e: e.tensor_max(out=kmx[:], in0=kmx[:], in1=kmt[:]), reads=["kmx", "kmt"], writes=["kmx"])
        for kt in range(S // 512):
            kmax_tile(kt)
        op("act", lambda e: e.activation(out=negK[:], in_=kmx[:], func=AF.Sqrt), reads=["kmx"], writes=["negK"])
        op("dve", lambda e: e.tensor_scalar(out=negK[:], in0=negK[:], scalar1=-1.0, scalar2=1.0, op0=ALU.mult, op1=ALU.mult),
           reads=["negK"], writes=["negK"])

        def load_q(qt):
            qb = qcnt[0] % 2
            qa_ap, qb_ap = src["q"](hh, qt)
            P.dma("sp", qA[qb][:], qa_ap, writes=[("qA", qb)], semkey=("qA", qb))
            P.dma("sp", qB[qb][0:64, :], qb_ap, writes=[("qB", qb)], semkey=("qB", qb))
            op("act", lambda e: e.activation(out=sqA[:], in_=qA[qb][:], func=AF.Square), reads=[("qA", qb)], writes=["sqA"])
            op("act", lambda e: e.activation(out=sqB[:], in_=qB[qb][0:64, :], func=AF.Square), reads=[("qB", qb)],
               writes=["sqB"])

            def mm(e):
                e.matmul(banks[7][:], lhsT=ones[:, :], rhs=sqA[:], start=True, stop=False)
                return e.matmul(banks[7][:], lhsT=ones[0:64, :], rhs=sqB[:], start=False, stop=True)
            op("pe", mm, reads=["ones", "sqA", "sqB"], writes=[("bank", 7)])
            op("act", lambda e: e.activation(out=nrm[64:96, :], in_=banks[7][64:96, :], func=AF.Sqrt),
               reads=[("bank", 7)], writes=["nrm"])
            op("dve", lambda e: e.tensor_scalar(out=qB[qb][64:96, :], in0=nrm[64:96, :], scalar1=negK[64:96, 0:1], scalar2=1.0,
                                                op0=ALU.mult, op1=ALU.mult),
               reads=["nrm", "negK"], writes=[("qBm", qb)])

        def do_qtile(qt):
            qb = qcnt[0] % 2
            qcnt[0] += 1
            ob = (3, 4) if qb == 0 else (5, 6)
            last = 4 * qt + 3
            QK = [("qA", qb), ("qB", qb), ("qBm", qb)]

            def S_(kb):
                i3 = (step[0] + kb) % 3
                nq0 = 0 if kb < 4 * qt else kb - 4 * qt
                cs = slice(nq0 * 128, 512)
                ks = slice(kb * 128, (kb + 1) * 128)

                def mm(e):
                    e.matmul(banks[i3][:, cs], lhsT=kA[:, ks], rhs=qA[qb][:, cs], start=True, stop=False)
                    return e.matmul(banks[i3][:, cs], lhsT=kB[0:96, ks], rhs=qB[qb][0:96, cs], start=False, stop=True)
                op("pe", mm, reads=KA + KBK + ["kBones"] + QK, writes=[("bank", i3)])
                op("act", lambda e: e.activation(out=pT[i3][:, cs], in_=banks[i3][:, cs], func=AF.Exp),
                   reads=[("bank", i3)], writes=[("pT", i3)])
                if kb >= 4 * qt:
                    ds = slice(nq0 * 128, (nq0 + 1) * 128)
                    op("pool", lambda e: e.tensor_tensor(out=pT[i3][:, ds], in0=pT[i3][:, ds], in1=maskb[:], op=ALU.mult),
                       reads=[("pT", i3), "maskb"], writes=[("pT", i3)])

            def PV(kb):
                i3 = (step[0] + kb) % 3
                nq0 = 0 if kb < 4 * qt else kb - 4 * qt

                def mm(e):
                    ins = None
                    for sb in range(nq0, 4):
                        ins = e.matmul(banks[ob[sb // 2]][:, (sb % 2) * 130:(sb % 2) * 130 + 129],
                                       lhsT=pT[i3][:, sb * 128:(sb + 1) * 128], rhs=vS[:, kb, 0:129],
                                       start=(kb == 0 and sb % 2 == 0), stop=(kb == 4 * qt + sb))
                    return ins
                op("pe", mm, reads=[("pT", i3)] + VS, writes=[("obank", qb)])
            S_(0)
            if last >= 1:
                S_(1)
            for kb in range(last + 1):
                if kb + 2 <= last:
                    S_(kb + 2)
                PV(kb)
            step[0] += last + 1
            for sb in range(4):
                col = (sb % 2) * 130
                bk = banks[ob[sb // 2]]
                op("dve", lambda e, sb=sb, col=col, bk=bk: e.reciprocal(out=rl[qb][:, sb:sb + 1], in_=bk[:, col + 128:col + 129]),
                   reads=[("obank", qb)], writes=[("rl", qb, sb)])
                op("dve", lambda e, sb=sb, col=col, bk=bk: e.tensor_scalar(
                    out=ost[qb][:, sb, :], in0=bk[:, col:col + 128], scalar1=rl[qb][:, sb:sb + 1], scalar2=1.0,
                    op0=ALU.mult, op1=ALU.mult),
                   reads=[("obank", qb), ("rl", qb, sb)], writes=[("ost", qb, sb)])
            for sb in range(4):
                op("pe", lambda e, sb=sb: e.transpose(pb7[:, sb * 128:(sb + 1) * 128], ost[qb][:, sb, :], identb[:]),
                   reads=[("ost", qb, sb), "identb"], writes=[("bank", 7)])
            op("act", lambda e: e.activation(out=oTs[qb][:], in_=pb7[:, 0:512], func=AF.Identity),
               reads=[("bank", 7)], writes=[("oTs", qb)])
            P.dma("sp", src["o"](hh, qt), oTs[qb][:], reads=[("oTs", qb)], semkey=("oTs", qb))

        load_q(0)
        for qt in range(NQT):
            if qt + 1 < NQT:
                qcnt[0] += 1
                load_q(qt + 1)
                qcnt[0] -= 1
            do_qtile(qt)

    for hh in range(NH):
        do_head(hh)


def attn_kpad(S):
    import ml_dtypes
    a = np.zeros((32, S), np.float32)
    a[0, :] = 1.0
    return a.astype(ml_dtypes.bfloat16)


def attn_mask():
    j = np.arange(128)[:, None]
    i = np.arange(128)[None, :]
    return (j <= i).astype(np.float32)


def _pre(P, prefix, specs):
    return {n: P.dram_in(prefix + n, sh, dt) for n, sh, dt in specs}


def build_tokA_prog(ntok):
    P = Prog(arena=True)
    banks = [P.ps("bank%d" % i, [128, 512]) for i in range(8)]
    P._bar_bank = banks[0]
    x1 = P.dram_in("x1T", [D, ntok], F32)
    y_d = P.dram_in("yT", [2048, ntok], BF16)
    x2 = P.dram_out("x2T", [D, ntok], F32)
    x3 = P.dram_out("x3T", [D, ntok], F32)
    x4 = P.dram_out("x4T", [D, ntok], F32)
    q_o = P.dram_out("qT", [1536, ntok], BF16)
    kn_o = P.dram_out("kTn", [1024, ntok], BF16)
    kr_o = P.dram_out("kTr", [64, ntok], BF16)
    v_o = P.dram_out("v", [ntok, 1024], BF16)
    Tm = _pre(P, "m0_", MIX_IN)
    Tm["wout"] = P.dram_in("m0_wout", [2048, D], F32)
    Tf1 = _pre(P, "f1_", FFN_IN)
    Tf2 = _pre(P, "f2_", FFN_IN)
    Tp = _pre(P, "p_", MLA_IN)
    Tp["pos"] = P.dram_in("p_pos", [1, ntok], I32)
    yv = y_d.rearrange("(c p) t -> p c t", p=128)

    def ysrc(t):
        return [(4 * h, 4, yv[:, h * 4:(h + 1) * 4, t * 512:(t + 1) * 512]) for h in range(4)]
    m = P.mark()
    emit_mixout_stage(P, banks, Tm, ntok, 16, x1, x2, ysrc, [])
    P.barrier()
    P.release(m)
    Tf1["xT"], Tf1["yT"] = x2, x3
    emit_ffn_stage(P, banks, Tf1, ntok)
    P.barrier()
    P.release(m)
    Tf2["xT"], Tf2["yT"] = x3, x4
    emit_ffn_stage(P, banks, Tf2, ntok)
    P.barrier()
    P.release(m)
    emit_mlaproj_stage(P, banks, Tp, ntok, x4, q_o, kn_o, kr_o, v_o.rearrange("t (h d) -> t h d", h=4))
    return P.build()


def build_tokB_prog(ntok):
    P = Prog(arena=True)
    banks = [P.ps("bank%d" % i, [128, 512]) for i in range(8)]
    P._bar_bank = banks[0]
    x4 = P.dram_in("x4T", [D, ntok], F32)
    o_d = P.dram_in("oT", [1024, ntok], BF16)
    x5 = P.dram_out("x5T", [D, ntok], F32)
    out = P.dram_out("outT", [D, ntok], F32)
    Tm = _pre(P, "m1_", MIX_IN)
    Tm["wout"] = P.dram_in("m1_wout", [1024, D], F32)
    Tf = _pre(P, "f3_", FFN_IN)
    ov = o_d.rearrange("(c p) t -> p c t", p=128)

    def ysrc(t):
        return [(4 * h, 4, ov[:, h * 4:(h + 1) * 4, t * 512:(t + 1) * 512]) for h in range(2)]
    m = P.mark()
    emit_mixout_stage(P, banks, Tm, ntok, 8, x4, x5, ysrc, [])
    P.barrier()
    P.release(m)
    Tf["xT"], Tf["yT"] = x5, out
    emit_ffn_stage(P, banks, Tf, ntok)
    return P.build()


def _pfx(prefix, d, drop=()):
    return {prefix + k: v for k, v in d.items() if k not in drop}


B_, S_ = 2, 16384
TPC = 4096


def _run(nc, maps):
    res = run_bass_kernel_spmd(nc, maps, core_ids=list(range(NCORES)))
    return res.results


def _tok(i):
    return i // 4, (i % 4) * TPC


def _ffn_in(I, b, l, f, sub):
    d = ffn_inputs(None, I["c"][b], I["mod_w"][l], I["mod_b"][l], sub, I["ffn_in"][l, f], I["ffn_out"][l, f],
                   I["ln_g"][l, sub], I["ln_b"][l, sub])
    d.pop("xT")
    return d


def _mix_in(I, b, l, w_out):
    d = mixout_inputs(None, None, I["c"][b], I["mod_w"][l], I["mod_b"][l], w_out, I["ln_g"][l, 1], I["ln_b"][l, 1])
    d.pop("xT")
    d.pop("yT")
    return d


def kernel(**inputs):
    I = {k: np.asarray(v) for k, v in inputs.items()}
    x = I["x"]
    xT = [np.ascontiguousarray(x[b, t0:t0 + TPC].T) for b, t0 in (_tok(i) for i in range(NCORES))]
    nc = build_ffn_prog(TPC)
    maps = []
    for i in range(NCORES):
        d = _ffn_in(I, i // 4, 0, 0, 0)
        d["xT"] = xT[i]
        maps.append(d)
    r = _run(nc, maps)
    x1 = [r[i]["yT"] for i in range(NCORES)]
    x1_full = [np.concatenate(x1[b * 4:(b + 1) * 4], axis=1) for b in range(B_)]
    nc = build_even_prog(S_)
    maps = [even_inputs(x1_full[i // 4], I["c"][i // 4], I["mod_w"][0], I["mod_b"][0], i % 4, I) for i in range(NCORES)]
    r = _run(nc, maps)
    yT_sh = []
    for i in range(NCORES):
        b, t0 = _tok(i)
        ya = np.concatenate([r[b * 4 + q]["y"][0:256, t0:t0 + TPC] for q in range(4)], axis=0)
        yb = np.concatenate([r[b * 4 + q]["y"][256:512, t0:t0 + TPC] for q in range(4)], axis=0)
        yT_sh.append(np.ascontiguousarray(np.concatenate([ya, yb], axis=0)))
    nc = build_tokA_prog(TPC)
    maps = []
    for i in range(NCORES):
        b, t0 = _tok(i)
        d = {"x1T": x1[i], "yT": yT_sh[i]}
        d.update(_pfx("m0_", _mix_in(I, b, 0, I["even_w_out"][0])))
        d.update(_pfx("f1_", _ffn_in(I, b, 0, 1, 2)))
        d.update(_pfx("f2_", _ffn_in(I, b, 1, 0, 0)))
        pm = mlaproj_inputs(None, I["c"][b], I["mod_w"][1], I["mod_b"][1], I["positions"][b, t0:t0 + TPC], I)
        pm.pop("xT")
        d.update(_pfx("p_", pm))
        maps.append(d)
    r = _run(nc, maps)
    x4 = [r[i]["x4T"] for i in range(NCORES)]
    nc = build_attn_prog(S_)
    maps = []
    mask = attn_mask()
    kpad = attn_kpad(S_)
    for i in range(NCORES):
        b, hp = i // 4, i % 4
        sh = range(b * 4, (b + 1) * 4)
        hs = [2 * hp, 2 * hp + 1]
        maps.append({
            "qT": np.ascontiguousarray(np.stack([np.concatenate([r[j]["qT"][h * 192:(h + 1) * 192] for j in sh], axis=1)
                                                 for h in hs])),
            "kTn": np.ascontiguousarray(np.stack([np.concatenate([r[j]["kTn"][h * 128:(h + 1) * 128] for j in sh], axis=1)
                                                  for h in hs])),
            "kTr": np.ascontiguousarray(np.concatenate([r[j]["kTr"] for j in sh], axis=1)),
            "v": np.ascontiguousarray(np.stack([np.concatenate([r[j]["v"][:, h * 128:(h + 1) * 128] for j in sh], axis=0)
                                                for h in hs])),
            "maskT": mask, "kpad": kpad, "ident": np.eye(128, dtype=np.float32)})
    ro = _run(nc, maps)
    nc = build_tokB_prog(TPC)
    maps = []
    for i in range(NCORES):
        b, t0 = _tok(i)
        oT = np.ascontiguousarray(np.concatenate(
            [ro[b * 4 + hp]["oT"][hh][:, t0:t0 + TPC] for hp in range(4) for hh in range(2)], axis=0))
        d = {"x4T": x4[i], "oT": oT}
        d.update(_pfx("m1_", _mix_in(I, b, 1, I["odd_w_out"][0])))
        d.update(_pfx("f3_", _ffn_in(I, b, 1, 1, 2)))
        maps.append(d)
    r = _run(nc, maps)
    out = np.empty((B_, S_, D), np.float32)
    for i in range(NCORES):
        b, t0 = _tok(i)
        out[b, t0:t0 + TPC] = r[i]["outT"].T
    return out
```
